# Optimizing a Trainium2 kernel written in Bass

```python
import math
import jax, jax.numpy as jnp
from jax import lax
import numpy as np

D_MODEL = 1024
BATCH = 2
SEQ = 8192
DEPTH = 1

EPS = 1e-6
DA_HEADS = 4
DA_HEAD_DIM = D_MODEL // 16
DA_V_DIM = 2 * DA_HEAD_DIM
DA_QK = DA_HEADS * 2 * DA_HEAD_DIM
DA_WIDTH = DA_HEADS * DA_V_DIM
Q_BLOCK = 128
GDN_HEADS = 4
GDN_HEAD_DIM = D_MODEL // 8
GDN_WIDTH = GDN_HEADS * GDN_HEAD_DIM
GDN_CONV = 4
GDN_CHUNK = 64
D_MIX = DA_WIDTH + GDN_WIDTH
SPLITS = [
    DA_QK,
    2 * DA_QK,
    2 * DA_QK + DA_WIDTH,
    2 * DA_QK + DA_WIDTH + GDN_WIDTH,
    2 * DA_QK + DA_WIDTH + 2 * GDN_WIDTH,
    2 * DA_QK + DA_WIDTH + 3 * GDN_WIDTH,
    2 * DA_QK + DA_WIDTH + 3 * GDN_WIDTH + GDN_HEADS,
    2 * DA_QK + DA_WIDTH + 3 * GDN_WIDTH + 2 * GDN_HEADS,
]
D_IN_PROJ = SPLITS[-1] + GDN_WIDTH
D_FF = ((8 * D_MODEL // 3 + 127) // 128) * 128
FFN_CONV = 3

kernel_name = "hybrid_diffattn_gdn_convffn"


def rms_norm(x, g):
    x32 = x.astype(jnp.float32)
    y = x32 * lax.rsqrt(jnp.mean(x32 * x32, axis=-1, keepdims=True) + EPS)
    return (y * g.astype(jnp.float32)).astype(x.dtype)


def l2_normalize(x):
    x32 = x.astype(jnp.float32)
    return x32 * lax.rsqrt(jnp.sum(x32 * x32, axis=-1, keepdims=True) + EPS)


def causal_depthwise_conv(x, w):
    K = w.shape[0]
    S = x.shape[1]
    xp = jnp.pad(x, ((0, 0), (K - 1, 0), (0, 0)))
    y = xp[:, 0:S] * w[0]
    for j in range(1, K):
        y = y + xp[:, j:j + S] * w[j]
    return y


def alibi_slopes(n_heads):
    return jnp.exp2(-8.0 * jnp.arange(1, n_heads + 1, dtype=jnp.float32) / n_heads)


def diff_attention(q, k, v, lam):
    B, S = q.shape[0], q.shape[1]
    nblk = S // Q_BLOCK
    qb = (q * DA_HEAD_DIM ** -0.5).reshape(B, nblk, Q_BLOCK, DA_HEADS, 2, DA_HEAD_DIM)
    qb = jnp.moveaxis(qb, 1, 0)
    slopes = alibi_slopes(DA_HEADS)
    key_pos = jnp.arange(S)

    def block(args):
        qi, idx = args
        q_pos = idx * Q_BLOCK + jnp.arange(Q_BLOCK)
        dist = q_pos[:, None] - key_pos[None, :]
        s = jnp.einsum('bqhcd,bkhcd->bhcqk', qi, k).astype(jnp.float32)
        s = s - slopes[None, :, None, None, None] * dist.astype(jnp.float32)
        s = jnp.where(dist >= 0, s, -jnp.inf)
        p = jax.nn.softmax(s, axis=-1)
        a = p[:, :, 0] - lam * p[:, :, 1]
        return jnp.einsum('bhqk,bkhd->bqhd', a.astype(v.dtype), v)

    out = lax.map(block, (qb, jnp.arange(nblk)))
    return jnp.moveaxis(out, 0, 1).reshape(B, S, DA_HEADS, DA_V_DIM)


def gated_delta_rule(q, k, v, g, beta):
    B, S, H, Dk = q.shape
    Dv = v.shape[-1]
    C = GDN_CHUNK
    N = S // C
    q = q * Dk ** -0.5

    def to_chunks(t):
        t = jnp.swapaxes(t, 1, 2)
        return t.reshape(B, H, N, C, *t.shape[3:])

    qc, kc, vc = to_chunks(q), to_chunks(k), to_chunks(v)
    gc = jnp.cumsum(to_chunks(g), axis=-1)
    bc = to_chunks(beta)
    tril = jnp.tril(jnp.ones((C, C), dtype=bool))
    strict = jnp.tril(jnp.ones((C, C), dtype=bool), -1)
    decay = jnp.exp(jnp.where(tril, gc[..., :, None] - gc[..., None, :], -jnp.inf))
    kb = kc * bc[..., None]
    vb = vc * bc[..., None]
    L = jnp.where(strict, jnp.einsum('bhnid,bhnjd->bhnij', kb, kc) * decay, 0.0)
    eye = jnp.eye(C, dtype=jnp.float32)
    T = lax.linalg.triangular_solve(eye + L, jnp.broadcast_to(eye, L.shape),
                                    left_side=True, lower=True, unit_diagonal=True)
    u = jnp.einsum('bhnij,bhnjd->bhnid', T, vb)
    w = jnp.einsum('bhnij,bhnjd->bhnid', T, kb * jnp.exp(gc)[..., None])
    qk = jnp.where(tril, jnp.einsum('bhnid,bhnjd->bhnij', qc, kc) * decay, 0.0)
    q_dec = qc * jnp.exp(gc)[..., None]
    k_dec = kc * jnp.exp(gc[..., -1:] - gc)[..., None]
    g_last = jnp.exp(gc[..., -1])

    def step(state, xs):
        qk_i, q_dec_i, k_dec_i, u_i, w_i, gl_i = xs
        v_new = u_i - jnp.einsum('bhck,bhkv->bhcv', w_i, state)
        o = jnp.einsum('bhck,bhkv->bhcv', q_dec_i, state) + jnp.einsum('bhij,bhjv->bhiv', qk_i, v_new)
        state = state * gl_i[..., None, None] + jnp.einsum('bhck,bhcv->bhkv', k_dec_i, v_new)
        return state, o

    xs = tuple(jnp.moveaxis(t, 2, 0) for t in (qk, q_dec, k_dec, u, w, g_last))
    state0 = jnp.zeros((B, H, Dk, Dv), dtype=jnp.float32)
    _, o = lax.scan(step, state0, xs)
    return jnp.transpose(o, (1, 0, 3, 2, 4)).reshape(B, S, H, Dv)


def setup_inputs(seed: int = 0) -> dict:
    key = jax.random.key(seed)
    ks = jax.random.split(key, 20)
    f32 = jnp.float32

    def nrm(k, shape, scale):
        return jax.random.normal(k, shape, dtype=f32) * scale

    def gain(k, n):
        return 1.0 + 0.02 * jax.random.normal(k, (DEPTH, n), dtype=f32)

    dt = jnp.exp(jax.random.uniform(ks[9], (DEPTH, GDN_HEADS), dtype=f32,
                                    minval=math.log(1e-3), maxval=math.log(1e-1)))
    return {
        "x": jax.random.normal(ks[0], (BATCH, SEQ, D_MODEL), dtype=f32),
        "attn_norm_g": gain(ks[1], D_MODEL),
        "w_in": nrm(ks[2], (DEPTH, D_MODEL, D_IN_PROJ), D_MODEL ** -0.5),
        "da_lambda_q1": nrm(ks[3], (DEPTH, DA_HEAD_DIM), 0.1),
        "da_lambda_k1": nrm(ks[4], (DEPTH, DA_HEAD_DIM), 0.1),
        "da_lambda_q2": nrm(ks[5], (DEPTH, DA_HEAD_DIM), 0.1),
        "da_lambda_k2": nrm(ks[6], (DEPTH, DA_HEAD_DIM), 0.1),
        "da_subln_g": gain(ks[7], DA_V_DIM),
        "gdn_conv_w": nrm(ks[8], (DEPTH, GDN_CONV, 3 * GDN_WIDTH), GDN_CONV ** -0.5),
        "gdn_a_log": jnp.log(jax.random.uniform(ks[10], (DEPTH, GDN_HEADS), dtype=f32, minval=1.0, maxval=16.0)),
        "gdn_dt_bias": dt + jnp.log(-jnp.expm1(-dt)),
        "gdn_norm_g": gain(ks[11], GDN_HEAD_DIM),
        "w_out": nrm(ks[12], (DEPTH, D_MIX, D_MODEL), D_MIX ** -0.5),
        "ffn_norm_g": gain(ks[13], D_MODEL),
        "w_up": nrm(ks[14], (DEPTH, D_MODEL, 2 * D_FF), D_MODEL ** -0.5),
        "ffn_conv_w": nrm(ks[15], (DEPTH, FFN_CONV, 2 * D_FF), FFN_CONV ** -0.5),
        "ffn_conv_b": nrm(ks[16], (DEPTH, 2 * D_FF), 0.02),
        "w_down": nrm(ks[17], (DEPTH, D_FF, D_MODEL), D_FF ** -0.5),
        "final_norm_g": 1.0 + 0.02 * jax.random.normal(ks[18], (D_MODEL,), dtype=f32),
    }


def reference(x, attn_norm_g, w_in, da_lambda_q1, da_lambda_k1, da_lambda_q2, da_lambda_k2,
              da_subln_g, gdn_conv_w, gdn_a_log, gdn_dt_bias, gdn_norm_g, w_out,
              ffn_norm_g, w_up, ffn_conv_w, ffn_conv_b, w_down, final_norm_g):
    B, S, _ = x.shape
    f32 = jnp.float32
    for l in range(DEPTH):
        lam_init = 0.8 - 0.6 * math.exp(-0.3 * l)
        h = rms_norm(x, attn_norm_g[l])
        proj = h @ w_in[l]
        dq, dk, dv, gq, gk, gv, ga, gb, gz = jnp.split(proj, SPLITS, axis=-1)

        lam = (jnp.exp(jnp.sum(da_lambda_q1[l].astype(f32) * da_lambda_k1[l].astype(f32)))
               - jnp.exp(jnp.sum(da_lambda_q2[l].astype(f32) * da_lambda_k2[l].astype(f32)))
               + lam_init)
        qa = dq.reshape(B, S, DA_HEADS, 2, DA_HEAD_DIM)
        ka = dk.reshape(B, S, DA_HEADS, 2, DA_HEAD_DIM)
        va = dv.reshape(B, S, DA_HEADS, DA_V_DIM)
        oa = diff_attention(qa, ka, va, lam)
        oa = (rms_norm(oa, da_subln_g[l]) * (1.0 - lam_init)).reshape(B, S, DA_WIDTH)

        qkv = jax.nn.silu(causal_depthwise_conv(jnp.concatenate([gq, gk, gv], axis=-1), gdn_conv_w[l]))
        qg, kg, vg = jnp.split(qkv, [GDN_WIDTH, 2 * GDN_WIDTH], axis=-1)
        qg = l2_normalize(qg.reshape(B, S, GDN_HEADS, GDN_HEAD_DIM))
        kg = l2_normalize(kg.reshape(B, S, GDN_HEADS, GDN_HEAD_DIM))
        vg = vg.reshape(B, S, GDN_HEADS, GDN_HEAD_DIM).astype(f32)
        beta = jax.nn.sigmoid(gb.astype(f32))
        g = -jnp.exp(gdn_a_log[l].astype(f32)) * jax.nn.softplus(ga.astype(f32) + gdn_dt_bias[l].astype(f32))
        og = gated_delta_rule(qg, kg, vg, g, beta).astype(x.dtype)
        z = gz.reshape(B, S, GDN_HEADS, GDN_HEAD_DIM)
        og = (rms_norm(og, gdn_norm_g[l]) * jax.nn.silu(z)).reshape(B, S, GDN_WIDTH)

        x = x + jnp.concatenate([oa, og], axis=-1) @ w_out[l]

        h = rms_norm(x, ffn_norm_g[l])
        up = causal_depthwise_conv(h @ w_up[l], ffn_conv_w[l]) + ffn_conv_b[l]
        gate, val = jnp.split(up, [D_FF], axis=-1)
        x = x + (jax.nn.silu(gate) * val) @ w_down[l]
    return rms_norm(x, final_norm_g)
```

```python
import math
from contextlib import ExitStack

import numpy as np
import ml_dtypes

import concourse.bass as bass
import concourse.mybir as mybir
from concourse.bass_utils import run_bass_kernel_spmd

F32 = mybir.dt.float32
BF16 = mybir.dt.bfloat16
AF = mybir.ActivationFunctionType
ALU = mybir.AluOpType
AX = mybir.AxisListType

D_MODEL = 1024
EPS = 1e-6
D_FF = 2816
NCORES = 8
NEG = -30000.0
SAME_ENGINE_SYNC = True


class Buf:
    __slots__ = ("w", "r", "name")

    def __init__(self, name=""):
        self.w = None
        self.r = []
        self.name = name


class Prog:
    CE = ("pe", "act", "dve", "pool")

    SEM_LIMIT = 2000

    def __init__(self, nc, es, n_dma_sems=40):
        self.nc = nc
        self.es = es
        self.nsem = 0
        self.q = {e: [] for e in ("pe", "act", "dve", "pool", "sp")}
        self.sem = {e: es.enter_context(nc.semaphore("s_" + e)) for e in self.CE}
        self.cnt = {e: 0 for e in self.CE}
        self.seen = {e: {} for e in self.q}
        self.dsem = [es.enter_context(nc.semaphore("d%d" % i)) for i in range(n_dma_sems)]
        self.dcnt = [0] * n_dma_sems
        self.dnext = 0
        self.dnext_pool = 0
        self.dma_toks = []
        self.nops = 0

    def _wait(self, eng, tok):
        sem, val, src = tok
        k = id(sem)
        if self.seen[eng].get(k, 0) >= val:
            return
        self.seen[eng][k] = val
        self.q[eng].append(lambda e, sem=sem, val=val: e.wait_ge(sem, val))

    def _deps(self, eng, reads, writes):
        for b in reads:
            if b.w is not None:
                if b.w[2] == eng and (eng == "pe" or not SAME_ENGINE_SYNC):
                    continue
                self._wait(eng, b.w)
        for b in writes:
            if b.w is not None and (b.w[2] != eng or (SAME_ENGINE_SYNC and eng != "pe")):
                self._wait(eng, b.w)
            for t in b.r:
                if t[2] != eng or (SAME_ENGINE_SYNC and eng != "pe"):
                    self._wait(eng, t)

    def op(self, eng, fn, reads=(), writes=()):
        self._deps(eng, reads, writes)
        if self.cnt[eng] >= self.SEM_LIMIT:
            self.nsem += 1
            self.sem[eng] = self.es.enter_context(self.nc.semaphore("s_%s_%d" % (eng, self.nsem)))
            self.cnt[eng] = 0
        self.cnt[eng] += 1
        sem = self.sem[eng]
        tok = (sem, self.cnt[eng], eng)
        self.q[eng].append(lambda e, fn=fn, sem=sem: fn(e).then_inc(sem, 1))
        for b in writes:
            b.w = tok
            b.r = []
        for b in reads:
            b.r.append(tok)
        self.nops += 1
        return tok

    def dma(self, eng, out, in_, reads=(), writes=()):
        self._deps(eng, reads, writes)
        npool = 8
        if eng == "pool":
            i = self.dnext_pool
            self.dnext_pool = (self.dnext_pool + 1) % npool
        else:
            i = npool + self.dnext
            self.dnext = (self.dnext + 1) % (len(self.dsem) - npool)
        sem = self.dsem[i]
        if self.dcnt[i] > 0:
            self._wait(eng, (sem, self.dcnt[i], "dma"))
        self.dcnt[i] += 16
        tok = (sem, self.dcnt[i], "dma")
        self.q[eng].append(lambda e, out=out, in_=in_, sem=sem: e.dma_start(
            out=out, in_=(in_(e) if callable(in_) else in_)).then_inc(sem, 16))
        for b in writes:
            b.w = tok
            b.r = []
        for b in reads:
            b.r.append(tok)
        self.dma_toks.append(tok)
        self.nops += 1
        return tok

    def collective(self, fn, sem, reads=(), writes=()):
        self._deps("pool", reads, writes)
        self.ccnt = getattr(self, "ccnt", 0) + 1
        tok = (sem, self.ccnt, "cc")
        self.q["pool"].append(lambda e, fn=fn, sem=sem: fn(e).then_inc(sem, 1))
        for b in writes:
            b.w = tok
            b.r = []
        for b in reads:
            b.r.append(tok)
        return tok

    def raw(self, eng, fn):
        self.q[eng].append(fn)

    def barrier(self):
        toks = [(self.sem[e], self.cnt[e], e) for e in self.CE if self.cnt[e] > 0]
        toks += [(self.dsem[i], self.dcnt[i], "dma") for i in range(len(self.dsem)) if self.dcnt[i] > 0]
        for e in self.q:
            for t in toks:
                if t[2] == e:
                    continue
                self._wait(e, t)

    def finish(self):
        for i in range(len(self.dsem)):
            if self.dcnt[i] > 0:
                self._wait("sp", (self.dsem[i], self.dcnt[i], "dma"))

    def replay(self, block):
        q = self.q

        @block.tensor
        def _(e):
            for f in q["pe"]:
                f(e)

        @block.scalar
        def _(e):
            for f in q["act"]:
                f(e)

        @block.vector
        def _(e):
            for f in q["dve"]:
                f(e)

        @block.gpsimd
        def _(e):
            for f in q["pool"]:
                f(e)

        @block.sync
        def _(e):
            for f in q["sp"]:
                f(e)


def act(out, in_, func, bias=0.0, scale=1.0):
    return lambda e: e.activation(out=out, in_=in_, func=func, bias=bias, scale=scale)


def mm(out, lhsT, rhs, start=True, stop=True):
    return lambda e: e.matmul(out, lhsT, rhs, start=start, stop=stop)


def tt(out, in0, in1, op):
    return lambda e: e.tensor_tensor(out=out, in0=in0, in1=in1, op=op)


def ts(out, in0, s1, s2, op0, op1=None):
    if op1 is None:
        return lambda e: e.tensor_scalar(out=out, in0=in0, scalar1=s1, scalar2=None, op0=op0)
    return lambda e: e.tensor_scalar(out=out, in0=in0, scalar1=s1, scalar2=s2, op0=op0, op1=op1)


def stt(out, in0, scalar, in1, op0, op1):
    return lambda e: e.scalar_tensor_tensor(out=out, in0=in0, scalar=scalar, in1=in1, op0=op0, op1=op1)


def cp(out, in_):
    return lambda e: e.tensor_copy(out=out, in_=in_)


class Builder:
    def __init__(self, S, debug=False, stop=None):
        self.S = S
        self.debug = debug
        self.stop = stop
        self.NT = S // 512
        self.NK = S // 128
        self.TPC = S // 4
        self.nc = bass.Bass("TRN2", target_bir_lowering=False)

    def declare_io(self):
        nc, S = self.nc, self.S
        di = lambda n, s, d=F32: nc.dram_tensor(n, s, d, kind="ExternalInput").ap()
        self.xT = di("xT", [D_MODEL, S])
        self.wh = di("wh", [D_MODEL, 898])
        self.gA = di("gA", [128, 8])
        self.convw = di("convw", [128, 12])
        self.qaug = di("qaug", [4, S], BF16)
        self.kaug = di("kaug", [4, S], BF16)
        self.cbf = di("cbf", [128, 4 * 128], BF16)
        self.cf32 = di("cf32", [128, 8 * 128])
        self.pvec = di("pvec", [128, 16])
        self.lamv = di("lamv", [128, 4 * 64])
        self.xT2 = di("xT2", [D_MODEL, self.TPC + 2])
        self.xtok2 = di("xtok2", [self.TPC, D_MODEL])
        self.wout = di("wout", [D_MODEL, D_MODEL])
        self.wup = di("wup", [D_MODEL, 2 * D_FF])
        self.wdown = di("wdown", [D_FF, D_MODEL])
        self.g2 = di("g2", [128, 8])
        self.fcw = di("fcw", [128, 44 * 4])
        self.gfin = di("gfin", [128, D_MODEL])
        self.out = nc.dram_tensor("out", [self.TPC, D_MODEL], F32, kind="ExternalOutput").ap()
        self.QD = nc.dram_tensor("QD", [128, S], BF16).ap()
        self.KD = nc.dram_tensor("KD", [128, S], BF16).ap()
        self.VD = nc.dram_tensor("VD", [S, 128], BF16).ap()
        self.MIXD = nc.dram_tensor("MIXD", [256, S], BF16).ap()
        W2 = self.TPC + 2
        self.ag_in = nc.dram_tensor("ag_in", [4 * 256, W2], BF16)
        self.ag_out = nc.dram_tensor("ag_out", [32 * 256, W2], BF16)
        self.wup_bf = nc.dram_tensor("wup_bf", [D_MODEL, 2 * D_FF], BF16).ap()
        self.wdown_bf = nc.dram_tensor("wdown_bf", [D_FF, D_MODEL], BF16).ap()
        self.wout_bf = nc.dram_tensor("wout_bf", [D_MODEL, D_MODEL], BF16).ap()
        if self.debug:
            do = lambda n, s, d=F32: nc.dram_tensor(n, s, d, kind="ExternalOutput").ap()
            self.dbg = {
                "QD": do("dQD", [128, S], BF16), "KD": do("dKD", [128, S], BF16), "VD": do("dVD", [S, 128], BF16),
                "QgT": do("dQgT", [128, S], BF16), "KgT": do("dKgT", [128, S], BF16),
                "Vg": do("dVg", [128, self.NK * 128], BF16), "Kt": do("dKt", [128, self.NK * 128], BF16),
                "Zs": do("dZs", [128, S], BF16), "GAB": do("dGAB", [128, self.NK * 2]),
                "MIX": do("dMIX", [256, S], BF16),
            }

    def build(self):
        nc = self.nc
        self.declare_io()
        with ExitStack() as es:
            P = self.P = Prog(nc, es)
            self.es = es
            self.ps = [es.enter_context(nc.psum_tensor("ps%d" % i, [128, 512], F32)) for i in range(8)]
            self.psb = [Buf("ps%d" % i) for i in range(8)]
            sb = lambda n, s, d: es.enter_context(nc.sbuf_tensor(n, s, d))
            self.c_bf = sb("c_bf", [128, 512], BF16)
            self.c_f32 = sb("c_f32", [128, 1024], F32)
            self.c_pv = sb("c_pv", [128, 16], F32)
            self.b_const = Buf("const")
            P.dma("sp", self.c_bf[:], self.cbf[:, :], writes=[self.b_const])
            P.dma("sp", self.c_f32[:], self.cf32[:, :], writes=[self.b_const])
            P.dma("sp", self.c_pv[:], self.pvec[:, :], writes=[self.b_const])
            self.ident = self.c_bf[:, 0:128]
            self.ones = self.c_bf[:, 128:256]
            self.onesmean = self.c_bf[:, 256:384]
            self.tri = self.c_bf[:, 384:512]

            self.bMIXa = [Buf() for _ in range(self.NT)]
            self.bMIXg = [Buf() for _ in range(self.NT)]
            self.cc_sem = es.enter_context(nc.semaphore("cc_sem"))
            self.bwcast = Buf("wcast")
            self._mix_init()
            with ExitStack() as es1:
                self.es1 = es1
                self.phase1a()
                if self.debug == "1a":
                    self.dump_1a()
                else:
                    self.phase_gdn()
            if self.debug != "1a":
                full = self.stop is None or self.stop.startswith("p2")
                if full and self.stop != "p2x0":
                    NFC_ = D_FF // 128
                    self.Wout_sb = sb("Wout", [128, 8 * 1024], BF16)
                    self.Wd_sb = sb("Wd", [128, NFC_ * 1024], BF16)
                    self.bWout, self.bWd = Buf(), Buf()
                    P.dma("sp", self.Wout_sb[:].rearrange("p (c n) -> p c n", c=8),
                          self.wout_bf.rearrange("(c p) n -> p c n", p=128), reads=[self.bwcast], writes=[self.bWout])
                    Wdv_ = self.Wd_sb[:].rearrange("p (c n) -> p c n", c=NFC_)
                    wdv_ = self.wdown_bf.rearrange("(c p) n -> p c n", p=128)
                    for i in range(0, NFC_, 4):
                        P.dma("sp", Wdv_[:, i:min(i + 4, NFC_), :], wdv_[:, i:min(i + 4, NFC_), :],
                              reads=[self.bwcast], writes=[self.bWd])
                if self.stop != "gdn":
                    self.phase_attn()
                if self.debug:
                    P.dma("sp", self.dbg["MIX"], self.MIXD, reads=self.bMIXa + self.bMIXg)
                if full and self.stop != "p2x0":
                    self.phase2()
            P.finish()
            block = es.enter_context(nc.Block())
            P.replay(block)
        return nc

    def rsqrt_act(self, out, in_, tmp, reads, writes, tmpbuf, eps=EPS):
        P = self.P
        P.op("act", act(tmp, in_, AF.Ln, bias=eps), reads=reads, writes=[tmpbuf])
        P.op("act", act(out, tmp, AF.Exp, scale=-0.5), reads=[tmpbuf], writes=writes)

    def phase1a(self):
        nc, P, S, NT = self.nc, self.P, self.S, self.NT
        es = self.es1
        sb = lambda n, s, d: es.enter_context(nc.sbuf_tensor(n, s, d))
        ps, psb = self.ps, self.psb
        bc = self.b_const
        self.QgT = sb("QgT", [128, S], BF16)
        self.KgT = sb("KgT", [128, S], BF16)
        self.Vg = sb("Vg", [128, self.NK * 128], BF16)
        self.Kt = sb("Kt", [128, self.NK * 128], BF16)
        self.Zs = sb("Zs", [128, S], BF16)
        self.GAB = sb("GAB", [128, self.NK * 2], F32)
        self.bQg = [Buf() for _ in range(NT)]
        self.bKg = [Buf() for _ in range(NT)]
        self.bVg = [Buf() for _ in range(NT)]
        self.bKt = [Buf() for _ in range(NT)]
        self.bZs = [Buf() for _ in range(NT)]
        self.bGAB = [Buf() for _ in range(NT)]
        self.bQD = [Buf() for _ in range(NT)]
        self.bKD = [Buf() for _ in range(NT)]
        self.bVD = [Buf() for _ in range(NT)]

        with ExitStack() as ws:
            wsb = lambda n, s, d: ws.enter_context(nc.sbuf_tensor(n, s, d))
            W = wsb("W_sb", [128, 8 * 898], BF16)
            bW = Buf("W")
            Wv = W[:].rearrange("p (c n) -> p c n", c=8)
            P.dma("pool", Wv, self.wh.rearrange("(c p) n -> p c n", p=128), writes=[bW])
            self.emit_wcasts = emit_wcasts = lambda: self._emit_wcasts()
            def _unused():
                P.dma("pool", self.wout_bf, self.wout[:, :], writes=[self.bwcast])
                P.dma("pool", self.wdown_bf, self.wdown[:, :], writes=[self.bwcast])
                for i in range(4):
                    P.dma("pool", self.wup_bf[i * 256:(i + 1) * 256, :], self.wup[i * 256:(i + 1) * 256, :], writes=[self.bwcast])
            gA = wsb("gA_sb", [128, 8], F32)
            cw = wsb("cw_sb", [128, 12], F32)
            bsm = Buf("small")
            P.dma("sp", gA[:], self.gA[:, :], writes=[bsm])
            P.dma("sp", cw[:], self.convw[:, :], writes=[bsm])

            xt = [wsb("xt%d" % i, [128, 8 * 512], F32) for i in range(2)]
            sq = [wsb("sq%d" % i, [128, 8 * 512], BF16) for i in range(2)]
            xb = [wsb("xb%d" % i, [128, 8 * 512], BF16) for i in range(2)]
            rstd = [wsb("rstd%d" % i, [128, 512], F32) for i in range(2)]
            lntmp = wsb("lntmp", [128, 512], F32)
            bxt = [Buf() for _ in range(2)]
            bsq = [Buf() for _ in range(2)]
            bxb = [Buf() for _ in range(2)]
            brstd = [Buf() for _ in range(2)]
            bln = Buf()
            qst = [wsb("qst%d" % i, [128, 512], BF16) for i in range(2)]
            kst = [wsb("kst%d" % i, [128, 512], BF16) for i in range(2)]
            vst = [wsb("vst%d" % i, [128, 512], BF16) for i in range(2)]
            bqst = [Buf() for _ in range(2)]
            bkst = [Buf() for _ in range(2)]
            bvst = [Buf() for _ in range(2)]
            cs = [[wsb("cs%d_%d" % (g, i), [128, 515], F32) for i in range(2)] for g in range(3)]
            bcs = [[Buf() for _ in range(2)] for g in range(3)]
            yc = [wsb("yc%d" % g, [128, 512], F32) for g in range(3)]
            byc = [Buf() for _ in range(3)]
            sl = [wsb("sl%d" % g, [128, 512], F32) for g in range(2)]
            bsl = [Buf() for _ in range(2)]
            s2 = [wsb("s2_%d" % g, [128, 512], BF16) for g in range(2)]
            bs2 = [Buf() for _ in range(2)]
            rn = [wsb("rn%d" % g, [128, 512], F32) for g in range(2)]
            brn = [Buf() for _ in range(2)]
            vs = wsb("vs", [128, 512], BF16)
            bvs = Buf()
            for g in range(3):
                P.op("dve", lambda e, g=g: e.memset(cs[g][1][:, 512:515], 0.0), writes=[bcs[g][1]])

            fm_rot = [4, 5, 6]
            rot = [0]

            def stageA0(t):
                sl_ = t % 2
                xv = xt[sl_][:].rearrange("p (c t) -> p c t", c=8)
                P.dma("sp", xv, self.xT[:, t * 512:(t + 1) * 512].rearrange("(c p) t -> p c t", p=128),
                      writes=[bxt[sl_]])
                P.op("pool", tt(sq[sl_][:], xt[sl_][:], xt[sl_][:], ALU.mult), reads=[bxt[sl_]], writes=[bsq[sl_]])

            def stageA1a(t):
                sl_ = t % 2
                for c in range(8):
                    P.op("pe", mm(ps[0][:, :], self.onesmean, sq[sl_][:, c * 512:(c + 1) * 512], start=(c == 0), stop=(c == 7)),
                         reads=[bsq[sl_], bc], writes=[psb[0]])
                self.rsqrt_act(rstd[sl_][:], ps[0][:, :], lntmp[:], [psb[0]], [brstd[sl_]], bln)

            def stageA1b(t):
                sl_ = t % 2
                for c in range(8):
                    P.op("dve", stt(xb[sl_][:, c * 512:(c + 1) * 512], xt[sl_][:, c * 512:(c + 1) * 512], gA[:, c:c + 1],
                                    rstd[sl_][:], ALU.mult, ALU.mult),
                         reads=[bxt[sl_], brstd[sl_], bsm], writes=[bxb[sl_]])

            def fm_group(t, gi):
                sl_ = t % 2
                b = fm_rot[rot[0] % 3]
                rot[0] += 1
                for c in range(8):
                    P.op("pe", mm(ps[b][:, :], W[:, c * 898 + gi * 128: c * 898 + (gi + 1) * 128], xb[sl_][:, c * 512:(c + 1) * 512],
                                  start=(c == 0), stop=(c == 7)), reads=[bxb[sl_], bW], writes=[psb[b]])
                return b

            def stageB1(t):
                sl_ = t % 2
                cols = slice(t * 512, (t + 1) * 512)
                b = fm_group(t, 0)
                P.op("act", act(qst[sl_][:], ps[b][:, :], AF.Copy, scale=0.125), reads=[psb[b]], writes=[bqst[sl_]])
                P.dma("sp", self.QD[:, cols], qst[sl_][:], reads=[bqst[sl_]], writes=[self.bQD[t]])
                b = fm_group(t, 1)
                P.op("act", act(kst[sl_][:], ps[b][:, :], AF.Copy), reads=[psb[b]], writes=[bkst[sl_]])
                P.dma("sp", self.KD[:, cols], kst[sl_][:], reads=[bkst[sl_]], writes=[self.bKD[t]])
                for g in range(3):
                    b = fm_group(t, 2 + g)
                    P.op("act", act(cs[g][sl_][:, 3:515], ps[b][:, :], AF.Copy), reads=[psb[b]], writes=[bcs[g][sl_]])
                    P.op("dve", cp(cs[g][sl_][:, 0:3], cs[g][1 - sl_][:, 512:515]), reads=[bcs[g][1 - sl_]], writes=[bcs[g][sl_]])
                for j in range(4):
                    for c in range(8):
                        P.op("pe", mm(ps[3][:, j * 128:(j + 1) * 128], xb[sl_][:, c * 512 + j * 128: c * 512 + (j + 1) * 128],
                                      W[:, c * 898 + 768: c * 898 + 896], start=(c == 0), stop=(c == 7)),
                             reads=[bxb[sl_], bW], writes=[psb[3]])
                    for c in range(8):
                        P.op("pe", mm(ps[7][:, j * 2:(j + 1) * 2], xb[sl_][:, c * 512 + j * 128: c * 512 + (j + 1) * 128],
                                      W[:, c * 898 + 896: c * 898 + 898], start=(c == 0), stop=(c == 7)),
                             reads=[bxb[sl_], bW], writes=[psb[7]])
                P.op("act", act(vst[sl_][:], ps[3][:, :], AF.Copy), reads=[psb[3]], writes=[bvst[sl_]])
                P.dma("sp", self.VD.rearrange("(n p) d -> p n d", p=128)[:, t * 4:(t + 1) * 4, :],
                      vst[sl_][:].rearrange("p (n d) -> p n d", n=4), reads=[bvst[sl_]], writes=[self.bVD[t]])
                P.op("dve", cp(self.GAB[:, t * 8:(t + 1) * 8], ps[7][:, 0:8]), reads=[psb[7]], writes=[self.bGAB[t]])
                b = fm_group(t, 5)
                P.op("act", act(self.Zs[:, cols], ps[b][:, :], AF.Silu), reads=[psb[b]], writes=[self.bZs[t]])

            def stageB2a(t):
                sl_ = t % 2
                for g in range(3):
                    P.op("dve", ts(yc[g][:], cs[g][sl_][:, 0:512], cw[:, g * 4:g * 4 + 1], None, ALU.mult),
                         reads=[bcs[g][sl_], bsm], writes=[byc[g]])
                for j in range(1, 4):
                    for g in range(3):
                        P.op("dve", stt(yc[g][:], cs[g][sl_][:, j:j + 512], cw[:, g * 4 + j:g * 4 + j + 1], yc[g][:], ALU.mult, ALU.add),
                             reads=[bcs[g][sl_], byc[g], bsm], writes=[byc[g]])
                for g in range(3):
                    if g < 2:
                        P.op("act", act(sl[g][:], yc[g][:], AF.Silu), reads=[byc[g]], writes=[bsl[g]])
                        P.op("pool", tt(s2[g][:], sl[g][:], sl[g][:], ALU.mult), reads=[bsl[g]], writes=[bs2[g]])
                    else:
                        P.op("act", act(vs[:], yc[g][:], AF.Silu), reads=[byc[g]], writes=[bvs])

            def stageB2b1(t):
                cols = slice(t * 512, (t + 1) * 512)
                for g in range(2):
                    P.op("pe", mm(ps[1][:, :], self.ones, s2[g][:]), reads=[bs2[g], bc], writes=[psb[1]])
                    self.rsqrt_act(rn[g][:], ps[1][:, :], lntmp[:], [psb[1]], [brn[g]], bln)
                    dst, bd = (self.QgT, self.bQg) if g == 0 else (self.KgT, self.bKg)
                    scl = 128.0 ** -0.5 if g == 0 else 1.0
                    P.op("dve", stt(dst[:, cols], sl[g][:], scl, rn[g][:], ALU.mult, ALU.mult),
                         reads=[bsl[g], brn[g]], writes=[bd[t]])

            def stageB2b2(t):
                cols = slice(t * 512, (t + 1) * 512)
                for (src, bsrc, dst, bdst) in ((vs[:], bvs, self.Vg, self.bVg[t]), (self.KgT[:, cols], self.bKg[t], self.Kt, self.bKt[t])):
                    for j in range(4):
                        P.op("pe", mm(ps[2][:, j * 128:(j + 1) * 128], src[:, j * 128:(j + 1) * 128], self.ident),
                             reads=[bsrc, bc], writes=[psb[2]])
                    P.op("act", act(dst[:, t * 512:(t + 1) * 512], ps[2][:, :], AF.Copy), reads=[psb[2]], writes=[bdst])

            stageA0(0)
            stageA1a(0)
            stageA1b(0)
            if NT > 1:
                stageA0(1)
            for t in range(NT):
                if t + 1 < NT:
                    stageA1a(t + 1)
                if t >= 1:
                    stageB2b1(t - 1)
                if t + 1 < NT:
                    stageA1b(t + 1)
                if t + 2 < NT:
                    stageA0(t + 2)
                stageB1(t)
                if t >= 1:
                    stageB2b2(t - 1)
                stageB2a(t)
            stageB2b1(NT - 1)
            stageB2b2(NT - 1)
            P.barrier()


    def _emit_wcasts(self):
        P = self.P
        P.dma("pool", self.wout_bf, self.wout[:, :], writes=[self.bwcast])
        P.dma("pool", self.wdown_bf, self.wdown[:, :], writes=[self.bwcast])
        for i in range(4):
            P.dma("pool", self.wup_bf[i * 256:(i + 1) * 256, :], self.wup[i * 256:(i + 1) * 256, :], writes=[self.bwcast])

    def phase_gdn(self):
        nc, P, S, NT, NK = self.nc, self.P, self.S, self.NT, self.NK
        ps, psb, bc = self.ps, self.psb, self.b_const
        cf = self.c_f32
        identf, onesf = cf[:, 0:128], cf[:, 128:256]
        tri_incl, blk, negm, strict = cf[:, 256:384], cf[:, 384:512], cf[:, 512:640], cf[:, 640:768]
        chm = [cf[:, 768:896], cf[:, 896:1024]]
        pv = self.c_pv
        with ExitStack() as ws:
            wsb = lambda n, s, d: ws.enter_context(nc.sbuf_tensor(n, s, d))
            gt_ = wsb("gtmp", [128, 8 * NK], F32)
            G, Bt, GC, GT, EG, EKD, NGC = [gt_[:, i * NK:(i + 1) * NK] for i in range(7)]
            GLb = [wsb("GLb%d" % c, [128, NK], F32) for c in range(2)]
            sc = wsb("gsc", [128, 4], F32)
            bg = Buf("gates")
            gab = self.GAB[:].rearrange("p (n t) -> p n t", t=2)
            rd = self.bGAB + [bc]
            P.op("act", act(G, gab[:, :, 0], AF.Exp, bias=pv[:, 3:4]), reads=rd, writes=[bg])
            P.op("act", act(G, G, AF.Ln, bias=1.0), reads=[bg], writes=[bg])
            P.op("act", act(sc[:, 0:1], pv[:, 2:3], AF.Exp), reads=[bc], writes=[bg])
            P.op("dve", ts(G, G, sc[:, 0:1], -1.0, ALU.mult, ALU.mult), reads=[bg], writes=[bg])
            P.op("act", act(Bt, gab[:, :, 1], AF.Exp, scale=-1.0), reads=rd + [bg], writes=[bg])
            P.op("dve", ts(Bt, Bt, 1.0, None, ALU.add), reads=[bg], writes=[bg])
            P.op("dve", lambda e: e.reciprocal(out=Bt, in_=Bt), reads=[bg], writes=[bg])
            P.op("pe", mm(ps[0][:, 0:NK], tri_incl, G), reads=[bg, bc], writes=[psb[0]])
            P.op("pe", mm(ps[0][:, NK:2 * NK], blk, G), reads=[bg, bc], writes=[psb[0]])
            P.op("pe", mm(ps[0][:, 2 * NK:3 * NK], chm[0], G), reads=[bg, bc], writes=[psb[0]])
            P.op("pe", mm(ps[0][:, 3 * NK:4 * NK], chm[1], G), reads=[bg, bc], writes=[psb[0]])
            P.op("dve", cp(GC, ps[0][:, 0:NK]), reads=[psb[0]], writes=[bg])
            P.op("dve", tt(GT, ps[0][:, NK:2 * NK], GC, ALU.subtract), reads=[psb[0], bg], writes=[bg])
            P.op("dve", ts(NGC, GC, -1.0, None, ALU.mult), reads=[bg], writes=[bg])
            P.op("act", act(EG, GC, AF.Exp), reads=[bg], writes=[bg])
            P.op("act", act(EKD, GT, AF.Exp), reads=[bg], writes=[bg])
            bgl = Buf()
            P.op("act", act(GLb[0][:], ps[0][:, 2 * NK:3 * NK], AF.Exp), reads=[psb[0]], writes=[bgl])
            P.op("act", act(GLb[1][:], ps[0][:, 3 * NK:4 * NK], AF.Exp), reads=[psb[0]], writes=[bgl])

            U_s = [wsb("U_s%d" % i, [128, 512], F32) for i in range(2)]
            WT_s = [wsb("WT_s%d" % i, [128, 512], BF16) for i in range(2)]
            KDEC_s = [wsb("KDEC_s%d" % i, [128, 512], BF16) for i in range(2)]
            QDECT_s = [wsb("QDECT_s%d" % i, [128, 512], BF16) for i in range(2)]
            QKT_s = [wsb("QKT_s%d" % i, [128, 512], BF16) for i in range(2)]
            bU = [Buf() for _ in range(2)]
            bWT = [Buf() for _ in range(2)]
            bKD = [Buf() for _ in range(2)]
            bQDT = [Buf() for _ in range(2)]
            bQKT = [Buf() for _ in range(2)]
            I4 = wsb("I4", [128, 512], BF16)
            ST4 = wsb("ST4", [128, 512], F32)
            bI4 = Buf()
            for j in range(4):
                P.op("pool", cp(I4[:, j * 128:(j + 1) * 128], self.ident), reads=[bc], writes=[bI4])
                P.op("pool", cp(ST4[:, j * 128:(j + 1) * 128], strict), reads=[bc], writes=[bI4])
            TriG = [wsb("TriG%d" % i, [128, 128], F32) for i in range(2)]
            bTriG = [Buf() for _ in range(2)]
            EGr = wsb("EGr", [128, 512], F32)
            Ei = wsb("Ei", [128, 512], F32)
            Es = wsb("Es", [128, 512], F32)
            KG = wsb("KG", [128, 512], BF16)
            bEGr, bEi, bEs, bKG = Buf(), Buf(), Buf(), Buf()
            Ub = [wsb("Ub%d" % i, [128, 512], BF16) for i in range(2)]
            Lb = [wsb("Lb%d" % i, [128, 512], BF16) for i in range(2)]
            Pb = [wsb("Pb%d" % i, [128, 512], BF16) for i in range(2)]
            bUb = [Buf() for _ in range(2)]
            bLb = [Buf() for _ in range(2)]
            bPb = [Buf() for _ in range(2)]
            wtok = wsb("wtok", [128, 512], BF16)
            bwtok = Buf()
            hb = lambda: [Buf(), Buf()]
            XB, YB = (0, 4), (1, 5)
            bUh, bWTh, bKDh, bQDTh, bQKTh = [hb(), hb()], [hb(), hb()], [hb(), hb()], [hb(), hb()], [hb(), hb()]
            TriG4 = [wsb("TriG4_%d" % i, [128, 128], F32) for i in range(4)]
            bTriG4 = [Buf() for _ in range(4)]
            EGrh = [wsb("EGrh%d" % i, [128, 256], F32) for i in range(2)]
            Eih = [wsb("Eih%d" % i, [128, 256], F32) for i in range(2)]
            Esh = [wsb("Esh%d" % i, [128, 256], F32) for i in range(2)]
            KGh = [wsb("KGh%d" % i, [128, 256], BF16) for i in range(2)]
            wtokh = [wsb("wtokh%d" % i, [128, 256], BF16) for i in range(2)]
            ULh = [[wsb("UL%d_%d" % (i, j), [128, 512], BF16) for j in range(2)] for i in range(2)]
            Pbh = [[wsb("Pbh%d_%d" % (i, j), [128, 256], BF16) for j in range(2)] for i in range(2)]
            bEGrh, bEih, bEsh, bKGh, bwtokh = hb(), hb(), hb(), hb(), hb()
            bULh = [hb(), hb()]
            bPbh = [hb(), hb()]

            def step2h(tg, hz):
                so = tg % 2
                X, Y = ps[XB[hz]], ps[YB[hz]]
                bX, bY = psb[XB[hz]], psb[YB[hz]]
                g0 = hz * 256
                GC_ = slice(g0, g0 + 256)
                cH = slice(tg * 512 + g0, tg * 512 + g0 + 256)
                ci = [slice(0, 128), slice(128, 256)]
                gci = [slice(g0, g0 + 128), slice(g0 + 128, g0 + 256)]
                tts = [tg * 4 + 2 * hz, tg * 4 + 2 * hz + 1]
                cts = [slice(t * 128, (t + 1) * 128) for t in tts]
                UL, Pb_ = ULh[hz], Pbh[hz]
                bUL, bPb_ = bULh[hz], bPbh[hz]
                U_ = lambda j: UL[j][:, 0:256]
                L_ = lambda j: UL[j][:, 256:512]
                for i in range(2):
                    t = tts[i]
                    tl = 2 * hz + i
                    P.op("dve", ts(TriG4[tl][:], tri_incl, G[:, t:t + 1], None, ALU.mult), reads=[bg, bc], writes=[bTriG4[tl]])
                    P.op("pe", mm(X[:, ci[i]], onesf, TriG4[tl][:]), reads=[bTriG4[tl], bc], writes=[bX])
                    P.op("pe", mm(Y[:, ci[i]], onesf, TriG4[tl][:], start=True, stop=False), reads=[bTriG4[tl], bc], writes=[bY])
                    P.op("pe", mm(Y[:, ci[i]], identf, negm, start=False, stop=True), reads=[bc], writes=[bY])
                yield
                P.op("act", act(EGrh[hz][:], X[:, 0:256], AF.Exp), reads=[bX], writes=[bEGrh[hz]])
                for i in range(2):
                    P.op("act", act(Eih[hz][:, ci[i]], Y[:, ci[i]], AF.Exp, bias=NGC[:, tts[i]:tts[i] + 1]), reads=[bY, bg], writes=[bEih[hz]])
                P.op("dve", tt(QDECT_s[so][:, GC_], self.QgT[:, cH], EGrh[hz][:], ALU.mult), reads=[self.bQg[tg], bEGrh[hz]], writes=[bQDTh[so][hz]])
                P.op("dve", tt(Esh[hz][:], Eih[hz][:], ST4[:, 0:256], ALU.mult), reads=[bEih[hz], bI4], writes=[bEsh[hz]])
                yield
                for i in range(2):
                    P.op("pe", mm(X[:, ci[i]], self.KgT[:, cts[i]], self.QgT[:, cts[i]]), reads=[self.bKg[tg], self.bQg[tg]], writes=[bX])
                    P.op("pe", mm(Y[:, ci[i]], self.KgT[:, cts[i]], self.KgT[:, cts[i]]), reads=[self.bKg[tg]], writes=[bY])
                P.op("dve", tt(QKT_s[so][:, GC_], X[:, 0:256], Eih[hz][:], ALU.mult), reads=[bX, bEih[hz]], writes=[bQKTh[so][hz]])
                for i in range(2):
                    t = tts[i]
                    P.op("dve", stt(UL[0][:, ci[i]], Y[:, ci[i]], Bt[:, t:t + 1], Esh[hz][:, ci[i]], ALU.mult, ALU.mult),
                         reads=[bY, bEsh[hz], bg], writes=[bUL[0]])
                    P.op("act", act(KGh[hz][:, ci[i]], self.Kt[:, cts[i]], AF.Copy, scale=EG[:, t:t + 1]), reads=[self.bKt[tg], bg], writes=[bKGh[hz]])
                    P.op("act", act(KDEC_s[so][:, gci[i]], self.Kt[:, cts[i]], AF.Copy, scale=EKD[:, t:t + 1]),
                         reads=[self.bKt[tg], bg], writes=[bKDh[so][hz]])
                yield
                P.op("dve", stt(Pb_[0][:], U_(0), -1.0, I4[:, 0:256], ALU.mult, ALU.add), reads=[bUL[0], bI4], writes=[bPb_[0]])
                for i in range(2):
                    P.op("pe", mm(X[:, ci[i]], UL[0][:, ci[i]], self.ident), reads=[bUL[0], bc], writes=[bX])
                P.op("act", act(L_(0), X[:, 0:256], AF.Copy), reads=[bX], writes=[bUL[0]])
                yield
                cu, pc = 0, 0
                for k in range(5):
                    nx = 1 - cu
                    for i in range(2):
                        ui, li = ci[i], slice(256 + i * 128, 256 + (i + 1) * 128)
                        if k < 4:
                            P.op("pe", mm(X[:, ui], UL[cu][:, li], UL[cu][:, ui]), reads=[bUL[cu]], writes=[bX])
                        P.op("pe", mm(X[:, li], UL[cu][:, ui], UL[cu][:, li]), reads=[bUL[cu]], writes=[bX])
                        if k >= 1:
                            P.op("pe", mm(Y[:, ui], UL[cu][:, li], Pb_[pc][:, ui]), reads=[bUL[cu], bPb_[pc]], writes=[bY])
                    if k < 4:
                        P.op("act", act(UL[nx][:, :], X[:, :], AF.Copy), reads=[bX], writes=[bUL[nx]])
                    else:
                        P.op("act", act(L_(nx), X[:, 256:512], AF.Copy), reads=[bX], writes=[bUL[nx]])
                    if k >= 1:
                        P.op("dve", tt(Pb_[1 - pc][:], Y[:, 0:256], Pb_[pc][:], ALU.add), reads=[bY, bPb_[pc]], writes=[bPb_[1 - pc]])
                        pc = 1 - pc
                    yield
                    cu = nx
                for i in range(2):
                    li = slice(256 + i * 128, 256 + (i + 1) * 128)
                    P.op("pe", mm(Y[:, ci[i]], UL[cu][:, li], Pb_[pc][:, ci[i]]), reads=[bUL[cu], bPb_[pc]], writes=[bY])
                P.op("dve", tt(Pb_[1 - pc][:], Y[:, 0:256], Pb_[pc][:], ALU.add), reads=[bY, bPb_[pc]], writes=[bPb_[1 - pc]])
                pc = 1 - pc
                yield
                Qm, bQm = Pb_[pc], bPb_[pc]
                for i in range(2):
                    P.op("pe", mm(X[:, ci[i]], Qm[:, ci[i]], self.Vg[:, cts[i]]), reads=[bQm, self.bVg[tg]], writes=[bX])
                    P.op("pe", mm(Y[:, ci[i]], Qm[:, ci[i]], KGh[hz][:, ci[i]]), reads=[bQm, bKGh[hz]], writes=[bY])
                for i in range(2):
                    t = tts[i]
                    P.op("dve", ts(U_s[so][:, gci[i]], X[:, ci[i]], Bt[:, t:t + 1], None, ALU.mult), reads=[bX, bg], writes=[bUh[so][hz]])
                    P.op("act", act(wtokh[hz][:, ci[i]], Y[:, ci[i]], AF.Copy, scale=Bt[:, t:t + 1]), reads=[bY, bg], writes=[bwtokh[hz]])
                yield
                for i in range(2):
                    P.op("pe", mm(X[:, ci[i]], wtokh[hz][:, ci[i]], self.ident), reads=[bwtokh[hz], bc], writes=[bX])
                P.op("act", act(WT_s[so][:, GC_], X[:, 0:256], AF.Copy), reads=[bX], writes=[bWTh[so][hz]])
                yield

            def step2(tg):
                ga, gb = step2h(tg, 0), step2h(tg, 1)
                while True:
                    ra = next(ga, "done")
                    rb = next(gb, "done")
                    if ra == "done" and rb == "done":
                        return
                    yield

            St = wsb("St", [128, 128], F32)
            Sb = [wsb("Sb%d" % i, [128, 128], BF16) for i in range(2)]
            vn = [wsb("vn%d" % i, [128, 128], BF16) for i in range(2)]
            bSt = Buf()
            bSb = [Buf() for _ in range(2)]
            bvn = [Buf() for _ in range(2)]
            og = wsb("og", [128, 512], F32)
            ogs = wsb("ogs", [128, 512], BF16)
            ogl = wsb("ogl", [128, 512], F32)
            ogr = wsb("ogr", [128, 512], F32)
            ogm = [wsb("ogm%d" % i, [128, 512], BF16) for i in range(2)]
            bog, bogs, bogl, bogr = Buf(), Buf(), Buf(), Buf()
            bogm = [Buf() for _ in range(2)]
            P.op("dve", lambda e: e.memset(St[:], 0.0), writes=[bSt])
            P.op("dve", lambda e: e.memset(Sb[0][:], 0.0), writes=[bSb[0]])
            self._cur = 0

            def scan_chunk(n, gen=None):
                adv = (lambda: next(gen, None)) if gen is not None else (lambda: None)
                cur = self._cur
                t, half = n // 2, n % 2
                tg = t // 4
                so = tg % 2
                tl = t % 4
                r0 = 64 * half
                cl = slice(tl * 128, (tl + 1) * 128)
                cc = slice(tl * 128 + r0, tl * 128 + r0 + 64)
                pa, pb_, po = 2, 3, 6 + (n // 8) % 2
                oc = slice((n % 8) * 64, (n % 8 + 1) * 64)
                v_ = vn[n % 2]
                bv_ = bvn[n % 2]
                hz = tl // 2
                P.op("pe", mm(ps[pa][:, 0:128], WT_s[so][:, cl], Sb[cur][:]), reads=[bWTh[so][hz], bSb[cur]], writes=[psb[pa]])
                P.op("pe", mm(ps[po][:, oc], Sb[cur][:], QDECT_s[so][:, cc], start=True, stop=False),
                     reads=[bSb[cur], bQDTh[so][hz]], writes=[psb[po]])
                P.op("dve", tt(v_[r0:r0 + 64, :], U_s[so][r0:r0 + 64, cl], ps[pa][r0:r0 + 64, 0:128], ALU.subtract),
                     reads=[bUh[so][hz], psb[pa]], writes=[bv_])
                adv()
                P.op("pe", mm(ps[pb_][:, 0:128], KDEC_s[so][r0:r0 + 64, cl], v_[r0:r0 + 64, :]), reads=[bKDh[so][hz], bv_], writes=[psb[pb_]])
                P.op("pe", mm(ps[po][:, oc], v_[r0:r0 + 64, :], QKT_s[so][r0:r0 + 64, cc], start=False, stop=True),
                     reads=[bv_, bQKTh[so][hz]], writes=[psb[po]])
                nxt = 1 - cur
                adv()
                P.op("dve", stt(Sb[nxt][:], St[:], GLb[half][:, t:t + 1], ps[pb_][:, 0:128], ALU.mult, ALU.add),
                     reads=[bSt, bgl, psb[pb_]], writes=[bSb[nxt]])
                P.op("dve", stt(St[:], St[:], GLb[half][:, t:t + 1], ps[pb_][:, 0:128], ALU.mult, ALU.add),
                     reads=[bSt, bgl, psb[pb_]], writes=[bSt])
                self._cur = nxt
                adv()
                if n % 8 == 7:
                    q8 = n // 8
                    c512 = slice(q8 * 512, (q8 + 1) * 512)
                    om = ogm[q8 % 2]
                    P.op("act", act(og[:], ps[po][:, :], AF.Copy), reads=[psb[po]], writes=[bog])
                    P.op("pool", tt(ogs[:], og[:], og[:], ALU.mult), reads=[bog], writes=[bogs])
                    P.op("pe", mm(ps[pb_][:, :], self.ones, ogs[:]), reads=[bogs, bc], writes=[psb[pb_]])
                    P.op("act", act(ogl[:], ps[pb_][:, :], AF.Ln, bias=EPS, scale=1.0 / 128), reads=[psb[pb_]], writes=[bogl])
                    P.op("act", act(ogr[:], ogl[:], AF.Exp, scale=-0.5), reads=[bogl], writes=[bogr])
                    P.op("dve", stt(og[:], og[:], pv[:, 1:2], ogr[:], ALU.mult, ALU.mult), reads=[bog, bogr, bc], writes=[bog])
                    P.op("dve", tt(om[:], og[:], self.Zs[:, c512], ALU.mult), reads=[bog, self.bZs[q8]], writes=[bogm[q8 % 2]])
                    self.mix_out(1, q8, om, bogm[q8 % 2])

            wc_list = [(self.wout_bf, self.wout[:, :]), (self.wdown_bf, self.wdown[:, :])]
            wc_list += [(self.wup_bf[i * 256:(i + 1) * 256, :], self.wup[i * 256:(i + 1) * 256, :]) for i in range(4)]
            for _ in step2(0):
                pass
            for tg in range(NT):
                gen = step2(tg + 1) if tg + 1 < NT else iter(())
                for n in range(tg * 8, tg * 8 + 8):
                    scan_chunk(n, gen)
                for _ in gen:
                    pass
                if wc_list and (tg % 2 == 1 or tg == NT - 1):
                    for _ in range(1 if tg < NT - 1 else len(wc_list)):
                        o_, i_ = wc_list.pop(0)
                        P.dma("pool", o_, i_, writes=[self.bwcast])
            self.mix_flush(1)
            P.barrier()

    def phase_attn(self):
        nc, P, S, NT = self.nc, self.P, self.S, self.NT
        ps, psb, bc = self.ps, self.psb, self.b_const
        with ExitStack() as ws:
            wsb = lambda n, s, d: ws.enter_context(nc.sbuf_tensor(n, s, d))
            QA = [wsb("QA0", [68, S], BF16), wsb("QA1", [68, S], BF16)]
            KA = [wsb("KA0", [68, S], BF16), wsb("KA1", [68, S], BF16)]
            V = wsb("Vat", [128, self.NK * 128], BF16)
            bQA, bKA, bV = Buf(), Buf(), Buf()
            allsc = self.bQD + self.bKD + self.bVD
            P.dma("sp", QA[0][0:64, :], self.QD[0:64, :], reads=allsc, writes=[bQA])
            P.dma("sp", QA[1][0:64, :], self.QD[64:128, :], reads=allsc, writes=[bQA])
            P.dma("sp", KA[0][0:64, :], self.KD[0:64, :], reads=allsc, writes=[bKA])
            P.dma("sp", KA[1][0:64, :], self.KD[64:128, :], reads=allsc, writes=[bKA])
            P.dma("sp", QA[0][64:68, :], self.qaug[:, :], writes=[bQA])
            P.dma("sp", KA[0][64:68, :], self.kaug[:, :], writes=[bKA])
            bQA1, bKA1 = Buf(), Buf()
            P.dma("sp", QA[1][64:68, :], self.qaug[:, :], writes=[bQA1])
            P.dma("sp", KA[1][64:68, :], self.kaug[:, :], writes=[bKA1])
            Vv = V[:].rearrange("p (n d) -> p n d", d=128)
            VDv = self.VD.rearrange("(n p) d -> p n d", p=128)
            for i in range(0, self.NK, 8):
                P.dma("sp", Vv[:, i:i + 8, :], VDv[:, i:i + 8, :], reads=allsc, writes=[bV])
            rows = [(0, 68), (0, 68)]
            lam = wsb("lam", [128, 256], F32)
            lt = wsb("lamt", [128, 128], F32)
            lsc = wsb("lsc", [128, 8], F32)
            blam = Buf()
            P.dma("sp", lam[:], self.lamv[:, :], writes=[blam])
            P.op("dve", tt(lt[:, 0:64], lam[:, 0:64], lam[:, 64:128], ALU.mult), reads=[blam], writes=[blam])
            P.op("dve", tt(lt[:, 64:128], lam[:, 128:192], lam[:, 192:256], ALU.mult), reads=[blam], writes=[blam])
            P.op("dve", lambda e: e.reduce_sum(out=lsc[:, 0:1], in_=lt[:, 0:64], axis=AX.X), reads=[blam], writes=[blam])
            P.op("dve", lambda e: e.reduce_sum(out=lsc[:, 1:2], in_=lt[:, 64:128], axis=AX.X), reads=[blam], writes=[blam])
            P.op("act", act(lsc[:, 2:4], lsc[:, 0:2], AF.Exp), reads=[blam], writes=[blam])
            P.op("dve", stt(lsc[:, 4:5], lsc[:, 3:4], -0.2, lsc[:, 2:3], ALU.add, ALU.subtract), reads=[blam], writes=[blam])
            P.op("dve", ts(lsc[:, 5:6], self.c_pv[:, 0:1], 0.8, None, ALU.mult), reads=[blam, bc], writes=[blam])
            neglam = lsc[:, 4:5]
            gsub = lsc[:, 5:6]

            pt = [wsb("pt%d" % i, [128, 512], BF16) for i in range(4)]
            bpt = [Buf() for _ in range(4)]
            rl = wsb("rl", [128, 512], F32)
            brl = Buf()
            On = [wsb("On%d" % i, [128, 512], F32) for i in range(2)]
            bOn = [Buf() for _ in range(2)]
            oa = wsb("oa", [128, 512], F32)
            boa = Buf()
            osq = wsb("osq", [128, 512], BF16)
            bosq = Buf()
            lnt = wsb("lnt2", [128, 512], F32)
            blnt = Buf()
            rs = wsb("rs", [128, 512], F32)
            brs = Buf()
            mst = [wsb("mst%d" % i, [128, 512], BF16) for i in range(2)]
            bmst = [Buf() for _ in range(2)]

            blocks = []
            for qi in range(NT):
                for c in range(2):
                    nkt = 4 * (qi + 1)
                    for kt in range(nkt):
                        blocks.append((qi, c, kt, nkt))
            nb = len(blocks)

            def qk(i):
                qi, c, kt, nkt = blocks[i]
                j = kt - 4 * qi
                col0 = 128 * j if j >= 0 else 0
                b = i % 3
                r0, r1 = rows[c]
                P.op("pe", mm(ps[b][:, col0:512], KA[c][r0:r1, kt * 128:(kt + 1) * 128],
                              QA[c][r0:r1, qi * 512 + col0:(qi + 1) * 512]),
                     reads=[bQA, bKA, bQA1, bKA1], writes=[psb[b]])

            def rest(i):
                qi, c, kt, nkt = blocks[i]
                g = qi * 2 + c
                j = kt - 4 * qi
                col0 = 128 * j if j >= 0 else 0
                b = i % 3
                sl_ = i % 4
                P.op("act", act(pt[sl_][:, col0:512], ps[b][:, col0:512], AF.Exp), reads=[psb[b]], writes=[bpt[sl_]])
                if j >= 0:
                    P.op("dve", tt(pt[sl_][:, col0:col0 + 128], pt[sl_][:, col0:col0 + 128], self.tri, ALU.mult),
                         reads=[bpt[sl_], bc], writes=[bpt[sl_]])
                po, pl = 3 + g % 2, 5 + g % 2
                P.op("pe", mm(ps[po][:, col0:512], V[:, kt * 128:(kt + 1) * 128], pt[sl_][:, col0:512],
                              start=(kt == 0), stop=(kt == nkt - 1)), reads=[bV, bpt[sl_]], writes=[psb[po]])
                P.op("pe", mm(ps[pl][:, col0:512], self.ones, pt[sl_][:, col0:512],
                              start=(kt == 0), stop=(kt == nkt - 1)), reads=[bc, bpt[sl_]], writes=[psb[pl]])
                if kt == nkt - 1:
                    cols = slice(qi * 512, (qi + 1) * 512)
                    P.op("dve", lambda e: e.reciprocal(out=rl[:], in_=ps[pl][:, :]), reads=[psb[pl]], writes=[brl])
                    P.op("dve", tt(On[c][:], ps[po][:, :], rl[:], ALU.mult), reads=[psb[po], brl], writes=[bOn[c]])
                    if c == 1:
                        P.op("dve", stt(oa[:], On[1][:], neglam, On[0][:], ALU.mult, ALU.add),
                             reads=[bOn[0], bOn[1], blam], writes=[boa])
                        P.op("pool", tt(osq[:], oa[:], oa[:], ALU.mult), reads=[boa], writes=[bosq])
                        P.op("pe", mm(ps[7][:, :], self.ones, osq[:]), reads=[bosq, bc], writes=[psb[7]])
                        P.op("act", act(lnt[:], ps[7][:, :], AF.Ln, bias=EPS, scale=1.0 / 128), reads=[psb[7]], writes=[blnt])
                        P.op("act", act(rs[:], lnt[:], AF.Exp, scale=-0.5), reads=[blnt], writes=[brs])
                        ms = mst[qi % 2]
                        P.op("dve", stt(ms[:], oa[:], gsub, rs[:], ALU.mult, ALU.mult),
                             reads=[boa, brs, blam], writes=[bmst[qi % 2]])
                        self.mix_out(0, qi, ms, bmst[qi % 2])

            qk(0)
            if nb > 1:
                qk(1)
            for i in range(nb):
                if i + 2 < nb:
                    pass
                self._attn_step(i, nb, qk, rest)
            self.mix_flush(0)
            P.barrier()

    def _attn_step(self, i, nb, qk, rest):
        if i + 2 < nb:
            qk(i + 2)
        rest(i)


    def _pidj(self, e):
        if getattr(self, "_pj", None) is None:
            pid = e.partition_id()
            self._pj = (pid % 4) * 8
        return self._pj

    def _mix_init(self):
        if hasattr(self, "zt"):
            return
        nc, P, es = self.nc, self.P, self.es
        self.zt = es.enter_context(nc.sbuf_tensor("zt", [128, 4], BF16))
        self.bz = Buf()
        P.op("pool", lambda e: e.memset(self.zt[:], 0.0), writes=[self.bz])
        self.bagi = [[Buf() for _ in range(4)] for _ in range(2)]
        self.bago = [Buf() for _ in range(8)]
        self._pend = [[], []]
        self.direct = (self.TPC % 512 == 0)
        self.agi = self.ag_in.ap().rearrange("(j f) t -> j f t", j=4)

    def _collective(self, half, j):
        P = self.P
        c = 2 * j + half
        ag_in, ag_out = self.ag_in, self.ag_out
        P.collective(lambda e, c=c: e.collective_compute(
            "AllGather", ALU.bypass, replica_groups=[[0, 1, 2, 3], [4, 5, 6, 7]],
            ins=[ag_in.ap()[c * 128:(c + 1) * 128, :]], outs=[ag_out.ap()[c * 512:(c + 1) * 512, :]]),
            self.cc_sem, reads=[self.bagi[half][j]], writes=[self.bago[c]])

    def mix_out(self, half, q8, ms, bms):
        P, TPC = self.P, self.TPC
        self._mix_init()
        r0, r1 = half * 128, (half + 1) * 128
        cols = slice(q8 * 512, (q8 + 1) * 512)
        bm = (self.bMIXa if half == 0 else self.bMIXg)[q8]
        for j in self._pend[half]:
            self._collective(half, j)
        self._pend[half] = []
        if self.debug or not self.direct:
            P.dma("sp", self.MIXD[r0:r1, cols], ms[:], reads=[bms], writes=[bm])
        if not self.direct:
            return
        j = (q8 * 512) // TPC
        off = q8 * 512 - j * TPC
        bi = self.bagi[half]
        if q8 == 0:
            P.dma("sp", self.agi[0, r0:r1, 0:2], self.zt[:, 0:2], reads=[self.bz], writes=[bi[0]])
        P.dma("sp", self.agi[j, r0:r1, 2 + off:2 + off + 512], ms[:], reads=[bms], writes=[bi[j]])
        if off + 512 == TPC:
            if j + 1 < 4:
                P.dma("sp", self.agi[j + 1, r0:r1, 0:2], ms[:, 510:512], reads=[bms], writes=[bi[j + 1]])
            self._pend[half].append(j)

    def mix_flush(self, half):
        P, TPC = self.P, self.TPC
        self._mix_init()
        W2 = TPC + 2
        if not self.direct:
            allm = self.bMIXa if half == 0 else self.bMIXg
            r0, r1 = half * 128, (half + 1) * 128
            for j in range(4):
                bi = self.bagi[half][j]
                P.dma("sp", self.agi[j, r0:r1, 2:W2], self.MIXD[r0:r1, j * TPC:(j + 1) * TPC], reads=allm, writes=[bi])
                if j > 0:
                    P.dma("sp", self.agi[j, r0:r1, 0:2], self.MIXD[r0:r1, j * TPC - 2:j * TPC], reads=allm, writes=[bi])
                else:
                    P.dma("sp", self.agi[0, r0:r1, 0:2], self.zt[:, 0:2], reads=[self.bz], writes=[bi])
                self._pend[half].append(j)
        for j in self._pend[half]:
            self._collective(half, j)
        self._pend[half] = []

    def phase2(self):
        nc, P, S, TPC = self.nc, self.P, self.S, self.TPC
        ps, psb, bc = self.ps, self.psb, self.b_const
        W2 = TPC + 2
        NH = 2
        HT = TPC // NH
        nt = -(-(HT + 2) // 512)
        base = (HT + 2) // nt
        tiles = []
        a0 = 0
        for i in range(nt):
            w = base + (1 if i < (HT + 2) - base * nt else 0)
            tiles.append((a0, w))
            a0 += w
        WM = max(w for _, w in tiles)
        NFC = D_FF // 128
        with ExitStack() as ws:
            wsb = lambda n, s, d: ws.enter_context(nc.sbuf_tensor(n, s, d))
            MIXT = wsb("MIXT", [128, 8 * W2], BF16)
            Wout, Wd = self.Wout_sb, self.Wd_sb
            H2 = wsb("H2", [128, 8 * (HT + 2)], BF16)
            ACTT = wsb("ACTT", [128, NFC * HT], BF16)
            g2 = wsb("g2_sb", [128, 8], F32)
            fcw = wsb("fcw_sb", [128, 44 * 4], F32)
            gfin = wsb("gfin_sb", [128, 1024], F32)
            bWout, bWd, bH2, bsm = self.bWout, self.bWd, Buf(), Buf()
            bMIXTk = [Buf() for _ in range(8)]
            bACTT = [Buf() for _ in range(NFC)]
            agv = self.ag_out.ap().rearrange("(r f) t -> r f t", f=128)
            MIXTv = MIXT[:].rearrange("p (c t) -> p c t", c=8)
            for h in range(4):
                for half in range(2):
                    P.dma("pool", MIXTv[:, 2 * h + half, :],
                          lambda e, h=h, half=half: agv[bass.ds(self._pidj(e) + (4 * half + h), 1), :, :]
                          .rearrange("o f t -> (o f) t"),
                          reads=self.bago, writes=[bMIXTk[2 * h + half]])
            P.dma("sp", g2[:], self.g2[:, :], writes=[bsm])
            P.dma("sp", fcw[:], self.fcw[:, :], writes=[bsm])
            P.dma("sp", gfin[:], self.gfin[:, :], writes=[bsm])

            for hf in range(NH):
                c0 = hf * HT
                with ExitStack() as wa:
                    asb = lambda n, s, d: wa.enter_context(nc.sbuf_tensor(n + '_h%d' % hf, s, d))
                    x2t = [asb("x2t%d" % i, [128, 8 * WM], F32) for i in range(2)]
                    x1T = asb("x1T", [128, 8 * WM], F32)
                    sq = asb("sq2", [128, 8 * WM], BF16)
                    lnt = asb("lnA", [128, WM], F32)
                    rstd = asb("rstdA", [128, WM], F32)
                    bx2t = [Buf() for _ in range(2)]
                    bx1, bsq, bln, brs = Buf(), Buf(), Buf(), Buf()
                    for ti, (a0, w) in enumerate(tiles):
                        cols = slice(c0 + a0, c0 + a0 + w)
                        xs = x2t[ti % 2]
                        P.dma("sp", xs[:, 0:8 * w].rearrange("p (c t) -> p c t", c=8),
                              self.xT2[:, cols].rearrange("(c p) t -> p c t", p=128), writes=[bx2t[ti % 2]])
                        for oc in range(8):
                            b = oc % 4
                            for kc in range(8):
                                P.op("pe", mm(ps[b][:, 0:w], Wout[:, kc * 1024 + oc * 128: kc * 1024 + (oc + 1) * 128],
                                              MIXT[:, kc * W2 + c0 + a0: kc * W2 + c0 + a0 + w], start=(kc == 0), stop=(kc == 7)),
                                     reads=[bWout, bMIXTk[kc]], writes=[psb[b]])
                            P.op("dve", tt(x1T[:, oc * w:(oc + 1) * w], ps[b][:, 0:w], xs[:, oc * w:(oc + 1) * w], ALU.add),
                                 reads=[psb[b], bx2t[ti % 2]], writes=[bx1])
                        P.op("pool", tt(sq[:, 0:8 * w], x1T[:, 0:8 * w], x1T[:, 0:8 * w], ALU.mult), reads=[bx1], writes=[bsq])
                        for c in range(8):
                            P.op("pe", mm(ps[4][:, 0:w], self.onesmean, sq[:, c * w:(c + 1) * w], start=(c == 0), stop=(c == 7)),
                                 reads=[bsq, bc], writes=[psb[4]])
                        P.op("act", act(lnt[:, 0:w], ps[4][:, 0:w], AF.Ln, bias=EPS), reads=[psb[4]], writes=[bln])
                        P.op("act", act(rstd[:, 0:w], lnt[:, 0:w], AF.Exp, scale=-0.5), reads=[bln], writes=[brs])
                        for kc in range(8):
                            P.op("dve", stt(H2[:, kc * (HT + 2) + a0: kc * (HT + 2) + a0 + w], x1T[:, kc * w:(kc + 1) * w],
                                            g2[:, kc:kc + 1], rstd[:, 0:w], ALU.mult, ALU.mult),
                                 reads=[bx1, brs, bsm], writes=[bH2])
                    P.barrier()
                if self.stop == "p2A":
                    break
                with ExitStack() as wc:
                    csb = lambda n, s, d: wc.enter_context(nc.sbuf_tensor(n + '_h%d' % hf, s, d))
                    Wg = [csb("Wg%d" % i, [128, 8 * 128], BF16) for i in range(2)]
                    Wv = [csb("Wv%d" % i, [128, 8 * 128], BF16) for i in range(2)]
                    cg = [csb("cg%d" % i, [128, HT], F32) for i in range(2)]
                    cv = [csb("cv%d" % i, [128, HT], F32) for i in range(2)]
                    sg = [csb("sg%d" % i, [128, HT], F32) for i in range(2)]
                    bWg = [Buf() for _ in range(2)]
                    bWv = [Buf() for _ in range(2)]
                    bcg = [Buf() for _ in range(2)]
                    bcv = [Buf() for _ in range(2)]
                    bsg = [Buf() for _ in range(2)]
                    nto = -(-HT // 510)
                    bo = HT // nto
                    otiles = []
                    o0 = 0
                    for i in range(nto):
                        wo = bo + (1 if i < HT - bo * nto else 0)
                        otiles.append((o0, wo))
                        o0 += wo
                    H2W = HT + 2

                    def loadw(fc):
                        sl_ = fc % 2
                        P.dma("sp", Wg[sl_][:].rearrange("p (c n) -> p c n", c=8),
                              self.wup_bf[:, fc * 128:(fc + 1) * 128].rearrange("(c p) n -> p c n", p=128),
                              reads=[self.bwcast], writes=[bWg[sl_]])
                        P.dma("sp", Wv[sl_][:].rearrange("p (c n) -> p c n", c=8),
                              self.wup_bf[:, D_FF + fc * 128: D_FF + (fc + 1) * 128].rearrange("(c p) n -> p c n", p=128),
                              reads=[self.bwcast], writes=[bWv[sl_]])

                    loadw(0)
                    pr = 0
                    for fc in range(NFC):
                        sl_ = fc % 2
                        if fc + 1 < NFC:
                            loadw(fc + 1)
                        wg_ = fcw[:, fc * 4: fc * 4 + 4]
                        wv_ = fcw[:, (NFC + fc) * 4: (NFC + fc) * 4 + 4]
                        for (o0, wo) in otiles:
                            bg_, bv_ = pr % 8, (pr + 1) % 8
                            pr += 2
                            n_ = wo + 2
                            for kc in range(8):
                                P.op("pe", mm(ps[bg_][:, 0:n_], Wg[sl_][:, kc * 128:(kc + 1) * 128],
                                              H2[:, kc * H2W + o0: kc * H2W + o0 + n_], start=(kc == 0), stop=(kc == 7)),
                                     reads=[bWg[sl_], bH2], writes=[psb[bg_]])
                            for kc in range(8):
                                P.op("pe", mm(ps[bv_][:, 0:n_], Wv[sl_][:, kc * 128:(kc + 1) * 128],
                                              H2[:, kc * H2W + o0: kc * H2W + o0 + n_], start=(kc == 0), stop=(kc == 7)),
                                     reads=[bWv[sl_], bH2], writes=[psb[bv_]])
                            og_ = cg[sl_][:, o0:o0 + wo]
                            ov_ = cv[sl_][:, o0:o0 + wo]
                            P.op("act", act(og_, ps[bg_][:, 0:wo], AF.Identity, bias=wg_[:, 3:4], scale=wg_[:, 0:1]),
                                 reads=[psb[bg_], bsm], writes=[bcg[sl_]])
                            P.op("act", act(ov_, ps[bv_][:, 0:wo], AF.Identity, bias=wv_[:, 3:4], scale=wv_[:, 0:1]),
                                 reads=[psb[bv_], bsm], writes=[bcv[sl_]])
                            for j in (1, 2):
                                P.op("dve", stt(og_, ps[bg_][:, j:j + wo], wg_[:, j:j + 1], og_, ALU.mult, ALU.add),
                                     reads=[psb[bg_], bsm], writes=[bcg[sl_]])
                                P.op("dve", stt(ov_, ps[bv_][:, j:j + wo], wv_[:, j:j + 1], ov_, ALU.mult, ALU.add),
                                     reads=[psb[bv_], bsm], writes=[bcv[sl_]])
                        P.op("act", act(sg[sl_][:], cg[sl_][:], AF.Silu), reads=[bcg[sl_]], writes=[bsg[sl_]])
                        P.op("pool", tt(ACTT[:, fc * HT:(fc + 1) * HT], sg[sl_][:], cv[sl_][:], ALU.mult),
                             reads=[bsg[sl_], bcv[sl_]], writes=[bACTT[fc]])
                    P.barrier()
                if self.stop == "p2C":
                    break
                with ExitStack() as wd:
                    dsb = lambda n, s, d: wd.enter_context(nc.sbuf_tensor(n + '_h%d' % hf, s, d))
                    xk = [dsb("xk%d" % i, [128, 1024], F32) for i in range(2)]
                    x2 = [dsb("x2_%d" % i, [128, 1024], F32) for i in range(2)]
                    sqd = dsb("sqd", [128, 1024], F32)
                    ot = [dsb("ot%d" % i, [128, 1024], F32) for i in range(2)]
                    st = dsb("std", [128, 8], F32)
                    bxk = [Buf() for _ in range(2)]
                    bx2 = [Buf() for _ in range(2)]
                    bot = [Buf() for _ in range(2)]
                    bsqd, bst = Buf(), Buf()
                    for sub in range(HT // 128):
                        s2_ = sub % 2
                        tok0 = hf * HT + sub * 128
                        P.dma("sp", xk[s2_][:], self.xtok2[tok0:tok0 + 128, :], writes=[bxk[s2_]])
                        for oh in range(2):
                            b = (sub * 2 + oh) % 4
                            for fc in range(NFC):
                                P.op("pe", mm(ps[b][:, :], ACTT[:, fc * HT + sub * 128: fc * HT + (sub + 1) * 128],
                                              Wd[:, fc * 1024 + oh * 512: fc * 1024 + (oh + 1) * 512], start=(fc == 0), stop=False),
                                     reads=[bACTT[fc], bWd], writes=[psb[b]])
                            for kc in range(8):
                                P.op("pe", mm(ps[b][:, :], MIXT[:, kc * W2 + 2 + tok0: kc * W2 + 2 + tok0 + 128],
                                              Wout[:, kc * 1024 + oh * 512: kc * 1024 + (oh + 1) * 512], start=False, stop=(kc == 7)),
                                     reads=[bMIXTk[kc], bWout], writes=[psb[b]])
                            P.op("dve", tt(x2[s2_][:, oh * 512:(oh + 1) * 512], ps[b][:, :], xk[s2_][:, oh * 512:(oh + 1) * 512], ALU.add),
                                 reads=[psb[b], bxk[s2_]], writes=[bx2[s2_]])
                        P.op("pool", tt(sqd[:], x2[s2_][:], x2[s2_][:], ALU.mult), reads=[bx2[s2_]], writes=[bsqd])
                        P.op("dve", lambda e, st=st, sqd=sqd: e.reduce_sum(out=st[:, 0:1], in_=sqd[:], axis=AX.X), reads=[bsqd], writes=[bst])
                        P.op("act", act(st[:, 1:2], st[:, 0:1], AF.Ln, bias=EPS, scale=1.0 / 1024), reads=[bst], writes=[bst])
                        P.op("act", act(st[:, 2:3], st[:, 1:2], AF.Exp, scale=-0.5), reads=[bst], writes=[bst])
                        P.op("dve", stt(ot[s2_][:], x2[s2_][:], st[:, 2:3], gfin[:], ALU.mult, ALU.mult),
                             reads=[bx2[s2_], bst, bsm], writes=[bot[s2_]])
                        P.dma("sp", self.out[tok0:tok0 + 128, :], ot[s2_][:], reads=[bot[s2_]])
                    P.barrier()

    def dump_1a(self):
        P, d = self.P, self.dbg
        allb = self.bQD + self.bKD + self.bVD
        P.dma("sp", d["QD"], self.QD, reads=allb)
        P.dma("sp", d["KD"], self.KD, reads=allb)
        P.dma("sp", d["VD"], self.VD, reads=allb)
        P.dma("sp", d["QgT"], self.QgT[:], reads=self.bQg)
        P.dma("sp", d["KgT"], self.KgT[:], reads=self.bKg)
        P.dma("sp", d["Vg"], self.Vg[:], reads=self.bVg)
        P.dma("sp", d["Kt"], self.Kt[:], reads=self.bKt)
        P.dma("sp", d["Zs"], self.Zs[:], reads=self.bZs)
        P.dma("sp", d["GAB"], self.GAB[:], reads=self.bGAB)


def bf(a):
    return np.ascontiguousarray(a).astype(ml_dtypes.bfloat16)


def host_consts(S, h):
    slope = 2.0 ** (-8.0 * (h + 1) / 4)
    pos = np.arange(S)
    a, b = pos // 128, pos % 128
    qaug = np.stack([-slope * 128.0 * a, -slope * b, np.ones(S), np.ones(S)]).astype(np.float32)
    kaug = np.stack([np.ones(S), np.ones(S), slope * 128.0 * a, slope * b]).astype(np.float32)
    ident = np.eye(128, dtype=np.float32)
    ones = np.ones((128, 128), np.float32)
    k = np.arange(128)[:, None]
    q = np.arange(128)[None, :]
    tri = (q >= k).astype(np.float32)
    cbf = np.concatenate([ident, ones, ones / 1024.0, tri], axis=1)
    same = (k // 64) == (q // 64)
    tri_incl = ((k <= q) & same).astype(np.float32)
    blk = same.astype(np.float32)
    negm = np.where((q >= k) & same, 0.0, NEG).astype(np.float32)
    strict = ((q > k) & same).astype(np.float32)
    ch0 = np.repeat((np.arange(128) < 64).astype(np.float32)[:, None], 128, axis=1)
    ch1 = 1.0 - ch0
    cf32 = np.concatenate([ident, ones, tri_incl, blk, negm, strict, ch0, ch1], axis=1)
    return bf(qaug), bf(kaug), bf(cbf), cf32.astype(np.float32)


def split_cols(h):
    r = lambda base, n: list(range(base + h * n, base + (h + 1) * n))
    cols = r(0, 128) + r(512, 128) + r(1536, 128) + r(2048, 128) + r(2560, 128) + r(3080, 128) + r(1024, 128)
    cols += [3072 + h, 3076 + h]
    return np.array(cols)


def make_in_maps(inputs, S):
    x = np.asarray(inputs["x"], np.float32)
    B = x.shape[0]
    TPC = S // 4
    w_in = np.asarray(inputs["w_in"], np.float32)[0]
    per = lambda v: np.ascontiguousarray(np.asarray(v, np.float32).reshape(8, 128).T)
    maps = []
    for r in range(NCORES):
        b, h = r // 4, r % 4
        j = h
        qaug, kaug, cbf, cf32 = host_consts(S, h)
        m = {}
        m["xT"] = np.ascontiguousarray(x[b].T)
        m["wh"] = np.ascontiguousarray(w_in[:, split_cols(h)])
        m["gA"] = per(inputs["attn_norm_g"][0])
        cwfull = np.asarray(inputs["gdn_conv_w"], np.float32)[0]
        cw = np.zeros((128, 12), np.float32)
        for g in range(3):
            cw[:, g * 4:(g + 1) * 4] = cwfull[:, g * 512 + h * 128: g * 512 + (h + 1) * 128].T
        m["convw"] = cw
        m["qaug"], m["kaug"], m["cbf"], m["cf32"] = qaug, kaug, cbf, cf32
        pv = np.zeros((128, 16), np.float32)
        pv[:, 0] = np.asarray(inputs["da_subln_g"], np.float32)[0]
        pv[:, 1] = np.asarray(inputs["gdn_norm_g"], np.float32)[0]
        pv[:, 2] = np.asarray(inputs["gdn_a_log"], np.float32)[0, h]
        pv[:, 3] = np.asarray(inputs["gdn_dt_bias"], np.float32)[0, h]
        m["pvec"] = pv
        lv = np.concatenate([np.asarray(inputs[k], np.float32)[0] for k in
                             ("da_lambda_q1", "da_lambda_k1", "da_lambda_q2", "da_lambda_k2")])
        m["lamv"] = np.ascontiguousarray(np.broadcast_to(lv[None, :], (128, 256)))
        xT2 = np.zeros((D_MODEL, TPC + 2), np.float32)
        lo = j * TPC
        xT2[:, 2:] = x[b, lo:lo + TPC].T
        if j > 0:
            xT2[:, 0:2] = x[b, lo - 2:lo].T
        m["xT2"] = xT2
        m["xtok2"] = np.ascontiguousarray(x[b, lo:lo + TPC])
        wo = np.asarray(inputs["w_out"], np.float32)[0]
        rows = []
        for hh in range(4):
            rows += list(range(hh * 128, (hh + 1) * 128)) + list(range(512 + hh * 128, 512 + (hh + 1) * 128))
        m["wout"] = np.ascontiguousarray(wo[np.array(rows)])
        m["wup"] = np.ascontiguousarray(np.asarray(inputs["w_up"], np.float32)[0])
        m["wdown"] = np.ascontiguousarray(np.asarray(inputs["w_down"], np.float32)[0])
        m["g2"] = per(inputs["ffn_norm_g"][0])
        fw = np.asarray(inputs["ffn_conv_w"], np.float32)[0]
        fb = np.asarray(inputs["ffn_conv_b"], np.float32)[0]
        fcw = np.zeros((128, 44 * 4), np.float32)
        for c in range(44):
            fcw[:, c * 4:c * 4 + 3] = fw[:, c * 128:(c + 1) * 128].T
            fcw[:, c * 4 + 3] = fb[c * 128:(c + 1) * 128]
        m["fcw"] = fcw
        m["gfin"] = np.ascontiguousarray(np.broadcast_to(np.asarray(inputs["final_norm_g"], np.float32)[None, :], (128, D_MODEL)))
        maps.append(m)
    return maps


_CACHE = {}


def kernel(**inputs):
    x = np.asarray(inputs["x"])
    B, S, _ = x.shape
    if S not in _CACHE:
        _CACHE[S] = Builder(S).build()
    nc = _CACHE[S]
    maps = make_in_maps(inputs, S)
    res = run_bass_kernel_spmd(nc, maps, core_ids=list(range(NCORES)))
    out = np.zeros((B, S, D_MODEL), np.float32)
    TPC = S // 4
    for r in range(NCORES):
        b, j = r // 4, r % 4
        out[b, j * TPC:(j + 1) * TPC] = np.asarray(res.results[r]["out"], np.float32)
    return out
```

```python
import math
from contextlib import ExitStack

import numpy as np
import ml_dtypes

import concourse.bass as bass
import concourse.mybir as mybir
from concourse.bass_utils import run_bass_kernel_spmd

F32 = mybir.dt.float32
BF16 = mybir.dt.bfloat16
AF = mybir.ActivationFunctionType
ALU = mybir.AluOpType
AX = mybir.AxisListType

D_MODEL = 1024
EPS = 1e-6
D_FF = 2816
NCORES = 8
NEG = -30000.0
SAME_ENGINE_SYNC = True


class Buf:
    __slots__ = ("w", "r", "name")

    def __init__(self, name=""):
        self.w = None
        self.r = []
        self.name = name


class Prog:
    CE = ("pe", "act", "dve", "pool")

    SEM_LIMIT = 2000

    def __init__(self, nc, es, n_dma_sems=40):
        self.nc = nc
        self.es = es
        self.nsem = 0
        self.q = {e: [] for e in ("pe", "act", "dve", "pool", "sp")}
        self.sem = {e: es.enter_context(nc.semaphore("s_" + e)) for e in self.CE}
        self.cnt = {e: 0 for e in self.CE}
        self.seen = {e: {} for e in self.q}
        self.dsem = [es.enter_context(nc.semaphore("d%d" % i)) for i in range(n_dma_sems)]
        self.dcnt = [0] * n_dma_sems
        self.dnext = 0
        self.dnext_pool = 0
        self.dma_toks = []
        self.nops = 0

    def _wait(self, eng, tok):
        sem, val, src = tok
        k = id(sem)
        if self.seen[eng].get(k, 0) >= val:
            return
        self.seen[eng][k] = val
        self.q[eng].append(lambda e, sem=sem, val=val: e.wait_ge(sem, val))

    def _deps(self, eng, reads, writes):
        for b in reads:
            if b.w is not None:
                if b.w[2] == eng and (eng == "pe" or not SAME_ENGINE_SYNC):
                    continue
                self._wait(eng, b.w)
        for b in writes:
            if b.w is not None and (b.w[2] != eng or (SAME_ENGINE_SYNC and eng != "pe")):
                self._wait(eng, b.w)
            for t in b.r:
                if t[2] != eng or (SAME_ENGINE_SYNC and eng != "pe"):
                    self._wait(eng, t)

    def op(self, eng, fn, reads=(), writes=()):
        self._deps(eng, reads, writes)
        if self.cnt[eng] >= self.SEM_LIMIT:
            self.nsem += 1
            self.sem[eng] = self.es.enter_context(self.nc.semaphore("s_%s_%d" % (eng, self.nsem)))
            self.cnt[eng] = 0
        self.cnt[eng] += 1
        sem = self.sem[eng]
        tok = (sem, self.cnt[eng], eng)
        self.q[eng].append(lambda e, fn=fn, sem=sem: fn(e).then_inc(sem, 1))
        for b in writes:
            b.w = tok
            b.r = []
        for b in reads:
            b.r.append(tok)
        self.nops += 1
        return tok

    def dma(self, eng, out, in_, reads=(), writes=()):
        self._deps(eng, reads, writes)
        npool = 8
        if eng == "pool":
            i = self.dnext_pool
            self.dnext_pool = (self.dnext_pool + 1) % npool
        else:
            i = npool + self.dnext
            self.dnext = (self.dnext + 1) % (len(self.dsem) - npool)
        sem = self.dsem[i]
        if self.dcnt[i] > 0:
            self._wait(eng, (sem, self.dcnt[i], "dma"))
        self.dcnt[i] += 16
        tok = (sem, self.dcnt[i], "dma")
        self.q[eng].append(lambda e, out=out, in_=in_, sem=sem: e.dma_start(
            out=out, in_=(in_(e) if callable(in_) else in_)).then_inc(sem, 16))
        for b in writes:
            b.w = tok
            b.r = []
        for b in reads:
            b.r.append(tok)
        self.dma_toks.append(tok)
        self.nops += 1
        return tok

    def collective(self, fn, sem, reads=(), writes=()):
        self._deps("pool", reads, writes)
        self.ccnt = getattr(self, "ccnt", 0) + 1
        tok = (sem, self.ccnt, "cc")
        self.q["pool"].append(lambda e, fn=fn, sem=sem: fn(e).then_inc(sem, 1))
        for b in writes:
            b.w = tok
            b.r = []
        for b in reads:
            b.r.append(tok)
        return tok

    def raw(self, eng, fn):
        self.q[eng].append(fn)

    def barrier(self):
        toks = [(self.sem[e], self.cnt[e], e) for e in self.CE if self.cnt[e] > 0]
        toks += [(self.dsem[i], self.dcnt[i], "dma") for i in range(len(self.dsem)) if self.dcnt[i] > 0]
        for e in self.q:
            for t in toks:
                if t[2] == e:
                    continue
                self._wait(e, t)

    def finish(self):
        for i in range(len(self.dsem)):
            if self.dcnt[i] > 0:
                self._wait("sp", (self.dsem[i], self.dcnt[i], "dma"))

    def replay(self, block):
        q = self.q

        @block.tensor
        def _(e):
            for f in q["pe"]:
                f(e)

        @block.scalar
        def _(e):
            for f in q["act"]:
                f(e)

        @block.vector
        def _(e):
            for f in q["dve"]:
                f(e)

        @block.gpsimd
        def _(e):
            for f in q["pool"]:
                f(e)

        @block.sync
        def _(e):
            for f in q["sp"]:
                f(e)


def act(out, in_, func, bias=0.0, scale=1.0):
    return lambda e: e.activation(out=out, in_=in_, func=func, bias=bias, scale=scale)


def mm(out, lhsT, rhs, start=True, stop=True):
    return lambda e: e.matmul(out, lhsT, rhs, start=start, stop=stop)


def tt(out, in0, in1, op):
    return lambda e: e.tensor_tensor(out=out, in0=in0, in1=in1, op=op)


def ts(out, in0, s1, s2, op0, op1=None):
    if op1 is None:
        return lambda e: e.tensor_scalar(out=out, in0=in0, scalar1=s1, scalar2=None, op0=op0)
    return lambda e: e.tensor_scalar(out=out, in0=in0, scalar1=s1, scalar2=s2, op0=op0, op1=op1)


def stt(out, in0, scalar, in1, op0, op1):
    return lambda e: e.scalar_tensor_tensor(out=out, in0=in0, scalar=scalar, in1=in1, op0=op0, op1=op1)


def cp(out, in_):
    return lambda e: e.tensor_copy(out=out, in_=in_)


class Builder:
    def __init__(self, S, debug=False, stop=None):
        self.S = S
        self.debug = debug
        self.stop = stop
        self.NT = S // 512
        self.NK = S // 128
        self.TPC = S // 4
        self.nc = bass.Bass("TRN2", target_bir_lowering=False)

    def declare_io(self):
        nc, S = self.nc, self.S
        di = lambda n, s, d=F32: nc.dram_tensor(n, s, d, kind="ExternalInput").ap()
        self.xT = di("xT", [D_MODEL, S])
        self.wh = di("wh", [D_MODEL, 898])
        self.gA = di("gA", [128, 8])
        self.convw = di("convw", [128, 12])
        self.qaug = di("qaug", [4, S], BF16)
        self.kaug = di("kaug", [4, S], BF16)
        self.cbf = di("cbf", [128, 4 * 128], BF16)
        self.cf32 = di("cf32", [128, 8 * 128])
        self.pvec = di("pvec", [128, 16])
        self.lamv = di("lamv", [128, 4 * 64])
        self.xT2 = di("xT2", [D_MODEL, self.TPC + 2])
        self.xtok2 = di("xtok2", [self.TPC, D_MODEL])
        self.wout = di("wout", [D_MODEL, D_MODEL])
        self.wup = di("wup", [D_MODEL, 2 * D_FF])
        self.wdown = di("wdown", [D_FF, D_MODEL])
        self.g2 = di("g2", [128, 8])
        self.fcw = di("fcw", [128, 44 * 4])
        self.gfin = di("gfin", [128, D_MODEL])
        self.out = nc.dram_tensor("out", [self.TPC, D_MODEL], F32, kind="ExternalOutput").ap()
        self.QD = nc.dram_tensor("QD", [128, S], BF16).ap()
        self.KD = nc.dram_tensor("KD", [128, S], BF16).ap()
        self.VD = nc.dram_tensor("VD", [S, 128], BF16).ap()
        self.MIXD = nc.dram_tensor("MIXD", [256, S], BF16).ap()
        W2 = self.TPC + 2
        self.ag_in = nc.dram_tensor("ag_in", [4 * 256, W2], BF16)
        self.ag_out = nc.dram_tensor("ag_out", [32 * 256, W2], BF16)
        self.wup_bf = nc.dram_tensor("wup_bf", [D_MODEL, 2 * D_FF], BF16).ap()
        self.wdown_bf = nc.dram_tensor("wdown_bf", [D_FF, D_MODEL], BF16).ap()
        self.wout_bf = nc.dram_tensor("wout_bf", [D_MODEL, D_MODEL], BF16).ap()
        if self.debug:
            do = lambda n, s, d=F32: nc.dram_tensor(n, s, d, kind="ExternalOutput").ap()
            self.dbg = {
                "QD": do("dQD", [128, S], BF16), "KD": do("dKD", [128, S], BF16), "VD": do("dVD", [S, 128], BF16),
                "QgT": do("dQgT", [128, S], BF16), "KgT": do("dKgT", [128, S], BF16),
                "Vg": do("dVg", [128, self.NK * 128], BF16), "Kt": do("dKt", [128, self.NK * 128], BF16),
                "Zs": do("dZs", [128, S], BF16), "GAB": do("dGAB", [128, self.NK * 2]),
                "MIX": do("dMIX", [256, S], BF16),
            }

    def build(self):
        nc = self.nc
        self.declare_io()
        with ExitStack() as es:
            P = self.P = Prog(nc, es)
            self.es = es
            self.ps = [es.enter_context(nc.psum_tensor("ps%d" % i, [128, 512], F32)) for i in range(8)]
            self.psb = [Buf("ps%d" % i) for i in range(8)]
            sb = lambda n, s, d: es.enter_context(nc.sbuf_tensor(n, s, d))
            self.c_bf = sb("c_bf", [128, 512], BF16)
            self.c_f32 = sb("c_f32", [128, 1024], F32)
            self.c_pv = sb("c_pv", [128, 16], F32)
            self.b_const = Buf("const")
            P.dma("sp", self.c_bf[:], self.cbf[:, :], writes=[self.b_const])
            P.dma("sp", self.c_f32[:], self.cf32[:, :], writes=[self.b_const])
            P.dma("sp", self.c_pv[:], self.pvec[:, :], writes=[self.b_const])
            self.ident = self.c_bf[:, 0:128]
            self.ones = self.c_bf[:, 128:256]
            self.onesmean = self.c_bf[:, 256:384]
            self.tri = self.c_bf[:, 384:512]

            self.bMIXa = [Buf() for _ in range(self.NT)]
            self.bMIXg = [Buf() for _ in range(self.NT)]
            self.cc_sem = es.enter_context(nc.semaphore("cc_sem"))
            self.bwcast = Buf("wcast")
            self._mix_init()
            with ExitStack() as es1:
                self.es1 = es1
                self.phase1a()
                if self.debug == "1a":
                    self.dump_1a()
                else:
                    self.phase_gdn()
            if self.debug != "1a":
                full = self.stop is None or self.stop.startswith("p2")
                if self.stop != "gdn":
                    self.phase_attn()
                if self.debug:
                    P.dma("sp", self.dbg["MIX"], self.MIXD, reads=self.bMIXa + self.bMIXg)
                if full and self.stop != "p2x0":
                    self.phase2()
            P.finish()
            block = es.enter_context(nc.Block())
            P.replay(block)
        return nc

    def rsqrt_act(self, out, in_, tmp, reads, writes, tmpbuf, eps=EPS):
        P = self.P
        P.op("act", act(tmp, in_, AF.Ln, bias=eps), reads=reads, writes=[tmpbuf])
        P.op("act", act(out, tmp, AF.Exp, scale=-0.5), reads=[tmpbuf], writes=writes)

    def phase1a(self):
        nc, P, S, NT = self.nc, self.P, self.S, self.NT
        es = self.es1
        sb = lambda n, s, d: es.enter_context(nc.sbuf_tensor(n, s, d))
        ps, psb = self.ps, self.psb
        bc = self.b_const
        self.QgT = sb("QgT", [128, S], BF16)
        self.KgT = sb("KgT", [128, S], BF16)
        self.Vg = sb("Vg", [128, self.NK * 128], BF16)
        self.Kt = sb("Kt", [128, self.NK * 128], BF16)
        self.Zs = sb("Zs", [128, S], BF16)
        self.GAB = sb("GAB", [128, self.NK * 2], F32)
        self.bQg = [Buf() for _ in range(NT)]
        self.bKg = [Buf() for _ in range(NT)]
        self.bVg = [Buf() for _ in range(NT)]
        self.bKt = [Buf() for _ in range(NT)]
        self.bZs = [Buf() for _ in range(NT)]
        self.bGAB = [Buf() for _ in range(NT)]
        self.bQD = [Buf() for _ in range(NT)]
        self.bKD = [Buf() for _ in range(NT)]
        self.bVD = [Buf() for _ in range(NT)]

        with ExitStack() as ws:
            wsb = lambda n, s, d: ws.enter_context(nc.sbuf_tensor(n, s, d))
            W = wsb("W_sb", [128, 8 * 898], BF16)
            bW = Buf("W")
            Wv = W[:].rearrange("p (c n) -> p c n", c=8)
            P.dma("pool", Wv, self.wh.rearrange("(c p) n -> p c n", p=128), writes=[bW])
            self.emit_wcasts = emit_wcasts = lambda: self._emit_wcasts()
            def _unused():
                P.dma("pool", self.wout_bf, self.wout[:, :], writes=[self.bwcast])
                P.dma("pool", self.wdown_bf, self.wdown[:, :], writes=[self.bwcast])
                for i in range(4):
                    P.dma("pool", self.wup_bf[i * 256:(i + 1) * 256, :], self.wup[i * 256:(i + 1) * 256, :], writes=[self.bwcast])
            gA = wsb("gA_sb", [128, 8], F32)
            cw = wsb("cw_sb", [128, 12], F32)
            bsm = Buf("small")
            P.dma("sp", gA[:], self.gA[:, :], writes=[bsm])
            P.dma("sp", cw[:], self.convw[:, :], writes=[bsm])

            xt = [wsb("xt%d" % i, [128, 8 * 512], F32) for i in range(2)]
            sq = [wsb("sq%d" % i, [128, 8 * 512], BF16) for i in range(2)]
            xb = [wsb("xb%d" % i, [128, 8 * 512], BF16) for i in range(2)]
            rstd = [wsb("rstd%d" % i, [128, 512], F32) for i in range(2)]
            lntmp = wsb("lntmp", [128, 512], F32)
            bxt = [Buf() for _ in range(2)]
            bsq = [Buf() for _ in range(2)]
            bxb = [Buf() for _ in range(2)]
            brstd = [Buf() for _ in range(2)]
            bln = Buf()
            qst = [wsb("qst%d" % i, [128, 512], BF16) for i in range(2)]
            kst = [wsb("kst%d" % i, [128, 512], BF16) for i in range(2)]
            vst = [wsb("vst%d" % i, [128, 512], BF16) for i in range(2)]
            bqst = [Buf() for _ in range(2)]
            bkst = [Buf() for _ in range(2)]
            bvst = [Buf() for _ in range(2)]
            cs = [[wsb("cs%d_%d" % (g, i), [128, 515], F32) for i in range(2)] for g in range(3)]
            bcs = [[Buf() for _ in range(2)] for g in range(3)]
            yc = [wsb("yc%d" % g, [128, 512], F32) for g in range(3)]
            byc = [Buf() for _ in range(3)]
            sl = [wsb("sl%d" % g, [128, 512], F32) for g in range(2)]
            bsl = [Buf() for _ in range(2)]
            s2 = [wsb("s2_%d" % g, [128, 512], BF16) for g in range(2)]
            bs2 = [Buf() for _ in range(2)]
            rn = [wsb("rn%d" % g, [128, 512], F32) for g in range(2)]
            brn = [Buf() for _ in range(2)]
            vs = wsb("vs", [128, 512], BF16)
            bvs = Buf()
            for g in range(3):
                P.op("dve", lambda e, g=g: e.memset(cs[g][1][:, 512:515], 0.0), writes=[bcs[g][1]])

            fm_rot = [4, 5, 6]
            rot = [0]

            def stageA0(t):
                sl_ = t % 2
                xv = xt[sl_][:].rearrange("p (c t) -> p c t", c=8)
                P.dma("sp", xv, self.xT[:, t * 512:(t + 1) * 512].rearrange("(c p) t -> p c t", p=128),
                      writes=[bxt[sl_]])
                P.op("pool", tt(sq[sl_][:], xt[sl_][:], xt[sl_][:], ALU.mult), reads=[bxt[sl_]], writes=[bsq[sl_]])

            def stageA1a(t):
                sl_ = t % 2
                for c in range(8):
                    P.op("pe", mm(ps[0][:, :], self.onesmean, sq[sl_][:, c * 512:(c + 1) * 512], start=(c == 0), stop=(c == 7)),
                         reads=[bsq[sl_], bc], writes=[psb[0]])
                self.rsqrt_act(rstd[sl_][:], ps[0][:, :], lntmp[:], [psb[0]], [brstd[sl_]], bln)

            def stageA1b(t):
                sl_ = t % 2
                for c in range(8):
                    P.op("dve", stt(xb[sl_][:, c * 512:(c + 1) * 512], xt[sl_][:, c * 512:(c + 1) * 512], gA[:, c:c + 1],
                                    rstd[sl_][:], ALU.mult, ALU.mult),
                         reads=[bxt[sl_], brstd[sl_], bsm], writes=[bxb[sl_]])

            def fm_group(t, gi):
                sl_ = t % 2
                b = fm_rot[rot[0] % 3]
                rot[0] += 1
                for c in range(8):
                    P.op("pe", mm(ps[b][:, :], W[:, c * 898 + gi * 128: c * 898 + (gi + 1) * 128], xb[sl_][:, c * 512:(c + 1) * 512],
                                  start=(c == 0), stop=(c == 7)), reads=[bxb[sl_], bW], writes=[psb[b]])
                return b

            def stageB1(t):
                sl_ = t % 2
                cols = slice(t * 512, (t + 1) * 512)
                b = fm_group(t, 0)
                P.op("act", act(qst[sl_][:], ps[b][:, :], AF.Copy, scale=0.125), reads=[psb[b]], writes=[bqst[sl_]])
                P.dma("sp", self.QD[:, cols], qst[sl_][:], reads=[bqst[sl_]], writes=[self.bQD[t]])
                b = fm_group(t, 1)
                P.op("act", act(kst[sl_][:], ps[b][:, :], AF.Copy), reads=[psb[b]], writes=[bkst[sl_]])
                P.dma("sp", self.KD[:, cols], kst[sl_][:], reads=[bkst[sl_]], writes=[self.bKD[t]])
                for g in range(3):
                    b = fm_group(t, 2 + g)
                    P.op("act", act(cs[g][sl_][:, 3:515], ps[b][:, :], AF.Copy), reads=[psb[b]], writes=[bcs[g][sl_]])
                    P.op("dve", cp(cs[g][sl_][:, 0:3], cs[g][1 - sl_][:, 512:515]), reads=[bcs[g][1 - sl_]], writes=[bcs[g][sl_]])
                for j in range(4):
                    for c in range(8):
                        P.op("pe", mm(ps[3][:, j * 128:(j + 1) * 128], xb[sl_][:, c * 512 + j * 128: c * 512 + (j + 1) * 128],
                                      W[:, c * 898 + 768: c * 898 + 896], start=(c == 0), stop=(c == 7)),
                             reads=[bxb[sl_], bW], writes=[psb[3]])
                    for c in range(8):
                        P.op("pe", mm(ps[7][:, j * 2:(j + 1) * 2], xb[sl_][:, c * 512 + j * 128: c * 512 + (j + 1) * 128],
                                      W[:, c * 898 + 896: c * 898 + 898], start=(c == 0), stop=(c == 7)),
                             reads=[bxb[sl_], bW], writes=[psb[7]])
                P.op("act", act(vst[sl_][:], ps[3][:, :], AF.Copy), reads=[psb[3]], writes=[bvst[sl_]])
                P.dma("sp", self.VD.rearrange("(n p) d -> p n d", p=128)[:, t * 4:(t + 1) * 4, :],
                      vst[sl_][:].rearrange("p (n d) -> p n d", n=4), reads=[bvst[sl_]], writes=[self.bVD[t]])
                P.op("dve", cp(self.GAB[:, t * 8:(t + 1) * 8], ps[7][:, 0:8]), reads=[psb[7]], writes=[self.bGAB[t]])
                b = fm_group(t, 5)
                P.op("act", act(self.Zs[:, cols], ps[b][:, :], AF.Silu), reads=[psb[b]], writes=[self.bZs[t]])

            def stageB2a(t):
                sl_ = t % 2
                for g in range(3):
                    P.op("act", act(yc[g][:], cs[g][sl_][:, 0:512], AF.Copy, scale=cw[:, g * 4:g * 4 + 1]),
                         reads=[bcs[g][sl_], bsm], writes=[byc[g]])
                for j in range(1, 4):
                    for g in range(3):
                        P.op("dve", stt(yc[g][:], cs[g][sl_][:, j:j + 512], cw[:, g * 4 + j:g * 4 + j + 1], yc[g][:], ALU.mult, ALU.add),
                             reads=[bcs[g][sl_], byc[g], bsm], writes=[byc[g]])
                for g in range(3):
                    if g < 2:
                        P.op("act", act(sl[g][:], yc[g][:], AF.Silu), reads=[byc[g]], writes=[bsl[g]])
                        P.op("pool", tt(s2[g][:], sl[g][:], sl[g][:], ALU.mult), reads=[bsl[g]], writes=[bs2[g]])
                    else:
                        P.op("act", act(vs[:], yc[g][:], AF.Silu), reads=[byc[g]], writes=[bvs])

            def stageB2b1(t):
                cols = slice(t * 512, (t + 1) * 512)
                for g in range(2):
                    P.op("pe", mm(ps[1][:, :], self.ones, s2[g][:]), reads=[bs2[g], bc], writes=[psb[1]])
                    self.rsqrt_act(rn[g][:], ps[1][:, :], lntmp[:], [psb[1]], [brn[g]], bln)
                    dst, bd = (self.QgT, self.bQg) if g == 0 else (self.KgT, self.bKg)
                    scl = 128.0 ** -0.5 if g == 0 else 1.0
                    P.op("dve", stt(dst[:, cols], sl[g][:], scl, rn[g][:], ALU.mult, ALU.mult),
                         reads=[bsl[g], brn[g]], writes=[bd[t]])

            def stageB2b2(t):
                cols = slice(t * 512, (t + 1) * 512)
                for (src, bsrc, dst, bdst) in ((vs[:], bvs, self.Vg, self.bVg[t]), (self.KgT[:, cols], self.bKg[t], self.Kt, self.bKt[t])):
                    for j in range(4):
                        P.op("pe", mm(ps[2][:, j * 128:(j + 1) * 128], src[:, j * 128:(j + 1) * 128], self.ident),
                             reads=[bsrc, bc], writes=[psb[2]])
                    P.op("act", act(dst[:, t * 512:(t + 1) * 512], ps[2][:, :], AF.Copy), reads=[psb[2]], writes=[bdst])

            stageA0(0)
            stageA1a(0)
            stageA1b(0)
            if NT > 1:
                stageA0(1)
            for t in range(NT):
                if t + 1 < NT:
                    stageA1a(t + 1)
                if t >= 1:
                    stageB2b1(t - 1)
                if t + 1 < NT:
                    stageA1b(t + 1)
                if t + 2 < NT:
                    stageA0(t + 2)
                stageB1(t)
                if t >= 1:
                    stageB2b2(t - 1)
                stageB2a(t)
            stageB2b1(NT - 1)
            stageB2b2(NT - 1)
            P.barrier()


    def _emit_wcasts(self):
        P = self.P
        P.dma("pool", self.wout_bf, self.wout[:, :], writes=[self.bwcast])
        P.dma("pool", self.wdown_bf, self.wdown[:, :], writes=[self.bwcast])
        for i in range(4):
            P.dma("pool", self.wup_bf[i * 256:(i + 1) * 256, :], self.wup[i * 256:(i + 1) * 256, :], writes=[self.bwcast])

    def phase_gdn(self):
        nc, P, S, NT, NK = self.nc, self.P, self.S, self.NT, self.NK
        ps, psb, bc = self.ps, self.psb, self.b_const
        cf = self.c_f32
        identf, onesf = cf[:, 0:128], cf[:, 128:256]
        tri_incl, blk, negm, strict = cf[:, 256:384], cf[:, 384:512], cf[:, 512:640], cf[:, 640:768]
        chm = [cf[:, 768:896], cf[:, 896:1024]]
        pv = self.c_pv
        with ExitStack() as ws:
            wsb = lambda n, s, d: ws.enter_context(nc.sbuf_tensor(n, s, d))
            gt_ = wsb("gtmp", [128, 8 * NK], F32)
            G, Bt, GC, GT, EG, EKD, NGC = [gt_[:, i * NK:(i + 1) * NK] for i in range(7)]
            GLb = [wsb("GLb%d" % c, [128, NK], F32) for c in range(2)]
            sc = wsb("gsc", [128, 4], F32)
            bg = Buf("gates")
            gab = self.GAB[:].rearrange("p (n t) -> p n t", t=2)
            rd = self.bGAB + [bc]
            P.op("act", act(G, gab[:, :, 0], AF.Exp, bias=pv[:, 3:4]), reads=rd, writes=[bg])
            P.op("act", act(G, G, AF.Ln, bias=1.0), reads=[bg], writes=[bg])
            P.op("act", act(sc[:, 0:1], pv[:, 2:3], AF.Exp), reads=[bc], writes=[bg])
            P.op("dve", ts(G, G, sc[:, 0:1], -1.0, ALU.mult, ALU.mult), reads=[bg], writes=[bg])
            P.op("act", act(Bt, gab[:, :, 1], AF.Exp, scale=-1.0), reads=rd + [bg], writes=[bg])
            P.op("dve", ts(Bt, Bt, 1.0, None, ALU.add), reads=[bg], writes=[bg])
            P.op("dve", lambda e: e.reciprocal(out=Bt, in_=Bt), reads=[bg], writes=[bg])
            P.op("pe", mm(ps[0][:, 0:NK], tri_incl, G), reads=[bg, bc], writes=[psb[0]])
            P.op("pe", mm(ps[0][:, NK:2 * NK], blk, G), reads=[bg, bc], writes=[psb[0]])
            P.op("pe", mm(ps[0][:, 2 * NK:3 * NK], chm[0], G), reads=[bg, bc], writes=[psb[0]])
            P.op("pe", mm(ps[0][:, 3 * NK:4 * NK], chm[1], G), reads=[bg, bc], writes=[psb[0]])
            P.op("dve", cp(GC, ps[0][:, 0:NK]), reads=[psb[0]], writes=[bg])
            P.op("dve", tt(GT, ps[0][:, NK:2 * NK], GC, ALU.subtract), reads=[psb[0], bg], writes=[bg])
            P.op("dve", ts(NGC, GC, -1.0, None, ALU.mult), reads=[bg], writes=[bg])
            P.op("act", act(EG, GC, AF.Exp), reads=[bg], writes=[bg])
            P.op("act", act(EKD, GT, AF.Exp), reads=[bg], writes=[bg])
            bgl = Buf()
            P.op("act", act(GLb[0][:], ps[0][:, 2 * NK:3 * NK], AF.Exp), reads=[psb[0]], writes=[bgl])
            P.op("act", act(GLb[1][:], ps[0][:, 3 * NK:4 * NK], AF.Exp), reads=[psb[0]], writes=[bgl])

            U_s = [wsb("U_s%d" % i, [128, 512], F32) for i in range(2)]
            WT_s = [wsb("WT_s%d" % i, [128, 512], BF16) for i in range(2)]
            KDEC_s = [wsb("KDEC_s%d" % i, [128, 512], BF16) for i in range(2)]
            QDECT_s = [wsb("QDECT_s%d" % i, [128, 512], BF16) for i in range(2)]
            QKT_s = [wsb("QKT_s%d" % i, [128, 512], BF16) for i in range(2)]
            bU = [Buf() for _ in range(2)]
            bWT = [Buf() for _ in range(2)]
            bKD = [Buf() for _ in range(2)]
            bQDT = [Buf() for _ in range(2)]
            bQKT = [Buf() for _ in range(2)]
            I4 = wsb("I4", [128, 512], BF16)
            ST4 = wsb("ST4", [128, 512], F32)
            bI4 = Buf()
            for j in range(4):
                P.op("pool", cp(I4[:, j * 128:(j + 1) * 128], self.ident), reads=[bc], writes=[bI4])
                P.op("pool", cp(ST4[:, j * 128:(j + 1) * 128], strict), reads=[bc], writes=[bI4])
            TriG = [wsb("TriG%d" % i, [128, 128], F32) for i in range(2)]
            bTriG = [Buf() for _ in range(2)]
            EGr = wsb("EGr", [128, 512], F32)
            Ei = wsb("Ei", [128, 512], F32)
            Es = wsb("Es", [128, 512], F32)
            KG = wsb("KG", [128, 512], BF16)
            bEGr, bEi, bEs, bKG = Buf(), Buf(), Buf(), Buf()
            Ub = [wsb("Ub%d" % i, [128, 512], BF16) for i in range(2)]
            Lb = [wsb("Lb%d" % i, [128, 512], BF16) for i in range(2)]
            Pb = [wsb("Pb%d" % i, [128, 512], BF16) for i in range(2)]
            bUb = [Buf() for _ in range(2)]
            bLb = [Buf() for _ in range(2)]
            bPb = [Buf() for _ in range(2)]
            wtok = wsb("wtok", [128, 512], BF16)
            bwtok = Buf()
            hb = lambda: [Buf(), Buf()]
            XB, YB = (0, 4), (1, 5)
            bUh, bWTh, bKDh, bQDTh, bQKTh = [hb(), hb()], [hb(), hb()], [hb(), hb()], [hb(), hb()], [hb(), hb()]
            TriG4 = [wsb("TriG4_%d" % i, [128, 128], F32) for i in range(4)]
            bTriG4 = [Buf() for _ in range(4)]
            EGrh = [wsb("EGrh%d" % i, [128, 256], F32) for i in range(2)]
            Eih = [wsb("Eih%d" % i, [128, 256], F32) for i in range(2)]
            Esh = [wsb("Esh%d" % i, [128, 256], F32) for i in range(2)]
            KGh = [wsb("KGh%d" % i, [128, 256], BF16) for i in range(2)]
            wtokh = [wsb("wtokh%d" % i, [128, 256], BF16) for i in range(2)]
            ULh = [[wsb("UL%d_%d" % (i, j), [128, 512], BF16) for j in range(2)] for i in range(2)]
            Pbh = [[wsb("Pbh%d_%d" % (i, j), [128, 256], BF16) for j in range(2)] for i in range(2)]
            bEGrh, bEih, bEsh, bKGh, bwtokh = hb(), hb(), hb(), hb(), hb()
            bULh = [hb(), hb()]
            bPbh = [hb(), hb()]

            def step2h(tg, hz):
                so = tg % 2
                X, Y = ps[XB[hz]], ps[YB[hz]]
                bX, bY = psb[XB[hz]], psb[YB[hz]]
                g0 = hz * 256
                GC_ = slice(g0, g0 + 256)
                cH = slice(tg * 512 + g0, tg * 512 + g0 + 256)
                ci = [slice(0, 128), slice(128, 256)]
                gci = [slice(g0, g0 + 128), slice(g0 + 128, g0 + 256)]
                tts = [tg * 4 + 2 * hz, tg * 4 + 2 * hz + 1]
                cts = [slice(t * 128, (t + 1) * 128) for t in tts]
                UL, Pb_ = ULh[hz], Pbh[hz]
                bUL, bPb_ = bULh[hz], bPbh[hz]
                U_ = lambda j: UL[j][:, 0:256]
                L_ = lambda j: UL[j][:, 256:512]
                for i in range(2):
                    t = tts[i]
                    tl = 2 * hz + i
                    P.op("dve", ts(TriG4[tl][:], tri_incl, G[:, t:t + 1], None, ALU.mult), reads=[bg, bc], writes=[bTriG4[tl]])
                    P.op("pe", mm(X[:, ci[i]], onesf, TriG4[tl][:]), reads=[bTriG4[tl], bc], writes=[bX])
                    P.op("pe", mm(Y[:, ci[i]], onesf, TriG4[tl][:], start=True, stop=False), reads=[bTriG4[tl], bc], writes=[bY])
                    P.op("pe", mm(Y[:, ci[i]], identf, negm, start=False, stop=True), reads=[bc], writes=[bY])
                yield
                P.op("act", act(EGrh[hz][:], X[:, 0:256], AF.Exp), reads=[bX], writes=[bEGrh[hz]])
                for i in range(2):
                    P.op("act", act(Eih[hz][:, ci[i]], Y[:, ci[i]], AF.Exp, bias=NGC[:, tts[i]:tts[i] + 1]), reads=[bY, bg], writes=[bEih[hz]])
                P.op("dve", tt(QDECT_s[so][:, GC_], self.QgT[:, cH], EGrh[hz][:], ALU.mult), reads=[self.bQg[tg], bEGrh[hz]], writes=[bQDTh[so][hz]])
                P.op("dve", tt(Esh[hz][:], Eih[hz][:], ST4[:, 0:256], ALU.mult), reads=[bEih[hz], bI4], writes=[bEsh[hz]])
                yield
                for i in range(2):
                    P.op("pe", mm(X[:, ci[i]], self.KgT[:, cts[i]], self.QgT[:, cts[i]]), reads=[self.bKg[tg], self.bQg[tg]], writes=[bX])
                    P.op("pe", mm(Y[:, ci[i]], self.KgT[:, cts[i]], self.KgT[:, cts[i]]), reads=[self.bKg[tg]], writes=[bY])
                P.op("dve", tt(QKT_s[so][:, GC_], X[:, 0:256], Eih[hz][:], ALU.mult), reads=[bX, bEih[hz]], writes=[bQKTh[so][hz]])
                for i in range(2):
                    t = tts[i]
                    P.op("dve", stt(UL[0][:, ci[i]], Y[:, ci[i]], Bt[:, t:t + 1], Esh[hz][:, ci[i]], ALU.mult, ALU.mult),
                         reads=[bY, bEsh[hz], bg], writes=[bUL[0]])
                    P.op("act", act(KGh[hz][:, ci[i]], self.Kt[:, cts[i]], AF.Copy, scale=EG[:, t:t + 1]), reads=[self.bKt[tg], bg], writes=[bKGh[hz]])
                    P.op("act", act(KDEC_s[so][:, gci[i]], self.Kt[:, cts[i]], AF.Copy, scale=EKD[:, t:t + 1]),
                         reads=[self.bKt[tg], bg], writes=[bKDh[so][hz]])
                yield
                P.op("dve", stt(Pb_[0][:], U_(0), -1.0, I4[:, 0:256], ALU.mult, ALU.add), reads=[bUL[0], bI4], writes=[bPb_[0]])
                for i in range(2):
                    P.op("pe", mm(X[:, ci[i]], UL[0][:, ci[i]], self.ident), reads=[bUL[0], bc], writes=[bX])
                P.op("act", act(L_(0), X[:, 0:256], AF.Copy), reads=[bX], writes=[bUL[0]])
                yield
                cu, pc = 0, 0
                for k in range(5):
                    nx = 1 - cu
                    for i in range(2):
                        ui, li = ci[i], slice(256 + i * 128, 256 + (i + 1) * 128)
                        if k < 4:
                            P.op("pe", mm(X[:, ui], UL[cu][:, li], UL[cu][:, ui]), reads=[bUL[cu]], writes=[bX])
                        P.op("pe", mm(X[:, li], UL[cu][:, ui], UL[cu][:, li]), reads=[bUL[cu]], writes=[bX])
                        if k >= 1:
                            P.op("pe", mm(Y[:, ui], UL[cu][:, li], Pb_[pc][:, ui]), reads=[bUL[cu], bPb_[pc]], writes=[bY])
                    if k < 4:
                        P.op("act", act(UL[nx][:, :], X[:, :], AF.Copy), reads=[bX], writes=[bUL[nx]])
                    else:
                        P.op("act", act(L_(nx), X[:, 256:512], AF.Copy), reads=[bX], writes=[bUL[nx]])
                    if k >= 1:
                        P.op("dve", tt(Pb_[1 - pc][:], Y[:, 0:256], Pb_[pc][:], ALU.add), reads=[bY, bPb_[pc]], writes=[bPb_[1 - pc]])
                        pc = 1 - pc
                    yield
                    cu = nx
                for i in range(2):
                    li = slice(256 + i * 128, 256 + (i + 1) * 128)
                    P.op("pe", mm(Y[:, ci[i]], UL[cu][:, li], Pb_[pc][:, ci[i]]), reads=[bUL[cu], bPb_[pc]], writes=[bY])
                P.op("dve", tt(Pb_[1 - pc][:], Y[:, 0:256], Pb_[pc][:], ALU.add), reads=[bY, bPb_[pc]], writes=[bPb_[1 - pc]])
                pc = 1 - pc
                yield
                Qm, bQm = Pb_[pc], bPb_[pc]
                for i in range(2):
                    P.op("pe", mm(X[:, ci[i]], Qm[:, ci[i]], self.Vg[:, cts[i]]), reads=[bQm, self.bVg[tg]], writes=[bX])
                    P.op("pe", mm(Y[:, ci[i]], Qm[:, ci[i]], KGh[hz][:, ci[i]]), reads=[bQm, bKGh[hz]], writes=[bY])
                for i in range(2):
                    t = tts[i]
                    P.op("dve", ts(U_s[so][:, gci[i]], X[:, ci[i]], Bt[:, t:t + 1], None, ALU.mult), reads=[bX, bg], writes=[bUh[so][hz]])
                    P.op("act", act(wtokh[hz][:, ci[i]], Y[:, ci[i]], AF.Copy, scale=Bt[:, t:t + 1]), reads=[bY, bg], writes=[bwtokh[hz]])
                yield
                for i in range(2):
                    P.op("pe", mm(X[:, ci[i]], wtokh[hz][:, ci[i]], self.ident), reads=[bwtokh[hz], bc], writes=[bX])
                P.op("act", act(WT_s[so][:, GC_], X[:, 0:256], AF.Copy), reads=[bX], writes=[bWTh[so][hz]])
                yield

            def step2(tg):
                ga, gb = step2h(tg, 0), step2h(tg, 1)
                while True:
                    ra = next(ga, "done")
                    rb = next(gb, "done")
                    if ra == "done" and rb == "done":
                        return
                    yield

            St = wsb("St", [128, 128], F32)
            Sb = [wsb("Sb%d" % i, [128, 128], BF16) for i in range(2)]
            vn = [wsb("vn%d" % i, [128, 128], BF16) for i in range(2)]
            bSt = Buf()
            bSb = [Buf() for _ in range(2)]
            bvn = [Buf() for _ in range(2)]
            og = wsb("og", [128, 512], F32)
            ogs = wsb("ogs", [128, 512], BF16)
            ogl = wsb("ogl", [128, 512], F32)
            ogr = wsb("ogr", [128, 512], F32)
            ogm = [wsb("ogm%d" % i, [128, 512], BF16) for i in range(2)]
            bog, bogs, bogl, bogr = Buf(), Buf(), Buf(), Buf()
            bogm = [Buf() for _ in range(2)]
            P.op("dve", lambda e: e.memset(St[:], 0.0), writes=[bSt])
            P.op("dve", lambda e: e.memset(Sb[0][:], 0.0), writes=[bSb[0]])
            self._cur = 0

            def scan_chunk(n, gen=None):
                adv = (lambda: next(gen, None)) if gen is not None else (lambda: None)
                cur = self._cur
                t, half = n // 2, n % 2
                tg = t // 4
                so = tg % 2
                tl = t % 4
                r0 = 64 * half
                cl = slice(tl * 128, (tl + 1) * 128)
                cc = slice(tl * 128 + r0, tl * 128 + r0 + 64)
                pa, pb_, po = 2, 3, 6 + (n // 8) % 2
                oc = slice((n % 8) * 64, (n % 8 + 1) * 64)
                v_ = vn[n % 2]
                bv_ = bvn[n % 2]
                hz = tl // 2
                P.op("pe", mm(ps[pa][:, 0:128], WT_s[so][:, cl], Sb[cur][:]), reads=[bWTh[so][hz], bSb[cur]], writes=[psb[pa]])
                P.op("pe", mm(ps[po][:, oc], Sb[cur][:], QDECT_s[so][:, cc], start=True, stop=False),
                     reads=[bSb[cur], bQDTh[so][hz]], writes=[psb[po]])
                P.op("dve", tt(v_[r0:r0 + 64, :], U_s[so][r0:r0 + 64, cl], ps[pa][r0:r0 + 64, 0:128], ALU.subtract),
                     reads=[bUh[so][hz], psb[pa]], writes=[bv_])
                adv()
                P.op("pe", mm(ps[pb_][:, 0:128], KDEC_s[so][r0:r0 + 64, cl], v_[r0:r0 + 64, :]), reads=[bKDh[so][hz], bv_], writes=[psb[pb_]])
                P.op("pe", mm(ps[po][:, oc], v_[r0:r0 + 64, :], QKT_s[so][r0:r0 + 64, cc], start=False, stop=True),
                     reads=[bv_, bQKTh[so][hz]], writes=[psb[po]])
                nxt = 1 - cur
                adv()
                P.op("dve", stt(Sb[nxt][:], St[:], GLb[half][:, t:t + 1], ps[pb_][:, 0:128], ALU.mult, ALU.add),
                     reads=[bSt, bgl, psb[pb_]], writes=[bSb[nxt]])
                P.op("dve", stt(St[:], St[:], GLb[half][:, t:t + 1], ps[pb_][:, 0:128], ALU.mult, ALU.add),
                     reads=[bSt, bgl, psb[pb_]], writes=[bSt])
                self._cur = nxt
                adv()
                if n % 8 == 7:
                    q8 = n // 8
                    c512 = slice(q8 * 512, (q8 + 1) * 512)
                    om = ogm[q8 % 2]
                    P.op("act", act(og[:], ps[po][:, :], AF.Copy), reads=[psb[po]], writes=[bog])
                    P.op("pool", tt(ogs[:], og[:], og[:], ALU.mult), reads=[bog], writes=[bogs])
                    P.op("pe", mm(ps[pb_][:, :], self.ones, ogs[:]), reads=[bogs, bc], writes=[psb[pb_]])
                    P.op("act", act(ogl[:], ps[pb_][:, :], AF.Ln, bias=EPS, scale=1.0 / 128), reads=[psb[pb_]], writes=[bogl])
                    P.op("act", act(ogr[:], ogl[:], AF.Exp, scale=-0.5), reads=[bogl], writes=[bogr])
                    P.op("dve", stt(og[:], og[:], pv[:, 1:2], ogr[:], ALU.mult, ALU.mult), reads=[bog, bogr, bc], writes=[bog])
                    P.op("dve", tt(om[:], og[:], self.Zs[:, c512], ALU.mult), reads=[bog, self.bZs[q8]], writes=[bogm[q8 % 2]])
                    self.mix_out(1, q8, om, bogm[q8 % 2])

            wc_list = [(self.wout_bf, self.wout[:, :]), (self.wdown_bf, self.wdown[:, :])]
            wc_list += [(self.wup_bf[i * 256:(i + 1) * 256, :], self.wup[i * 256:(i + 1) * 256, :]) for i in range(4)]
            for _ in step2(0):
                pass
            for tg in range(NT):
                gen = step2(tg + 1) if tg + 1 < NT else iter(())
                for n in range(tg * 8, tg * 8 + 8):
                    scan_chunk(n, gen)
                for _ in gen:
                    pass
                if wc_list and (tg % 2 == 1 or tg == NT - 1):
                    for _ in range(1 if tg < NT - 1 else len(wc_list)):
                        o_, i_ = wc_list.pop(0)
                        P.dma("pool", o_, i_, writes=[self.bwcast])
            self.mix_flush(1)
            P.barrier()

    def phase_attn(self):
        nc, P, S, NT = self.nc, self.P, self.S, self.NT
        ps, psb, bc = self.ps, self.psb, self.b_const
        with ExitStack() as ws:
            wsb = lambda n, s, d: ws.enter_context(nc.sbuf_tensor(n, s, d))
            QA = [wsb("QA0", [68, S], BF16), wsb("QA1", [68, S], BF16)]
            KA = [wsb("KA0", [68, S], BF16), wsb("KA1", [68, S], BF16)]
            V = wsb("Vat", [128, self.NK * 128], BF16)
            bQA, bKA, bV = Buf(), Buf(), Buf()
            allsc = self.bQD + self.bKD + self.bVD
            P.dma("sp", QA[0][0:64, :], self.QD[0:64, :], reads=allsc, writes=[bQA])
            P.dma("sp", QA[1][0:64, :], self.QD[64:128, :], reads=allsc, writes=[bQA])
            P.dma("sp", KA[0][0:64, :], self.KD[0:64, :], reads=allsc, writes=[bKA])
            P.dma("sp", KA[1][0:64, :], self.KD[64:128, :], reads=allsc, writes=[bKA])
            P.dma("sp", QA[0][64:68, :], self.qaug[:, :], writes=[bQA])
            P.dma("sp", KA[0][64:68, :], self.kaug[:, :], writes=[bKA])
            bQA1, bKA1 = Buf(), Buf()
            P.dma("sp", QA[1][64:68, :], self.qaug[:, :], writes=[bQA1])
            P.dma("sp", KA[1][64:68, :], self.kaug[:, :], writes=[bKA1])
            Vv = V[:].rearrange("p (n d) -> p n d", d=128)
            VDv = self.VD.rearrange("(n p) d -> p n d", p=128)
            for i in range(0, self.NK, 8):
                P.dma("sp", Vv[:, i:i + 8, :], VDv[:, i:i + 8, :], reads=allsc, writes=[bV])
            rows = [(0, 68), (0, 68)]
            lam = wsb("lam", [128, 256], F32)
            lt = wsb("lamt", [128, 128], F32)
            lsc = wsb("lsc", [128, 8], F32)
            blam = Buf()
            P.dma("sp", lam[:], self.lamv[:, :], writes=[blam])
            P.op("dve", tt(lt[:, 0:64], lam[:, 0:64], lam[:, 64:128], ALU.mult), reads=[blam], writes=[blam])
            P.op("dve", tt(lt[:, 64:128], lam[:, 128:192], lam[:, 192:256], ALU.mult), reads=[blam], writes=[blam])
            P.op("dve", lambda e: e.reduce_sum(out=lsc[:, 0:1], in_=lt[:, 0:64], axis=AX.X), reads=[blam], writes=[blam])
            P.op("dve", lambda e: e.reduce_sum(out=lsc[:, 1:2], in_=lt[:, 64:128], axis=AX.X), reads=[blam], writes=[blam])
            P.op("act", act(lsc[:, 2:4], lsc[:, 0:2], AF.Exp), reads=[blam], writes=[blam])
            P.op("dve", stt(lsc[:, 4:5], lsc[:, 3:4], -0.2, lsc[:, 2:3], ALU.add, ALU.subtract), reads=[blam], writes=[blam])
            P.op("dve", ts(lsc[:, 5:6], self.c_pv[:, 0:1], 0.8, None, ALU.mult), reads=[blam, bc], writes=[blam])
            neglam = lsc[:, 4:5]
            gsub = lsc[:, 5:6]

            pt = [wsb("pt%d" % i, [128, 512], BF16) for i in range(4)]
            bpt = [Buf() for _ in range(4)]
            rl = wsb("rl", [128, 512], F32)
            brl = Buf()
            On = [wsb("On%d" % i, [128, 512], F32) for i in range(2)]
            bOn = [Buf() for _ in range(2)]
            oa = wsb("oa", [128, 512], F32)
            boa = Buf()
            osq = wsb("osq", [128, 512], BF16)
            bosq = Buf()
            lnt = wsb("lnt2", [128, 512], F32)
            blnt = Buf()
            rs = wsb("rs", [128, 512], F32)
            brs = Buf()
            mst = [wsb("mst%d" % i, [128, 512], BF16) for i in range(2)]
            bmst = [Buf() for _ in range(2)]

            blocks = []
            for qi in range(NT):
                for c in range(2):
                    nkt = 4 * (qi + 1)
                    for kt in range(nkt):
                        blocks.append((qi, c, kt, nkt))
            nb = len(blocks)

            def qk(i):
                qi, c, kt, nkt = blocks[i]
                j = kt - 4 * qi
                col0 = 128 * j if j >= 0 else 0
                b = i % 3
                r0, r1 = rows[c]
                P.op("pe", mm(ps[b][:, col0:512], KA[c][r0:r1, kt * 128:(kt + 1) * 128],
                              QA[c][r0:r1, qi * 512 + col0:(qi + 1) * 512]),
                     reads=[bQA, bKA, bQA1, bKA1], writes=[psb[b]])

            def rest(i):
                qi, c, kt, nkt = blocks[i]
                g = qi * 2 + c
                j = kt - 4 * qi
                col0 = 128 * j if j >= 0 else 0
                b = i % 3
                sl_ = i % 4
                P.op("act", act(pt[sl_][:, col0:512], ps[b][:, col0:512], AF.Exp), reads=[psb[b]], writes=[bpt[sl_]])
                if j >= 0:
                    P.op("dve", tt(pt[sl_][:, col0:col0 + 128], pt[sl_][:, col0:col0 + 128], self.tri, ALU.mult),
                         reads=[bpt[sl_], bc], writes=[bpt[sl_]])
                po, pl = 3 + g % 2, 5 + g % 2
                P.op("pe", mm(ps[po][:, col0:512], V[:, kt * 128:(kt + 1) * 128], pt[sl_][:, col0:512],
                              start=(kt == 0), stop=(kt == nkt - 1)), reads=[bV, bpt[sl_]], writes=[psb[po]])
                P.op("pe", mm(ps[pl][:, col0:512], self.ones, pt[sl_][:, col0:512],
                              start=(kt == 0), stop=(kt == nkt - 1)), reads=[bc, bpt[sl_]], writes=[psb[pl]])
                if kt == nkt - 1:
                    cols = slice(qi * 512, (qi + 1) * 512)
                    P.op("dve", lambda e: e.reciprocal(out=rl[:], in_=ps[pl][:, :]), reads=[psb[pl]], writes=[brl])
                    P.op("dve", tt(On[c][:], ps[po][:, :], rl[:], ALU.mult), reads=[psb[po], brl], writes=[bOn[c]])
                    if c == 1:
                        P.op("dve", stt(oa[:], On[1][:], neglam, On[0][:], ALU.mult, ALU.add),
                             reads=[bOn[0], bOn[1], blam], writes=[boa])
                        P.op("pool", tt(osq[:], oa[:], oa[:], ALU.mult), reads=[boa], writes=[bosq])
                        P.op("pe", mm(ps[7][:, :], self.ones, osq[:]), reads=[bosq, bc], writes=[psb[7]])
                        P.op("act", act(lnt[:], ps[7][:, :], AF.Ln, bias=EPS, scale=1.0 / 128), reads=[psb[7]], writes=[blnt])
                        P.op("act", act(rs[:], lnt[:], AF.Exp, scale=-0.5), reads=[blnt], writes=[brs])
                        ms = mst[qi % 2]
                        P.op("dve", stt(ms[:], oa[:], gsub, rs[:], ALU.mult, ALU.mult),
                             reads=[boa, brs, blam], writes=[bmst[qi % 2]])
                        self.mix_out(0, qi, ms, bmst[qi % 2])

            qk(0)
            if nb > 1:
                qk(1)
            for i in range(nb):
                if i + 2 < nb:
                    pass
                self._attn_step(i, nb, qk, rest)
            self.mix_flush(0)
            P.barrier()

    def _attn_step(self, i, nb, qk, rest):
        if i + 2 < nb:
            qk(i + 2)
        rest(i)


    def _pidj(self, e):
        if getattr(self, "_pj", None) is None:
            pid = e.partition_id()
            self._pj = (pid % 4) * 8
        return self._pj

    def _mix_init(self):
        if hasattr(self, "zt"):
            return
        nc, P, es = self.nc, self.P, self.es
        self.zt = es.enter_context(nc.sbuf_tensor("zt", [128, 4], BF16))
        self.bz = Buf()
        P.op("pool", lambda e: e.memset(self.zt[:], 0.0), writes=[self.bz])
        self.bagi = [[Buf() for _ in range(4)] for _ in range(2)]
        self.bago = [Buf() for _ in range(8)]
        self._pend = [[], []]
        self.direct = (self.TPC % 512 == 0)
        self.agi = self.ag_in.ap().rearrange("(j f) t -> j f t", j=4)

    def _collective(self, half, j):
        P = self.P
        c = 2 * j + half
        ag_in, ag_out = self.ag_in, self.ag_out
        P.collective(lambda e, c=c: e.collective_compute(
            "AllGather", ALU.bypass, replica_groups=[[0, 1, 2, 3], [4, 5, 6, 7]],
            ins=[ag_in.ap()[c * 128:(c + 1) * 128, :]], outs=[ag_out.ap()[c * 512:(c + 1) * 512, :]]),
            self.cc_sem, reads=[self.bagi[half][j]], writes=[self.bago[c]])

    def mix_out(self, half, q8, ms, bms):
        P, TPC = self.P, self.TPC
        self._mix_init()
        r0, r1 = half * 128, (half + 1) * 128
        cols = slice(q8 * 512, (q8 + 1) * 512)
        bm = (self.bMIXa if half == 0 else self.bMIXg)[q8]
        for j in self._pend[half]:
            self._collective(half, j)
        self._pend[half] = []
        if self.debug or not self.direct:
            P.dma("sp", self.MIXD[r0:r1, cols], ms[:], reads=[bms], writes=[bm])
        if not self.direct:
            return
        j = (q8 * 512) // TPC
        off = q8 * 512 - j * TPC
        bi = self.bagi[half]
        if q8 == 0:
            P.dma("sp", self.agi[0, r0:r1, 0:2], self.zt[:, 0:2], reads=[self.bz], writes=[bi[0]])
        P.dma("sp", self.agi[j, r0:r1, 2 + off:2 + off + 512], ms[:], reads=[bms], writes=[bi[j]])
        if off + 512 == TPC:
            if j + 1 < 4:
                P.dma("sp", self.agi[j + 1, r0:r1, 0:2], ms[:, 510:512], reads=[bms], writes=[bi[j + 1]])
            self._pend[half].append(j)

    def mix_flush(self, half):
        P, TPC = self.P, self.TPC
        self._mix_init()
        W2 = TPC + 2
        if not self.direct:
            allm = self.bMIXa if half == 0 else self.bMIXg
            r0, r1 = half * 128, (half + 1) * 128
            for j in range(4):
                bi = self.bagi[half][j]
                P.dma("sp", self.agi[j, r0:r1, 2:W2], self.MIXD[r0:r1, j * TPC:(j + 1) * TPC], reads=allm, writes=[bi])
                if j > 0:
                    P.dma("sp", self.agi[j, r0:r1, 0:2], self.MIXD[r0:r1, j * TPC - 2:j * TPC], reads=allm, writes=[bi])
                else:
                    P.dma("sp", self.agi[0, r0:r1, 0:2], self.zt[:, 0:2], reads=[self.bz], writes=[bi])
                self._pend[half].append(j)
        for j in self._pend[half]:
            self._collective(half, j)
        self._pend[half] = []

    def phase2(self):
        nc, P, S, TPC = self.nc, self.P, self.S, self.TPC
        ps, psb, bc = self.ps, self.psb, self.b_const
        W2 = TPC + 2
        NH = 2
        HT = TPC // NH
        nt = -(-(HT + 2) // 512)
        base = (HT + 2) // nt
        tiles = []
        a0 = 0
        for i in range(nt):
            w = base + (1 if i < (HT + 2) - base * nt else 0)
            tiles.append((a0, w))
            a0 += w
        WM = max(w for _, w in tiles)
        NFC = D_FF // 128
        with ExitStack() as ws:
            wsb = lambda n, s, d: ws.enter_context(nc.sbuf_tensor(n, s, d))
            MIXT = wsb("MIXT", [128, 8 * W2], BF16)
            Wout = wsb("Wout", [128, 8 * 1024], BF16)
            Wd = wsb("Wd", [128, NFC * 1024], BF16)
            H2 = wsb("H2", [128, 8 * (HT + 2)], BF16)
            ACTT = wsb("ACTT", [128, NFC * HT], BF16)
            g2 = wsb("g2_sb", [128, 8], F32)
            fcw = wsb("fcw_sb", [128, 44 * 4], F32)
            gfin = wsb("gfin_sb", [128, 1024], F32)
            bWout, bWd, bH2, bsm = Buf(), Buf(), Buf(), Buf()
            bMIXTk = [Buf() for _ in range(8)]
            bACTT = [Buf() for _ in range(NFC)]
            agv = self.ag_out.ap().rearrange("(r f) t -> r f t", f=128)
            MIXTv = MIXT[:].rearrange("p (c t) -> p c t", c=8)
            for h in range(4):
                for half in range(2):
                    P.dma("pool", MIXTv[:, 2 * h + half, :],
                          lambda e, h=h, half=half: agv[bass.ds(self._pidj(e) + (4 * half + h), 1), :, :]
                          .rearrange("o f t -> (o f) t"),
                          reads=self.bago, writes=[bMIXTk[2 * h + half]])
            if self.stop == "p2x1":
                return
            P.dma("sp", Wout[:].rearrange("p (c n) -> p c n", c=8), self.wout_bf.rearrange("(c p) n -> p c n", p=128),
                  reads=[self.bwcast], writes=[bWout])
            Wdv = Wd[:].rearrange("p (c n) -> p c n", c=NFC)
            wdv = self.wdown_bf.rearrange("(c p) n -> p c n", p=128)
            for i in range(0, NFC, 4):
                P.dma("sp", Wdv[:, i:min(i + 4, NFC), :], wdv[:, i:min(i + 4, NFC), :], reads=[self.bwcast], writes=[bWd])
            P.dma("sp", g2[:], self.g2[:, :], writes=[bsm])
            P.dma("sp", fcw[:], self.fcw[:, :], writes=[bsm])
            P.dma("sp", gfin[:], self.gfin[:, :], writes=[bsm])

            for hf in range(NH):
                c0 = hf * HT
                with ExitStack() as wa:
                    asb = lambda n, s, d: wa.enter_context(nc.sbuf_tensor(n + '_h%d' % hf, s, d))
                    x2t = [asb("x2t%d" % i, [128, 8 * WM], F32) for i in range(2)]
                    x1T = asb("x1T", [128, 8 * WM], F32)
                    sq = asb("sq2", [128, 8 * WM], BF16)
                    lnt = asb("lnA", [128, WM], F32)
                    rstd = asb("rstdA", [128, WM], F32)
                    bx2t = [Buf() for _ in range(2)]
                    bx1, bsq, bln, brs = Buf(), Buf(), Buf(), Buf()
                    for ti, (a0, w) in enumerate(tiles):
                        cols = slice(c0 + a0, c0 + a0 + w)
                        xs = x2t[ti % 2]
                        P.dma("sp", xs[:, 0:8 * w].rearrange("p (c t) -> p c t", c=8),
                              self.xT2[:, cols].rearrange("(c p) t -> p c t", p=128), writes=[bx2t[ti % 2]])
                        for oc in range(8):
                            b = oc % 4
                            for kc in range(8):
                                P.op("pe", mm(ps[b][:, 0:w], Wout[:, kc * 1024 + oc * 128: kc * 1024 + (oc + 1) * 128],
                                              MIXT[:, kc * W2 + c0 + a0: kc * W2 + c0 + a0 + w], start=(kc == 0), stop=(kc == 7)),
                                     reads=[bWout, bMIXTk[kc]], writes=[psb[b]])
                            P.op("dve", tt(x1T[:, oc * w:(oc + 1) * w], ps[b][:, 0:w], xs[:, oc * w:(oc + 1) * w], ALU.add),
                                 reads=[psb[b], bx2t[ti % 2]], writes=[bx1])
                        P.op("pool", tt(sq[:, 0:8 * w], x1T[:, 0:8 * w], x1T[:, 0:8 * w], ALU.mult), reads=[bx1], writes=[bsq])
                        for c in range(8):
                            P.op("pe", mm(ps[4][:, 0:w], self.onesmean, sq[:, c * w:(c + 1) * w], start=(c == 0), stop=(c == 7)),
                                 reads=[bsq, bc], writes=[psb[4]])
                        P.op("act", act(lnt[:, 0:w], ps[4][:, 0:w], AF.Ln, bias=EPS), reads=[psb[4]], writes=[bln])
                        P.op("act", act(rstd[:, 0:w], lnt[:, 0:w], AF.Exp, scale=-0.5), reads=[bln], writes=[brs])
                        for kc in range(8):
                            P.op("dve", stt(H2[:, kc * (HT + 2) + a0: kc * (HT + 2) + a0 + w], x1T[:, kc * w:(kc + 1) * w],
                                            g2[:, kc:kc + 1], rstd[:, 0:w], ALU.mult, ALU.mult),
                                 reads=[bx1, brs, bsm], writes=[bH2])
                    P.barrier()
                if self.stop == "p2A":
                    break
                with ExitStack() as wc:
                    csb = lambda n, s, d: wc.enter_context(nc.sbuf_tensor(n + '_h%d' % hf, s, d))
                    Wg = [csb("Wg%d" % i, [128, 8 * 128], BF16) for i in range(2)]
                    Wv = [csb("Wv%d" % i, [128, 8 * 128], BF16) for i in range(2)]
                    cg = [csb("cg%d" % i, [128, HT], F32) for i in range(2)]
                    cv = [csb("cv%d" % i, [128, HT], F32) for i in range(2)]
                    sg = [csb("sg%d" % i, [128, HT], F32) for i in range(2)]
                    bWg = [Buf() for _ in range(2)]
                    bWv = [Buf() for _ in range(2)]
                    bcg = [Buf() for _ in range(2)]
                    bcv = [Buf() for _ in range(2)]
                    bsg = [Buf() for _ in range(2)]
                    nto = -(-HT // 510)
                    bo = HT // nto
                    otiles = []
                    o0 = 0
                    for i in range(nto):
                        wo = bo + (1 if i < HT - bo * nto else 0)
                        otiles.append((o0, wo))
                        o0 += wo
                    H2W = HT + 2

                    def loadw(fc):
                        sl_ = fc % 2
                        P.dma("sp", Wg[sl_][:].rearrange("p (c n) -> p c n", c=8),
                              self.wup_bf[:, fc * 128:(fc + 1) * 128].rearrange("(c p) n -> p c n", p=128),
                              reads=[self.bwcast], writes=[bWg[sl_]])
                        P.dma("sp", Wv[sl_][:].rearrange("p (c n) -> p c n", c=8),
                              self.wup_bf[:, D_FF + fc * 128: D_FF + (fc + 1) * 128].rearrange("(c p) n -> p c n", p=128),
                              reads=[self.bwcast], writes=[bWv[sl_]])

                    loadw(0)
                    pr = 0
                    for fc in range(NFC):
                        sl_ = fc % 2
                        if fc + 1 < NFC:
                            loadw(fc + 1)
                        wg_ = fcw[:, fc * 4: fc * 4 + 4]
                        wv_ = fcw[:, (NFC + fc) * 4: (NFC + fc) * 4 + 4]
                        for (o0, wo) in otiles:
                            bg_, bv_ = pr % 8, (pr + 1) % 8
                            pr += 2
                            n_ = wo + 2
                            for kc in range(8):
                                P.op("pe", mm(ps[bg_][:, 0:n_], Wg[sl_][:, kc * 128:(kc + 1) * 128],
                                              H2[:, kc * H2W + o0: kc * H2W + o0 + n_], start=(kc == 0), stop=(kc == 7)),
                                     reads=[bWg[sl_], bH2], writes=[psb[bg_]])
                            for kc in range(8):
                                P.op("pe", mm(ps[bv_][:, 0:n_], Wv[sl_][:, kc * 128:(kc + 1) * 128],
                                              H2[:, kc * H2W + o0: kc * H2W + o0 + n_], start=(kc == 0), stop=(kc == 7)),
                                     reads=[bWv[sl_], bH2], writes=[psb[bv_]])
                            og_ = cg[sl_][:, o0:o0 + wo]
                            ov_ = cv[sl_][:, o0:o0 + wo]
                            P.op("act", act(og_, ps[bg_][:, 0:wo], AF.Identity, bias=wg_[:, 3:4], scale=wg_[:, 0:1]),
                                 reads=[psb[bg_], bsm], writes=[bcg[sl_]])
                            P.op("act", act(ov_, ps[bv_][:, 0:wo], AF.Identity, bias=wv_[:, 3:4], scale=wv_[:, 0:1]),
                                 reads=[psb[bv_], bsm], writes=[bcv[sl_]])
                            for j in (1, 2):
                                P.op("dve", stt(og_, ps[bg_][:, j:j + wo], wg_[:, j:j + 1], og_, ALU.mult, ALU.add),
                                     reads=[psb[bg_], bsm], writes=[bcg[sl_]])
                                P.op("dve", stt(ov_, ps[bv_][:, j:j + wo], wv_[:, j:j + 1], ov_, ALU.mult, ALU.add),
                                     reads=[psb[bv_], bsm], writes=[bcv[sl_]])
                        P.op("act", act(sg[sl_][:], cg[sl_][:], AF.Silu), reads=[bcg[sl_]], writes=[bsg[sl_]])
                        P.op("pool", tt(ACTT[:, fc * HT:(fc + 1) * HT], sg[sl_][:], cv[sl_][:], ALU.mult),
                             reads=[bsg[sl_], bcv[sl_]], writes=[bACTT[fc]])
                    P.barrier()
                if self.stop == "p2C":
                    break
                with ExitStack() as wd:
                    dsb = lambda n, s, d: wd.enter_context(nc.sbuf_tensor(n + '_h%d' % hf, s, d))
                    xk = [dsb("xk%d" % i, [128, 1024], F32) for i in range(2)]
                    x2 = [dsb("x2_%d" % i, [128, 1024], F32) for i in range(2)]
                    sqd = dsb("sqd", [128, 1024], F32)
                    ot = [dsb("ot%d" % i, [128, 1024], F32) for i in range(2)]
                    st = dsb("std", [128, 8], F32)
                    bxk = [Buf() for _ in range(2)]
                    bx2 = [Buf() for _ in range(2)]
                    bot = [Buf() for _ in range(2)]
                    bsqd, bst = Buf(), Buf()
                    for sub in range(HT // 128):
                        s2_ = sub % 2
                        tok0 = hf * HT + sub * 128
                        P.dma("sp", xk[s2_][:], self.xtok2[tok0:tok0 + 128, :], writes=[bxk[s2_]])
                        for oh in range(2):
                            b = (sub * 2 + oh) % 4
                            for fc in range(NFC):
                                P.op("pe", mm(ps[b][:, :], ACTT[:, fc * HT + sub * 128: fc * HT + (sub + 1) * 128],
                                              Wd[:, fc * 1024 + oh * 512: fc * 1024 + (oh + 1) * 512], start=(fc == 0), stop=False),
                                     reads=[bACTT[fc], bWd], writes=[psb[b]])
                            for kc in range(8):
                                P.op("pe", mm(ps[b][:, :], MIXT[:, kc * W2 + 2 + tok0: kc * W2 + 2 + tok0 + 128],
                                              Wout[:, kc * 1024 + oh * 512: kc * 1024 + (oh + 1) * 512], start=False, stop=(kc == 7)),
                                     reads=[bMIXTk[kc], bWout], writes=[psb[b]])
                            P.op("dve", tt(x2[s2_][:, oh * 512:(oh + 1) * 512], ps[b][:, :], xk[s2_][:, oh * 512:(oh + 1) * 512], ALU.add),
                                 reads=[psb[b], bxk[s2_]], writes=[bx2[s2_]])
                        P.op("pool", tt(sqd[:], x2[s2_][:], x2[s2_][:], ALU.mult), reads=[bx2[s2_]], writes=[bsqd])
                        P.op("dve", lambda e, st=st, sqd=sqd: e.reduce_sum(out=st[:, 0:1], in_=sqd[:], axis=AX.X), reads=[bsqd], writes=[bst])
                        P.op("act", act(st[:, 1:2], st[:, 0:1], AF.Ln, bias=EPS, scale=1.0 / 1024), reads=[bst], writes=[bst])
                        P.op("act", act(st[:, 2:3], st[:, 1:2], AF.Exp, scale=-0.5), reads=[bst], writes=[bst])
                        P.op("dve", stt(ot[s2_][:], x2[s2_][:], st[:, 2:3], gfin[:], ALU.mult, ALU.mult),
                             reads=[bx2[s2_], bst, bsm], writes=[bot[s2_]])
                        P.dma("sp", self.out[tok0:tok0 + 128, :], ot[s2_][:], reads=[bot[s2_]])
                    P.barrier()

    def dump_1a(self):
        P, d = self.P, self.dbg
        allb = self.bQD + self.bKD + self.bVD
        P.dma("sp", d["QD"], self.QD, reads=allb)
        P.dma("sp", d["KD"], self.KD, reads=allb)
        P.dma("sp", d["VD"], self.VD, reads=allb)
        P.dma("sp", d["QgT"], self.QgT[:], reads=self.bQg)
        P.dma("sp", d["KgT"], self.KgT[:], reads=self.bKg)
        P.dma("sp", d["Vg"], self.Vg[:], reads=self.bVg)
        P.dma("sp", d["Kt"], self.Kt[:], reads=self.bKt)
        P.dma("sp", d["Zs"], self.Zs[:], reads=self.bZs)
        P.dma("sp", d["GAB"], self.GAB[:], reads=self.bGAB)


def bf(a):
    return np.ascontiguousarray(a).astype(ml_dtypes.bfloat16)


def host_consts(S, h):
    slope = 2.0 ** (-8.0 * (h + 1) / 4)
    pos = np.arange(S)
    a, b = pos // 128, pos % 128
    qaug = np.stack([-slope * 128.0 * a, -slope * b, np.ones(S), np.ones(S)]).astype(np.float32)
    kaug = np.stack([np.ones(S), np.ones(S), slope * 128.0 * a, slope * b]).astype(np.float32)
    ident = np.eye(128, dtype=np.float32)
    ones = np.ones((128, 128), np.float32)
    k = np.arange(128)[:, None]
    q = np.arange(128)[None, :]
    tri = (q >= k).astype(np.float32)
    cbf = np.concatenate([ident, ones, ones / 1024.0, tri], axis=1)
    same = (k // 64) == (q // 64)
    tri_incl = ((k <= q) & same).astype(np.float32)
    blk = same.astype(np.float32)
    negm = np.where((q >= k) & same, 0.0, NEG).astype(np.float32)
    strict = ((q > k) & same).astype(np.float32)
    ch0 = np.repeat((np.arange(128) < 64).astype(np.float32)[:, None], 128, axis=1)
    ch1 = 1.0 - ch0
    cf32 = np.concatenate([ident, ones, tri_incl, blk, negm, strict, ch0, ch1], axis=1)
    return bf(qaug), bf(kaug), bf(cbf), cf32.astype(np.float32)


def split_cols(h):
    r = lambda base, n: list(range(base + h * n, base + (h + 1) * n))
    cols = r(0, 128) + r(512, 128) + r(1536, 128) + r(2048, 128) + r(2560, 128) + r(3080, 128) + r(1024, 128)
    cols += [3072 + h, 3076 + h]
    return np.array(cols)


def make_in_maps(inputs, S):
    x = np.asarray(inputs["x"], np.float32)
    B = x.shape[0]
    TPC = S // 4
    w_in = np.asarray(inputs["w_in"], np.float32)[0]
    per = lambda v: np.ascontiguousarray(np.asarray(v, np.float32).reshape(8, 128).T)
    maps = []
    for r in range(NCORES):
        b, h = r // 4, r % 4
        j = h
        qaug, kaug, cbf, cf32 = host_consts(S, h)
        m = {}
        m["xT"] = np.ascontiguousarray(x[b].T)
        m["wh"] = np.ascontiguousarray(w_in[:, split_cols(h)])
        m["gA"] = per(inputs["attn_norm_g"][0])
        cwfull = np.asarray(inputs["gdn_conv_w"], np.float32)[0]
        cw = np.zeros((128, 12), np.float32)
        for g in range(3):
            cw[:, g * 4:(g + 1) * 4] = cwfull[:, g * 512 + h * 128: g * 512 + (h + 1) * 128].T
        m["convw"] = cw
        m["qaug"], m["kaug"], m["cbf"], m["cf32"] = qaug, kaug, cbf, cf32
        pv = np.zeros((128, 16), np.float32)
        pv[:, 0] = np.asarray(inputs["da_subln_g"], np.float32)[0]
        pv[:, 1] = np.asarray(inputs["gdn_norm_g"], np.float32)[0]
        pv[:, 2] = np.asarray(inputs["gdn_a_log"], np.float32)[0, h]
        pv[:, 3] = np.asarray(inputs["gdn_dt_bias"], np.float32)[0, h]
        m["pvec"] = pv
        lv = np.concatenate([np.asarray(inputs[k], np.float32)[0] for k in
                             ("da_lambda_q1", "da_lambda_k1", "da_lambda_q2", "da_lambda_k2")])
        m["lamv"] = np.ascontiguousarray(np.broadcast_to(lv[None, :], (128, 256)))
        xT2 = np.zeros((D_MODEL, TPC + 2), np.float32)
        lo = j * TPC
        xT2[:, 2:] = x[b, lo:lo + TPC].T
        if j > 0:
            xT2[:, 0:2] = x[b, lo - 2:lo].T
        m["xT2"] = xT2
        m["xtok2"] = np.ascontiguousarray(x[b, lo:lo + TPC])
        wo = np.asarray(inputs["w_out"], np.float32)[0]
        rows = []
        for hh in range(4):
            rows += list(range(hh * 128, (hh + 1) * 128)) + list(range(512 + hh * 128, 512 + (hh + 1) * 128))
        m["wout"] = np.ascontiguousarray(wo[np.array(rows)])
        m["wup"] = np.ascontiguousarray(np.asarray(inputs["w_up"], np.float32)[0])
        m["wdown"] = np.ascontiguousarray(np.asarray(inputs["w_down"], np.float32)[0])
        m["g2"] = per(inputs["ffn_norm_g"][0])
        fw = np.asarray(inputs["ffn_conv_w"], np.float32)[0]
        fb = np.asarray(inputs["ffn_conv_b"], np.float32)[0]
        fcw = np.zeros((128, 44 * 4), np.float32)
        for c in range(44):
            fcw[:, c * 4:c * 4 + 3] = fw[:, c * 128:(c + 1) * 128].T
            fcw[:, c * 4 + 3] = fb[c * 128:(c + 1) * 128]
        m["fcw"] = fcw
        m["gfin"] = np.ascontiguousarray(np.broadcast_to(np.asarray(inputs["final_norm_g"], np.float32)[None, :], (128, D_MODEL)))
        maps.append(m)
    return maps


_CACHE = {}


def kernel(**inputs):
    x = np.asarray(inputs["x"])
    B, S, _ = x.shape
    if S not in _CACHE:
        _CACHE[S] = Builder(S).build()
    nc = _CACHE[S]
    maps = make_in_maps(inputs, S)
    res = run_bass_kernel_spmd(nc, maps, core_ids=list(range(NCORES)))
    out = np.zeros((B, S, D_MODEL), np.float32)
    TPC = S // 4
    for r in range(NCORES):
        b, j = r // 4, r % 4
        out[b, j * TPC:(j + 1) * TPC] = np.asarray(res.results[r]["out"], np.float32)
    return out
```

```python
import math
from contextlib import ExitStack

import numpy as np
import ml_dtypes

import concourse.bass as bass
import concourse.mybir as mybir
from concourse.bass_utils import run_bass_kernel_spmd

F32 = mybir.dt.float32
BF16 = mybir.dt.bfloat16
AF = mybir.ActivationFunctionType
ALU = mybir.AluOpType
AX = mybir.AxisListType

D_MODEL = 1024
EPS = 1e-6
D_FF = 2816
NCORES = 8
NEG = -30000.0
SAME_ENGINE_SYNC = True


class Buf:
    __slots__ = ("w", "r", "name")

    def __init__(self, name=""):
        self.w = None
        self.r = []
        self.name = name


class Prog:
    CE = ("pe", "act", "dve", "pool")

    SEM_LIMIT = 2000

    def __init__(self, nc, es, n_dma_sems=40):
        self.nc = nc
        self.es = es
        self.nsem = 0
        self.q = {e: [] for e in ("pe", "act", "dve", "pool", "sp")}
        self.sem = {e: es.enter_context(nc.semaphore("s_" + e)) for e in self.CE}
        self.cnt = {e: 0 for e in self.CE}
        self.seen = {e: {} for e in self.q}
        self.dsem = [es.enter_context(nc.semaphore("d%d" % i)) for i in range(n_dma_sems)]
        self.dcnt = [0] * n_dma_sems
        self.dnext = 0
        self.dnext_pool = 0
        self.dma_toks = []
        self.nops = 0

    def _wait(self, eng, tok):
        sem, val, src = tok
        k = id(sem)
        if self.seen[eng].get(k, 0) >= val:
            return
        self.seen[eng][k] = val
        self.q[eng].append(lambda e, sem=sem, val=val: e.wait_ge(sem, val))

    def _deps(self, eng, reads, writes):
        for b in reads:
            if b.w is not None:
                if b.w[2] == eng and (eng == "pe" or not SAME_ENGINE_SYNC):
                    continue
                self._wait(eng, b.w)
        for b in writes:
            if b.w is not None and (b.w[2] != eng or (SAME_ENGINE_SYNC and eng != "pe")):
                self._wait(eng, b.w)
            for t in b.r:
                if t[2] != eng or (SAME_ENGINE_SYNC and eng != "pe"):
                    self._wait(eng, t)

    def op(self, eng, fn, reads=(), writes=()):
        self._deps(eng, reads, writes)
        if self.cnt[eng] >= self.SEM_LIMIT:
            self.nsem += 1
            self.sem[eng] = self.es.enter_context(self.nc.semaphore("s_%s_%d" % (eng, self.nsem)))
            self.cnt[eng] = 0
        self.cnt[eng] += 1
        sem = self.sem[eng]
        tok = (sem, self.cnt[eng], eng)
        self.q[eng].append(lambda e, fn=fn, sem=sem: fn(e).then_inc(sem, 1))
        for b in writes:
            b.w = tok
            b.r = []
        for b in reads:
            b.r.append(tok)
        self.nops += 1
        return tok

    def dma(self, eng, out, in_, reads=(), writes=()):
        self._deps(eng, reads, writes)
        npool = 8
        if eng == "pool":
            i = self.dnext_pool
            self.dnext_pool = (self.dnext_pool + 1) % npool
        else:
            i = npool + self.dnext
            self.dnext = (self.dnext + 1) % (len(self.dsem) - npool)
        sem = self.dsem[i]
        if self.dcnt[i] > 0:
            self._wait(eng, (sem, self.dcnt[i], "dma"))
        self.dcnt[i] += 16
        tok = (sem, self.dcnt[i], "dma")
        self.q[eng].append(lambda e, out=out, in_=in_, sem=sem: e.dma_start(
            out=out, in_=(in_(e) if callable(in_) else in_)).then_inc(sem, 16))
        for b in writes:
            b.w = tok
            b.r = []
        for b in reads:
            b.r.append(tok)
        self.dma_toks.append(tok)
        self.nops += 1
        return tok

    def collective(self, fn, sem, reads=(), writes=()):
        self._deps("pool", reads, writes)
        self.ccnt = getattr(self, "ccnt", 0) + 1
        tok = (sem, self.ccnt, "cc")
        self.q["pool"].append(lambda e, fn=fn, sem=sem: fn(e).then_inc(sem, 1))
        for b in writes:
            b.w = tok
            b.r = []
        for b in reads:
            b.r.append(tok)
        return tok

    def raw(self, eng, fn):
        self.q[eng].append(fn)

    def barrier(self):
        toks = [(self.sem[e], self.cnt[e], e) for e in self.CE if self.cnt[e] > 0]
        toks += [(self.dsem[i], self.dcnt[i], "dma") for i in range(len(self.dsem)) if self.dcnt[i] > 0]
        for e in self.q:
            for t in toks:
                if t[2] == e:
                    continue
                self._wait(e, t)

    def finish(self):
        for i in range(len(self.dsem)):
            if self.dcnt[i] > 0:
                self._wait("sp", (self.dsem[i], self.dcnt[i], "dma"))

    def replay(self, block):
        q = self.q

        @block.tensor
        def _(e):
            for f in q["pe"]:
                f(e)

        @block.scalar
        def _(e):
            for f in q["act"]:
                f(e)

        @block.vector
        def _(e):
            for f in q["dve"]:
                f(e)

        @block.gpsimd
        def _(e):
            for f in q["pool"]:
                f(e)

        @block.sync
        def _(e):
            for f in q["sp"]:
                f(e)


def act(out, in_, func, bias=0.0, scale=1.0):
    return lambda e: e.activation(out=out, in_=in_, func=func, bias=bias, scale=scale)


def mm(out, lhsT, rhs, start=True, stop=True):
    return lambda e: e.matmul(out, lhsT, rhs, start=start, stop=stop)


def tt(out, in0, in1, op):
    return lambda e: e.tensor_tensor(out=out, in0=in0, in1=in1, op=op)


def ts(out, in0, s1, s2, op0, op1=None):
    if op1 is None:
        return lambda e: e.tensor_scalar(out=out, in0=in0, scalar1=s1, scalar2=None, op0=op0)
    return lambda e: e.tensor_scalar(out=out, in0=in0, scalar1=s1, scalar2=s2, op0=op0, op1=op1)


def stt(out, in0, scalar, in1, op0, op1):
    return lambda e: e.scalar_tensor_tensor(out=out, in0=in0, scalar=scalar, in1=in1, op0=op0, op1=op1)


def cp(out, in_):
    return lambda e: e.tensor_copy(out=out, in_=in_)


class Builder:
    def __init__(self, S, debug=False, stop=None):
        self.S = S
        self.debug = debug
        self.stop = stop
        self.NT = S // 512
        self.NK = S // 128
        self.TPC = S // 4
        self.nc = bass.Bass("TRN2", target_bir_lowering=False)

    def declare_io(self):
        nc, S = self.nc, self.S
        di = lambda n, s, d=F32: nc.dram_tensor(n, s, d, kind="ExternalInput").ap()
        self.xT = di("xT", [D_MODEL, S])
        self.wh = di("wh", [D_MODEL, 898])
        self.gA = di("gA", [128, 8])
        self.convw = di("convw", [128, 12])
        self.qaug = di("qaug", [4, S], BF16)
        self.kaug = di("kaug", [4, S], BF16)
        self.cbf = di("cbf", [128, 4 * 128], BF16)
        self.cf32 = di("cf32", [128, 8 * 128])
        self.pvec = di("pvec", [128, 16])
        self.lamv = di("lamv", [128, 4 * 64])
        self.xT2 = di("xT2", [D_MODEL, self.TPC + 2])
        self.xtok2 = di("xtok2", [self.TPC, D_MODEL])
        self.wout = di("wout", [D_MODEL, D_MODEL])
        self.wup = di("wup", [D_MODEL, 2 * D_FF])
        self.wdown = di("wdown", [D_FF, D_MODEL])
        self.g2 = di("g2", [128, 8])
        self.fcw = di("fcw", [128, 44 * 4])
        self.gfin = di("gfin", [128, D_MODEL])
        self.out = nc.dram_tensor("out", [self.TPC, D_MODEL], F32, kind="ExternalOutput").ap()
        self.QD = nc.dram_tensor("QD", [128, S], BF16).ap()
        self.KD = nc.dram_tensor("KD", [128, S], BF16).ap()
        self.VD = nc.dram_tensor("VD", [S, 128], BF16).ap()
        self.MIXD = nc.dram_tensor("MIXD", [256, S], BF16).ap()
        W2 = self.TPC + 2
        self.ag_in = nc.dram_tensor("ag_in", [4 * 256, W2], BF16)
        self.ag_out = nc.dram_tensor("ag_out", [32 * 256, W2], BF16)
        self.wup_bf = nc.dram_tensor("wup_bf", [D_MODEL, 2 * D_FF], BF16).ap()
        self.wdown_bf = nc.dram_tensor("wdown_bf", [D_FF, D_MODEL], BF16).ap()
        self.wout_bf = nc.dram_tensor("wout_bf", [D_MODEL, D_MODEL], BF16).ap()
        if self.debug:
            do = lambda n, s, d=F32: nc.dram_tensor(n, s, d, kind="ExternalOutput").ap()
            self.dbg = {
                "QD": do("dQD", [128, S], BF16), "KD": do("dKD", [128, S], BF16), "VD": do("dVD", [S, 128], BF16),
                "QgT": do("dQgT", [128, S], BF16), "KgT": do("dKgT", [128, S], BF16),
                "Vg": do("dVg", [128, self.NK * 128], BF16), "Kt": do("dKt", [128, self.NK * 128], BF16),
                "Zs": do("dZs", [128, S], BF16), "GAB": do("dGAB", [128, self.NK * 2]),
                "MIX": do("dMIX", [256, S], BF16),
            }

    def build(self):
        nc = self.nc
        self.declare_io()
        with ExitStack() as es:
            P = self.P = Prog(nc, es)
            self.es = es
            self.ps = [es.enter_context(nc.psum_tensor("ps%d" % i, [128, 512], F32)) for i in range(8)]
            self.psb = [Buf("ps%d" % i) for i in range(8)]
            sb = lambda n, s, d: es.enter_context(nc.sbuf_tensor(n, s, d))
            self.c_bf = sb("c_bf", [128, 512], BF16)
            self.c_f32 = sb("c_f32", [128, 1024], F32)
            self.c_pv = sb("c_pv", [128, 16], F32)
            self.b_const = Buf("const")
            P.dma("sp", self.c_bf[:], self.cbf[:, :], writes=[self.b_const])
            P.dma("sp", self.c_f32[:], self.cf32[:, :], writes=[self.b_const])
            P.dma("sp", self.c_pv[:], self.pvec[:, :], writes=[self.b_const])
            self.ident = self.c_bf[:, 0:128]
            self.ones = self.c_bf[:, 128:256]
            self.onesmean = self.c_bf[:, 256:384]
            self.tri = self.c_bf[:, 384:512]

            self.bMIXa = [Buf() for _ in range(self.NT)]
            self.bMIXg = [Buf() for _ in range(self.NT)]
            self.cc_sem = es.enter_context(nc.semaphore("cc_sem"))
            self.bwcast = Buf("wcast")
            self._mix_init()
            with ExitStack() as es1:
                self.es1 = es1
                self.phase1a()
                if self.debug == "1a":
                    self.dump_1a()
                else:
                    self.phase_gdn()
            if self.debug != "1a":
                full = self.stop is None or self.stop.startswith("p2")
                if self.stop != "gdn":
                    self.phase_attn()
                if self.debug:
                    P.dma("sp", self.dbg["MIX"], self.MIXD, reads=self.bMIXa + self.bMIXg)
                if full and self.stop != "p2x0":
                    self.phase2()
            P.finish()
            block = es.enter_context(nc.Block())
            P.replay(block)
        return nc

    def rsqrt_act(self, out, in_, tmp, reads, writes, tmpbuf, eps=EPS):
        P = self.P
        P.op("act", act(tmp, in_, AF.Ln, bias=eps), reads=reads, writes=[tmpbuf])
        P.op("act", act(out, tmp, AF.Exp, scale=-0.5), reads=[tmpbuf], writes=writes)

    def phase1a(self):
        nc, P, S, NT = self.nc, self.P, self.S, self.NT
        es = self.es1
        sb = lambda n, s, d: es.enter_context(nc.sbuf_tensor(n, s, d))
        ps, psb = self.ps, self.psb
        bc = self.b_const
        self.QgT = sb("QgT", [128, S], BF16)
        self.KgT = sb("KgT", [128, S], BF16)
        self.Vg = sb("Vg", [128, self.NK * 128], BF16)
        self.Kt = sb("Kt", [128, self.NK * 128], BF16)
        self.Zs = sb("Zs", [128, S], BF16)
        self.GAB = sb("GAB", [128, self.NK * 2], F32)
        self.bQg = [Buf() for _ in range(NT)]
        self.bKg = [Buf() for _ in range(NT)]
        self.bVg = [Buf() for _ in range(NT)]
        self.bKt = [Buf() for _ in range(NT)]
        self.bZs = [Buf() for _ in range(NT)]
        self.bGAB = [Buf() for _ in range(NT)]
        self.bQD = [Buf() for _ in range(NT)]
        self.bKD = [Buf() for _ in range(NT)]
        self.bVD = [Buf() for _ in range(NT)]

        with ExitStack() as ws:
            wsb = lambda n, s, d: ws.enter_context(nc.sbuf_tensor(n, s, d))
            W = wsb("W_sb", [128, 8 * 898], BF16)
            bW = Buf("W")
            Wv = W[:].rearrange("p (c n) -> p c n", c=8)
            P.dma("pool", Wv, self.wh.rearrange("(c p) n -> p c n", p=128), writes=[bW])
            self.emit_wcasts = emit_wcasts = lambda: self._emit_wcasts()
            def _unused():
                P.dma("pool", self.wout_bf, self.wout[:, :], writes=[self.bwcast])
                P.dma("pool", self.wdown_bf, self.wdown[:, :], writes=[self.bwcast])
                for i in range(4):
                    P.dma("pool", self.wup_bf[i * 256:(i + 1) * 256, :], self.wup[i * 256:(i + 1) * 256, :], writes=[self.bwcast])
            gA = wsb("gA_sb", [128, 8], F32)
            cw = wsb("cw_sb", [128, 12], F32)
            bsm = Buf("small")
            P.dma("sp", gA[:], self.gA[:, :], writes=[bsm])
            P.dma("sp", cw[:], self.convw[:, :], writes=[bsm])

            xt = [wsb("xt%d" % i, [128, 8 * 512], F32) for i in range(2)]
            sq = [wsb("sq%d" % i, [128, 8 * 512], BF16) for i in range(2)]
            xb = [wsb("xb%d" % i, [128, 8 * 512], BF16) for i in range(2)]
            rstd = [wsb("rstd%d" % i, [128, 512], F32) for i in range(2)]
            lntmp = wsb("lntmp", [128, 512], F32)
            bxt = [Buf() for _ in range(2)]
            bsq = [Buf() for _ in range(2)]
            bxb = [Buf() for _ in range(2)]
            brstd = [Buf() for _ in range(2)]
            bln = Buf()
            qst = [wsb("qst%d" % i, [128, 512], BF16) for i in range(2)]
            kst = [wsb("kst%d" % i, [128, 512], BF16) for i in range(2)]
            vst = [wsb("vst%d" % i, [128, 512], BF16) for i in range(2)]
            bqst = [Buf() for _ in range(2)]
            bkst = [Buf() for _ in range(2)]
            bvst = [Buf() for _ in range(2)]
            cs = [[wsb("cs%d_%d" % (g, i), [128, 515], F32) for i in range(2)] for g in range(3)]
            bcs = [[Buf() for _ in range(2)] for g in range(3)]
            yc = [wsb("yc%d" % g, [128, 512], F32) for g in range(3)]
            byc = [Buf() for _ in range(3)]
            sl = [wsb("sl%d" % g, [128, 512], F32) for g in range(2)]
            bsl = [Buf() for _ in range(2)]
            s2 = [wsb("s2_%d" % g, [128, 512], BF16) for g in range(2)]
            bs2 = [Buf() for _ in range(2)]
            rn = [wsb("rn%d" % g, [128, 512], F32) for g in range(2)]
            brn = [Buf() for _ in range(2)]
            vs = wsb("vs", [128, 512], BF16)
            bvs = Buf()
            for g in range(3):
                P.op("dve", lambda e, g=g: e.memset(cs[g][1][:, 512:515], 0.0), writes=[bcs[g][1]])

            fm_rot = [4, 5, 6]
            rot = [0]

            def stageA0(t):
                sl_ = t % 2
                xv = xt[sl_][:].rearrange("p (c t) -> p c t", c=8)
                P.dma("sp", xv, self.xT[:, t * 512:(t + 1) * 512].rearrange("(c p) t -> p c t", p=128),
                      writes=[bxt[sl_]])
                P.op("pool", tt(sq[sl_][:], xt[sl_][:], xt[sl_][:], ALU.mult), reads=[bxt[sl_]], writes=[bsq[sl_]])

            def stageA1a(t):
                sl_ = t % 2
                for c in range(8):
                    P.op("pe", mm(ps[0][:, :], self.onesmean, sq[sl_][:, c * 512:(c + 1) * 512], start=(c == 0), stop=(c == 7)),
                         reads=[bsq[sl_], bc], writes=[psb[0]])
                self.rsqrt_act(rstd[sl_][:], ps[0][:, :], lntmp[:], [psb[0]], [brstd[sl_]], bln)

            def stageA1b(t):
                sl_ = t % 2
                for c in range(8):
                    P.op("dve", stt(xb[sl_][:, c * 512:(c + 1) * 512], xt[sl_][:, c * 512:(c + 1) * 512], gA[:, c:c + 1],
                                    rstd[sl_][:], ALU.mult, ALU.mult),
                         reads=[bxt[sl_], brstd[sl_], bsm], writes=[bxb[sl_]])

            def fm_group(t, gi):
                sl_ = t % 2
                b = fm_rot[rot[0] % 3]
                rot[0] += 1
                for c in range(8):
                    P.op("pe", mm(ps[b][:, :], W[:, c * 898 + gi * 128: c * 898 + (gi + 1) * 128], xb[sl_][:, c * 512:(c + 1) * 512],
                                  start=(c == 0), stop=(c == 7)), reads=[bxb[sl_], bW], writes=[psb[b]])
                return b

            def stageB1(t):
                sl_ = t % 2
                cols = slice(t * 512, (t + 1) * 512)
                b = fm_group(t, 0)
                P.op("act", act(qst[sl_][:], ps[b][:, :], AF.Copy, scale=0.125), reads=[psb[b]], writes=[bqst[sl_]])
                P.dma("sp", self.QD[:, cols], qst[sl_][:], reads=[bqst[sl_]], writes=[self.bQD[t]])
                b = fm_group(t, 1)
                P.op("act", act(kst[sl_][:], ps[b][:, :], AF.Copy), reads=[psb[b]], writes=[bkst[sl_]])
                P.dma("sp", self.KD[:, cols], kst[sl_][:], reads=[bkst[sl_]], writes=[self.bKD[t]])
                for g in range(3):
                    b = fm_group(t, 2 + g)
                    P.op("act", act(cs[g][sl_][:, 3:515], ps[b][:, :], AF.Copy), reads=[psb[b]], writes=[bcs[g][sl_]])
                    P.op("dve", cp(cs[g][sl_][:, 0:3], cs[g][1 - sl_][:, 512:515]), reads=[bcs[g][1 - sl_]], writes=[bcs[g][sl_]])
                for j in range(4):
                    for c in range(8):
                        P.op("pe", mm(ps[3][:, j * 128:(j + 1) * 128], xb[sl_][:, c * 512 + j * 128: c * 512 + (j + 1) * 128],
                                      W[:, c * 898 + 768: c * 898 + 896], start=(c == 0), stop=(c == 7)),
                             reads=[bxb[sl_], bW], writes=[psb[3]])
                    for c in range(8):
                        P.op("pe", mm(ps[7][:, j * 2:(j + 1) * 2], xb[sl_][:, c * 512 + j * 128: c * 512 + (j + 1) * 128],
                                      W[:, c * 898 + 896: c * 898 + 898], start=(c == 0), stop=(c == 7)),
                             reads=[bxb[sl_], bW], writes=[psb[7]])
                P.op("act", act(vst[sl_][:], ps[3][:, :], AF.Copy), reads=[psb[3]], writes=[bvst[sl_]])
                P.dma("sp", self.VD.rearrange("(n p) d -> p n d", p=128)[:, t * 4:(t + 1) * 4, :],
                      vst[sl_][:].rearrange("p (n d) -> p n d", n=4), reads=[bvst[sl_]], writes=[self.bVD[t]])
                P.op("dve", cp(self.GAB[:, t * 8:(t + 1) * 8], ps[7][:, 0:8]), reads=[psb[7]], writes=[self.bGAB[t]])
                b = fm_group(t, 5)
                P.op("act", act(self.Zs[:, cols], ps[b][:, :], AF.Silu), reads=[psb[b]], writes=[self.bZs[t]])

            def stageB2a(t):
                sl_ = t % 2
                for g in range(3):
                    P.op("dve", ts(yc[g][:], cs[g][sl_][:, 0:512], cw[:, g * 4:g * 4 + 1], None, ALU.mult),
                         reads=[bcs[g][sl_], bsm], writes=[byc[g]])
                for j in range(1, 4):
                    for g in range(3):
                        P.op("dve", stt(yc[g][:], cs[g][sl_][:, j:j + 512], cw[:, g * 4 + j:g * 4 + j + 1], yc[g][:], ALU.mult, ALU.add),
                             reads=[bcs[g][sl_], byc[g], bsm], writes=[byc[g]])
                for g in range(3):
                    if g < 2:
                        P.op("act", act(sl[g][:], yc[g][:], AF.Silu), reads=[byc[g]], writes=[bsl[g]])
                        P.op("pool", tt(s2[g][:], sl[g][:], sl[g][:], ALU.mult), reads=[bsl[g]], writes=[bs2[g]])
                    else:
                        P.op("act", act(vs[:], yc[g][:], AF.Silu), reads=[byc[g]], writes=[bvs])

            def stageB2b1(t):
                cols = slice(t * 512, (t + 1) * 512)
                for g in range(2):
                    P.op("pe", mm(ps[1][:, :], self.ones, s2[g][:]), reads=[bs2[g], bc], writes=[psb[1]])
                    self.rsqrt_act(rn[g][:], ps[1][:, :], lntmp[:], [psb[1]], [brn[g]], bln)
                    dst, bd = (self.QgT, self.bQg) if g == 0 else (self.KgT, self.bKg)
                    scl = 128.0 ** -0.5 if g == 0 else 1.0
                    P.op("dve", stt(dst[:, cols], sl[g][:], scl, rn[g][:], ALU.mult, ALU.mult),
                         reads=[bsl[g], brn[g]], writes=[bd[t]])

            def stageB2b2(t):
                cols = slice(t * 512, (t + 1) * 512)
                for (src, bsrc, dst, bdst) in ((vs[:], bvs, self.Vg, self.bVg[t]), (self.KgT[:, cols], self.bKg[t], self.Kt, self.bKt[t])):
                    for j in range(4):
                        P.op("pe", mm(ps[2][:, j * 128:(j + 1) * 128], src[:, j * 128:(j + 1) * 128], self.ident),
                             reads=[bsrc, bc], writes=[psb[2]])
                    P.op("act", act(dst[:, t * 512:(t + 1) * 512], ps[2][:, :], AF.Copy), reads=[psb[2]], writes=[bdst])

            stageA0(0)
            stageA1a(0)
            stageA1b(0)
            if NT > 1:
                stageA0(1)
            for t in range(NT):
                if t + 1 < NT:
                    stageA1a(t + 1)
                if t >= 1:
                    stageB2b1(t - 1)
                if t + 1 < NT:
                    stageA1b(t + 1)
                if t + 2 < NT:
                    stageA0(t + 2)
                stageB1(t)
                if t >= 1:
                    stageB2b2(t - 1)
                stageB2a(t)
            stageB2b1(NT - 1)
            stageB2b2(NT - 1)
            P.barrier()


    def _emit_wcasts(self):
        P = self.P
        P.dma("pool", self.wout_bf, self.wout[:, :], writes=[self.bwcast])
        P.dma("pool", self.wdown_bf, self.wdown[:, :], writes=[self.bwcast])
        for i in range(4):
            P.dma("pool", self.wup_bf[i * 256:(i + 1) * 256, :], self.wup[i * 256:(i + 1) * 256, :], writes=[self.bwcast])

    def phase_gdn(self):
        nc, P, S, NT, NK = self.nc, self.P, self.S, self.NT, self.NK
        ps, psb, bc = self.ps, self.psb, self.b_const
        cf = self.c_f32
        identf, onesf = cf[:, 0:128], cf[:, 128:256]
        tri_incl, blk, negm, strict = cf[:, 256:384], cf[:, 384:512], cf[:, 512:640], cf[:, 640:768]
        chm = [cf[:, 768:896], cf[:, 896:1024]]
        pv = self.c_pv
        with ExitStack() as ws:
            wsb = lambda n, s, d: ws.enter_context(nc.sbuf_tensor(n, s, d))
            gt_ = wsb("gtmp", [128, 8 * NK], F32)
            G, Bt, GC, GT, EG, EKD, NGC = [gt_[:, i * NK:(i + 1) * NK] for i in range(7)]
            GLb = [wsb("GLb%d" % c, [128, NK], F32) for c in range(2)]
            sc = wsb("gsc", [128, 4], F32)
            bg = Buf("gates")
            gab = self.GAB[:].rearrange("p (n t) -> p n t", t=2)
            rd = self.bGAB + [bc]
            P.op("act", act(G, gab[:, :, 0], AF.Exp, bias=pv[:, 3:4]), reads=rd, writes=[bg])
            P.op("act", act(G, G, AF.Ln, bias=1.0), reads=[bg], writes=[bg])
            P.op("act", act(sc[:, 0:1], pv[:, 2:3], AF.Exp), reads=[bc], writes=[bg])
            P.op("dve", ts(G, G, sc[:, 0:1], -1.0, ALU.mult, ALU.mult), reads=[bg], writes=[bg])
            P.op("act", act(Bt, gab[:, :, 1], AF.Exp, scale=-1.0), reads=rd + [bg], writes=[bg])
            P.op("dve", ts(Bt, Bt, 1.0, None, ALU.add), reads=[bg], writes=[bg])
            P.op("dve", lambda e: e.reciprocal(out=Bt, in_=Bt), reads=[bg], writes=[bg])
            P.op("pe", mm(ps[0][:, 0:NK], tri_incl, G), reads=[bg, bc], writes=[psb[0]])
            P.op("pe", mm(ps[0][:, NK:2 * NK], blk, G), reads=[bg, bc], writes=[psb[0]])
            P.op("pe", mm(ps[0][:, 2 * NK:3 * NK], chm[0], G), reads=[bg, bc], writes=[psb[0]])
            P.op("pe", mm(ps[0][:, 3 * NK:4 * NK], chm[1], G), reads=[bg, bc], writes=[psb[0]])
            P.op("dve", cp(GC, ps[0][:, 0:NK]), reads=[psb[0]], writes=[bg])
            P.op("dve", tt(GT, ps[0][:, NK:2 * NK], GC, ALU.subtract), reads=[psb[0], bg], writes=[bg])
            P.op("dve", ts(NGC, GC, -1.0, None, ALU.mult), reads=[bg], writes=[bg])
            P.op("act", act(EG, GC, AF.Exp), reads=[bg], writes=[bg])
            P.op("act", act(EKD, GT, AF.Exp), reads=[bg], writes=[bg])
            bgl = Buf()
            P.op("act", act(GLb[0][:], ps[0][:, 2 * NK:3 * NK], AF.Exp), reads=[psb[0]], writes=[bgl])
            P.op("act", act(GLb[1][:], ps[0][:, 3 * NK:4 * NK], AF.Exp), reads=[psb[0]], writes=[bgl])

            U_s = [wsb("U_s%d" % i, [128, 512], F32) for i in range(2)]
            WT_s = [wsb("WT_s%d" % i, [128, 512], BF16) for i in range(2)]
            KDEC_s = [wsb("KDEC_s%d" % i, [128, 512], BF16) for i in range(2)]
            QDECT_s = [wsb("QDECT_s%d" % i, [128, 512], BF16) for i in range(2)]
            QKT_s = [wsb("QKT_s%d" % i, [128, 512], BF16) for i in range(2)]
            bU = [Buf() for _ in range(2)]
            bWT = [Buf() for _ in range(2)]
            bKD = [Buf() for _ in range(2)]
            bQDT = [Buf() for _ in range(2)]
            bQKT = [Buf() for _ in range(2)]
            I4 = wsb("I4", [128, 512], BF16)
            ST4 = wsb("ST4", [128, 512], F32)
            bI4 = Buf()
            for j in range(4):
                P.op("pool", cp(I4[:, j * 128:(j + 1) * 128], self.ident), reads=[bc], writes=[bI4])
                P.op("pool", cp(ST4[:, j * 128:(j + 1) * 128], strict), reads=[bc], writes=[bI4])
            TriG = [wsb("TriG%d" % i, [128, 128], F32) for i in range(2)]
            bTriG = [Buf() for _ in range(2)]
            EGr = wsb("EGr", [128, 512], F32)
            Ei = wsb("Ei", [128, 512], F32)
            Es = wsb("Es", [128, 512], F32)
            KG = wsb("KG", [128, 512], BF16)
            bEGr, bEi, bEs, bKG = Buf(), Buf(), Buf(), Buf()
            Ub = [wsb("Ub%d" % i, [128, 512], BF16) for i in range(2)]
            Lb = [wsb("Lb%d" % i, [128, 512], BF16) for i in range(2)]
            Pb = [wsb("Pb%d" % i, [128, 512], BF16) for i in range(2)]
            bUb = [Buf() for _ in range(2)]
            bLb = [Buf() for _ in range(2)]
            bPb = [Buf() for _ in range(2)]
            wtok = wsb("wtok", [128, 512], BF16)
            bwtok = Buf()
            hb = lambda: [Buf(), Buf()]
            XB, YB = (0, 4), (1, 5)
            bUh, bWTh, bKDh, bQDTh, bQKTh = [hb(), hb()], [hb(), hb()], [hb(), hb()], [hb(), hb()], [hb(), hb()]
            TriG4 = [wsb("TriG4_%d" % i, [128, 128], F32) for i in range(4)]
            bTriG4 = [Buf() for _ in range(4)]
            EGrh = [wsb("EGrh%d" % i, [128, 256], F32) for i in range(2)]
            Eih = [wsb("Eih%d" % i, [128, 256], F32) for i in range(2)]
            Esh = [wsb("Esh%d" % i, [128, 256], F32) for i in range(2)]
            KGh = [wsb("KGh%d" % i, [128, 256], BF16) for i in range(2)]
            wtokh = [wsb("wtokh%d" % i, [128, 256], BF16) for i in range(2)]
            ULh = [[wsb("UL%d_%d" % (i, j), [128, 512], BF16) for j in range(2)] for i in range(2)]
            Pbh = [[wsb("Pbh%d_%d" % (i, j), [128, 256], BF16) for j in range(2)] for i in range(2)]
            bEGrh, bEih, bEsh, bKGh, bwtokh = hb(), hb(), hb(), hb(), hb()
            bULh = [hb(), hb()]
            bPbh = [hb(), hb()]

            def step2h(tg, hz):
                so = tg % 2
                X, Y = ps[XB[hz]], ps[YB[hz]]
                bX, bY = psb[XB[hz]], psb[YB[hz]]
                g0 = hz * 256
                GC_ = slice(g0, g0 + 256)
                cH = slice(tg * 512 + g0, tg * 512 + g0 + 256)
                ci = [slice(0, 128), slice(128, 256)]
                gci = [slice(g0, g0 + 128), slice(g0 + 128, g0 + 256)]
                tts = [tg * 4 + 2 * hz, tg * 4 + 2 * hz + 1]
                cts = [slice(t * 128, (t + 1) * 128) for t in tts]
                UL, Pb_ = ULh[hz], Pbh[hz]
                bUL, bPb_ = bULh[hz], bPbh[hz]
                U_ = lambda j: UL[j][:, 0:256]
                L_ = lambda j: UL[j][:, 256:512]
                for i in range(2):
                    t = tts[i]
                    tl = 2 * hz + i
                    P.op("dve", ts(TriG4[tl][:], tri_incl, G[:, t:t + 1], None, ALU.mult), reads=[bg, bc], writes=[bTriG4[tl]])
                    P.op("pe", mm(X[:, ci[i]], onesf, TriG4[tl][:]), reads=[bTriG4[tl], bc], writes=[bX])
                    P.op("pe", mm(Y[:, ci[i]], onesf, TriG4[tl][:], start=True, stop=False), reads=[bTriG4[tl], bc], writes=[bY])
                    P.op("pe", mm(Y[:, ci[i]], identf, negm, start=False, stop=True), reads=[bc], writes=[bY])
                yield
                P.op("act", act(EGrh[hz][:], X[:, 0:256], AF.Exp), reads=[bX], writes=[bEGrh[hz]])
                for i in range(2):
                    P.op("act", act(Eih[hz][:, ci[i]], Y[:, ci[i]], AF.Exp, bias=NGC[:, tts[i]:tts[i] + 1]), reads=[bY, bg], writes=[bEih[hz]])
                P.op("dve", tt(QDECT_s[so][:, GC_], self.QgT[:, cH], EGrh[hz][:], ALU.mult), reads=[self.bQg[tg], bEGrh[hz]], writes=[bQDTh[so][hz]])
                P.op("dve", tt(Esh[hz][:], Eih[hz][:], ST4[:, 0:256], ALU.mult), reads=[bEih[hz], bI4], writes=[bEsh[hz]])
                yield
                for i in range(2):
                    P.op("pe", mm(X[:, ci[i]], self.KgT[:, cts[i]], self.QgT[:, cts[i]]), reads=[self.bKg[tg], self.bQg[tg]], writes=[bX])
                    P.op("pe", mm(Y[:, ci[i]], self.KgT[:, cts[i]], self.KgT[:, cts[i]]), reads=[self.bKg[tg]], writes=[bY])
                P.op("dve", tt(QKT_s[so][:, GC_], X[:, 0:256], Eih[hz][:], ALU.mult), reads=[bX, bEih[hz]], writes=[bQKTh[so][hz]])
                for i in range(2):
                    t = tts[i]
                    P.op("dve", stt(UL[0][:, ci[i]], Y[:, ci[i]], Bt[:, t:t + 1], Esh[hz][:, ci[i]], ALU.mult, ALU.mult),
                         reads=[bY, bEsh[hz], bg], writes=[bUL[0]])
                    P.op("act", act(KGh[hz][:, ci[i]], self.Kt[:, cts[i]], AF.Copy, scale=EG[:, t:t + 1]), reads=[self.bKt[tg], bg], writes=[bKGh[hz]])
                    P.op("act", act(KDEC_s[so][:, gci[i]], self.Kt[:, cts[i]], AF.Copy, scale=EKD[:, t:t + 1]),
                         reads=[self.bKt[tg], bg], writes=[bKDh[so][hz]])
                yield
                P.op("dve", stt(Pb_[0][:], U_(0), -1.0, I4[:, 0:256], ALU.mult, ALU.add), reads=[bUL[0], bI4], writes=[bPb_[0]])
                for i in range(2):
                    P.op("pe", mm(X[:, ci[i]], UL[0][:, ci[i]], self.ident), reads=[bUL[0], bc], writes=[bX])
                P.op("act", act(L_(0), X[:, 0:256], AF.Copy), reads=[bX], writes=[bUL[0]])
                yield
                cu, pc = 0, 0
                for k in range(5):
                    nx = 1 - cu
                    for i in range(2):
                        ui, li = ci[i], slice(256 + i * 128, 256 + (i + 1) * 128)
                        if k < 4:
                            P.op("pe", mm(X[:, ui], UL[cu][:, li], UL[cu][:, ui]), reads=[bUL[cu]], writes=[bX])
                        P.op("pe", mm(X[:, li], UL[cu][:, ui], UL[cu][:, li]), reads=[bUL[cu]], writes=[bX])
                        if k >= 1:
                            P.op("pe", mm(Y[:, ui], UL[cu][:, li], Pb_[pc][:, ui]), reads=[bUL[cu], bPb_[pc]], writes=[bY])
                    if k < 4:
                        P.op("act", act(UL[nx][:, :], X[:, :], AF.Copy), reads=[bX], writes=[bUL[nx]])
                    else:
                        P.op("act", act(L_(nx), X[:, 256:512], AF.Copy), reads=[bX], writes=[bUL[nx]])
                    if k >= 1:
                        P.op("dve", tt(Pb_[1 - pc][:], Y[:, 0:256], Pb_[pc][:], ALU.add), reads=[bY, bPb_[pc]], writes=[bPb_[1 - pc]])
                        pc = 1 - pc
                    yield
                    cu = nx
                for i in range(2):
                    li = slice(256 + i * 128, 256 + (i + 1) * 128)
                    P.op("pe", mm(Y[:, ci[i]], UL[cu][:, li], Pb_[pc][:, ci[i]]), reads=[bUL[cu], bPb_[pc]], writes=[bY])
                P.op("dve", tt(Pb_[1 - pc][:], Y[:, 0:256], Pb_[pc][:], ALU.add), reads=[bY, bPb_[pc]], writes=[bPb_[1 - pc]])
                pc = 1 - pc
                yield
                Qm, bQm = Pb_[pc], bPb_[pc]
                for i in range(2):
                    P.op("pe", mm(X[:, ci[i]], Qm[:, ci[i]], self.Vg[:, cts[i]]), reads=[bQm, self.bVg[tg]], writes=[bX])
                    P.op("pe", mm(Y[:, ci[i]], Qm[:, ci[i]], KGh[hz][:, ci[i]]), reads=[bQm, bKGh[hz]], writes=[bY])
                for i in range(2):
                    t = tts[i]
                    P.op("dve", ts(U_s[so][:, gci[i]], X[:, ci[i]], Bt[:, t:t + 1], None, ALU.mult), reads=[bX, bg], writes=[bUh[so][hz]])
                    P.op("act", act(wtokh[hz][:, ci[i]], Y[:, ci[i]], AF.Copy, scale=Bt[:, t:t + 1]), reads=[bY, bg], writes=[bwtokh[hz]])
                yield
                for i in range(2):
                    P.op("pe", mm(X[:, ci[i]], wtokh[hz][:, ci[i]], self.ident), reads=[bwtokh[hz], bc], writes=[bX])
                P.op("act", act(WT_s[so][:, GC_], X[:, 0:256], AF.Copy), reads=[bX], writes=[bWTh[so][hz]])
                yield

            def step2(tg):
                ga, gb = step2h(tg, 0), step2h(tg, 1)
                while True:
                    ra = next(ga, "done")
                    rb = next(gb, "done")
                    if ra == "done" and rb == "done":
                        return
                    yield

            St = wsb("St", [128, 128], F32)
            Sb = [wsb("Sb%d" % i, [128, 128], BF16) for i in range(2)]
            vn = [wsb("vn%d" % i, [128, 128], BF16) for i in range(2)]
            bSt = Buf()
            bSb = [Buf() for _ in range(2)]
            bvn = [Buf() for _ in range(2)]
            og = wsb("og", [128, 512], F32)
            ogs = wsb("ogs", [128, 512], BF16)
            ogl = wsb("ogl", [128, 512], F32)
            ogr = wsb("ogr", [128, 512], F32)
            ogm = [wsb("ogm%d" % i, [128, 512], BF16) for i in range(2)]
            bog, bogs, bogl, bogr = Buf(), Buf(), Buf(), Buf()
            bogm = [Buf() for _ in range(2)]
            P.op("dve", lambda e: e.memset(St[:], 0.0), writes=[bSt])
            P.op("dve", lambda e: e.memset(Sb[0][:], 0.0), writes=[bSb[0]])
            self._cur = 0

            def scan_chunk(n, gen=None):
                adv = (lambda: next(gen, None)) if gen is not None else (lambda: None)
                cur = self._cur
                t, half = n // 2, n % 2
                tg = t // 4
                so = tg % 2
                tl = t % 4
                r0 = 64 * half
                cl = slice(tl * 128, (tl + 1) * 128)
                cc = slice(tl * 128 + r0, tl * 128 + r0 + 64)
                pa, pb_, po = 2, 3, 6 + (n // 8) % 2
                oc = slice((n % 8) * 64, (n % 8 + 1) * 64)
                v_ = vn[n % 2]
                bv_ = bvn[n % 2]
                hz = tl // 2
                P.op("pe", mm(ps[pa][:, 0:128], WT_s[so][:, cl], Sb[cur][:]), reads=[bWTh[so][hz], bSb[cur]], writes=[psb[pa]])
                P.op("pe", mm(ps[po][:, oc], Sb[cur][:], QDECT_s[so][:, cc], start=True, stop=False),
                     reads=[bSb[cur], bQDTh[so][hz]], writes=[psb[po]])
                P.op("dve", tt(v_[r0:r0 + 64, :], U_s[so][r0:r0 + 64, cl], ps[pa][r0:r0 + 64, 0:128], ALU.subtract),
                     reads=[bUh[so][hz], psb[pa]], writes=[bv_])
                adv()
                P.op("pe", mm(ps[pb_][:, 0:128], KDEC_s[so][r0:r0 + 64, cl], v_[r0:r0 + 64, :]), reads=[bKDh[so][hz], bv_], writes=[psb[pb_]])
                P.op("pe", mm(ps[po][:, oc], v_[r0:r0 + 64, :], QKT_s[so][r0:r0 + 64, cc], start=False, stop=True),
                     reads=[bv_, bQKTh[so][hz]], writes=[psb[po]])
                nxt = 1 - cur
                P.op("dve", stt(Sb[nxt][:], St[:], GLb[half][:, t:t + 1], ps[pb_][:, 0:128], ALU.mult, ALU.add),
                     reads=[bSt, bgl, psb[pb_]], writes=[bSb[nxt]])
                P.op("dve", stt(St[:], St[:], GLb[half][:, t:t + 1], ps[pb_][:, 0:128], ALU.mult, ALU.add),
                     reads=[bSt, bgl, psb[pb_]], writes=[bSt])
                self._cur = nxt
                adv()
                if n % 8 == 7:
                    q8 = n // 8
                    c512 = slice(q8 * 512, (q8 + 1) * 512)
                    om = ogm[q8 % 2]
                    P.op("act", act(og[:], ps[po][:, :], AF.Copy), reads=[psb[po]], writes=[bog])
                    P.op("pool", tt(ogs[:], og[:], og[:], ALU.mult), reads=[bog], writes=[bogs])
                    P.op("pe", mm(ps[pb_][:, :], self.ones, ogs[:]), reads=[bogs, bc], writes=[psb[pb_]])
                    P.op("act", act(ogl[:], ps[pb_][:, :], AF.Ln, bias=EPS, scale=1.0 / 128), reads=[psb[pb_]], writes=[bogl])
                    P.op("act", act(ogr[:], ogl[:], AF.Exp, scale=-0.5), reads=[bogl], writes=[bogr])
                    P.op("dve", stt(og[:], og[:], pv[:, 1:2], ogr[:], ALU.mult, ALU.mult), reads=[bog, bogr, bc], writes=[bog])
                    P.op("dve", tt(om[:], og[:], self.Zs[:, c512], ALU.mult), reads=[bog, self.bZs[q8]], writes=[bogm[q8 % 2]])
                    self.mix_out(1, q8, om, bogm[q8 % 2])

            wc_list = [(self.wout_bf, self.wout[:, :]), (self.wdown_bf, self.wdown[:, :])]
            wc_list += [(self.wup_bf[i * 256:(i + 1) * 256, :], self.wup[i * 256:(i + 1) * 256, :]) for i in range(4)]
            for _ in step2(0):
                pass
            for tg in range(NT):
                gen = step2(tg + 1) if tg + 1 < NT else iter(())
                for n in range(tg * 8, tg * 8 + 8):
                    scan_chunk(n, gen)
                for _ in gen:
                    pass
                if wc_list and (tg % 2 == 1 or tg == NT - 1):
                    for _ in range(1 if tg < NT - 1 else len(wc_list)):
                        o_, i_ = wc_list.pop(0)
                        P.dma("pool", o_, i_, writes=[self.bwcast])
            self.mix_flush(1)
            P.barrier()

    def phase_attn(self):
        nc, P, S, NT = self.nc, self.P, self.S, self.NT
        ps, psb, bc = self.ps, self.psb, self.b_const
        with ExitStack() as ws:
            wsb = lambda n, s, d: ws.enter_context(nc.sbuf_tensor(n, s, d))
            QA = [wsb("QA0", [68, S], BF16), wsb("QA1", [68, S], BF16)]
            KA = [wsb("KA0", [68, S], BF16), wsb("KA1", [68, S], BF16)]
            V = wsb("Vat", [128, self.NK * 128], BF16)
            bQA, bKA, bV = Buf(), Buf(), Buf()
            allsc = self.bQD + self.bKD + self.bVD
            P.dma("sp", QA[0][0:64, :], self.QD[0:64, :], reads=allsc, writes=[bQA])
            P.dma("sp", QA[1][0:64, :], self.QD[64:128, :], reads=allsc, writes=[bQA])
            P.dma("sp", KA[0][0:64, :], self.KD[0:64, :], reads=allsc, writes=[bKA])
            P.dma("sp", KA[1][0:64, :], self.KD[64:128, :], reads=allsc, writes=[bKA])
            P.dma("sp", QA[0][64:68, :], self.qaug[:, :], writes=[bQA])
            P.dma("sp", KA[0][64:68, :], self.kaug[:, :], writes=[bKA])
            bQA1, bKA1 = Buf(), Buf()
            P.dma("sp", QA[1][64:68, :], self.qaug[:, :], writes=[bQA1])
            P.dma("sp", KA[1][64:68, :], self.kaug[:, :], writes=[bKA1])
            Vv = V[:].rearrange("p (n d) -> p n d", d=128)
            VDv = self.VD.rearrange("(n p) d -> p n d", p=128)
            for i in range(0, self.NK, 8):
                P.dma("sp", Vv[:, i:i + 8, :], VDv[:, i:i + 8, :], reads=allsc, writes=[bV])
            rows = [(0, 68), (0, 68)]
            lam = wsb("lam", [128, 256], F32)
            lt = wsb("lamt", [128, 128], F32)
            lsc = wsb("lsc", [128, 8], F32)
            blam = Buf()
            P.dma("sp", lam[:], self.lamv[:, :], writes=[blam])
            P.op("dve", tt(lt[:, 0:64], lam[:, 0:64], lam[:, 64:128], ALU.mult), reads=[blam], writes=[blam])
            P.op("dve", tt(lt[:, 64:128], lam[:, 128:192], lam[:, 192:256], ALU.mult), reads=[blam], writes=[blam])
            P.op("dve", lambda e: e.reduce_sum(out=lsc[:, 0:1], in_=lt[:, 0:64], axis=AX.X), reads=[blam], writes=[blam])
            P.op("dve", lambda e: e.reduce_sum(out=lsc[:, 1:2], in_=lt[:, 64:128], axis=AX.X), reads=[blam], writes=[blam])
            P.op("act", act(lsc[:, 2:4], lsc[:, 0:2], AF.Exp), reads=[blam], writes=[blam])
            P.op("dve", stt(lsc[:, 4:5], lsc[:, 3:4], -0.2, lsc[:, 2:3], ALU.add, ALU.subtract), reads=[blam], writes=[blam])
            P.op("dve", ts(lsc[:, 5:6], self.c_pv[:, 0:1], 0.8, None, ALU.mult), reads=[blam, bc], writes=[blam])
            neglam = lsc[:, 4:5]
            gsub = lsc[:, 5:6]

            pt = [wsb("pt%d" % i, [128, 512], BF16) for i in range(4)]
            bpt = [Buf() for _ in range(4)]
            rl = wsb("rl", [128, 512], F32)
            brl = Buf()
            On = [wsb("On%d" % i, [128, 512], F32) for i in range(2)]
            bOn = [Buf() for _ in range(2)]
            oa = wsb("oa", [128, 512], F32)
            boa = Buf()
            osq = wsb("osq", [128, 512], BF16)
            bosq = Buf()
            lnt = wsb("lnt2", [128, 512], F32)
            blnt = Buf()
            rs = wsb("rs", [128, 512], F32)
            brs = Buf()
            mst = [wsb("mst%d" % i, [128, 512], BF16) for i in range(2)]
            bmst = [Buf() for _ in range(2)]

            blocks = []
            for qi in range(NT):
                for c in range(2):
                    nkt = 4 * (qi + 1)
                    for kt in range(nkt):
                        blocks.append((qi, c, kt, nkt))
            nb = len(blocks)

            def qk(i):
                qi, c, kt, nkt = blocks[i]
                j = kt - 4 * qi
                col0 = 128 * j if j >= 0 else 0
                b = i % 3
                r0, r1 = rows[c]
                P.op("pe", mm(ps[b][:, col0:512], KA[c][r0:r1, kt * 128:(kt + 1) * 128],
                              QA[c][r0:r1, qi * 512 + col0:(qi + 1) * 512]),
                     reads=[bQA, bKA, bQA1, bKA1], writes=[psb[b]])

            def rest(i):
                qi, c, kt, nkt = blocks[i]
                g = qi * 2 + c
                j = kt - 4 * qi
                col0 = 128 * j if j >= 0 else 0
                b = i % 3
                sl_ = i % 4
                P.op("act", act(pt[sl_][:, col0:512], ps[b][:, col0:512], AF.Exp), reads=[psb[b]], writes=[bpt[sl_]])
                if j >= 0:
                    P.op("dve", tt(pt[sl_][:, col0:col0 + 128], pt[sl_][:, col0:col0 + 128], self.tri, ALU.mult),
                         reads=[bpt[sl_], bc], writes=[bpt[sl_]])
                po, pl = 3 + g % 2, 5 + g % 2
                P.op("pe", mm(ps[po][:, col0:512], V[:, kt * 128:(kt + 1) * 128], pt[sl_][:, col0:512],
                              start=(kt == 0), stop=(kt == nkt - 1)), reads=[bV, bpt[sl_]], writes=[psb[po]])
                P.op("pe", mm(ps[pl][:, col0:512], self.ones, pt[sl_][:, col0:512],
                              start=(kt == 0), stop=(kt == nkt - 1)), reads=[bc, bpt[sl_]], writes=[psb[pl]])
                if kt == nkt - 1:
                    cols = slice(qi * 512, (qi + 1) * 512)
                    P.op("dve", lambda e: e.reciprocal(out=rl[:], in_=ps[pl][:, :]), reads=[psb[pl]], writes=[brl])
                    P.op("dve", tt(On[c][:], ps[po][:, :], rl[:], ALU.mult), reads=[psb[po], brl], writes=[bOn[c]])
                    if c == 1:
                        P.op("dve", stt(oa[:], On[1][:], neglam, On[0][:], ALU.mult, ALU.add),
                             reads=[bOn[0], bOn[1], blam], writes=[boa])
                        P.op("pool", tt(osq[:], oa[:], oa[:], ALU.mult), reads=[boa], writes=[bosq])
                        P.op("pe", mm(ps[7][:, :], self.ones, osq[:]), reads=[bosq, bc], writes=[psb[7]])
                        P.op("act", act(lnt[:], ps[7][:, :], AF.Ln, bias=EPS, scale=1.0 / 128), reads=[psb[7]], writes=[blnt])
                        P.op("act", act(rs[:], lnt[:], AF.Exp, scale=-0.5), reads=[blnt], writes=[brs])
                        ms = mst[qi % 2]
                        P.op("dve", stt(ms[:], oa[:], gsub, rs[:], ALU.mult, ALU.mult),
                             reads=[boa, brs, blam], writes=[bmst[qi % 2]])
                        self.mix_out(0, qi, ms, bmst[qi % 2])

            qk(0)
            if nb > 1:
                qk(1)
            for i in range(nb):
                if i + 2 < nb:
                    pass
                self._attn_step(i, nb, qk, rest)
            self.mix_flush(0)
            P.barrier()

    def _attn_step(self, i, nb, qk, rest):
        if i + 2 < nb:
            qk(i + 2)
        rest(i)


    def _pidj(self, e):
        if getattr(self, "_pj", None) is None:
            pid = e.partition_id()
            self._pj = (pid % 4) * 8
        return self._pj

    def _mix_init(self):
        if hasattr(self, "zt"):
            return
        nc, P, es = self.nc, self.P, self.es
        self.zt = es.enter_context(nc.sbuf_tensor("zt", [128, 4], BF16))
        self.bz = Buf()
        P.op("pool", lambda e: e.memset(self.zt[:], 0.0), writes=[self.bz])
        self.bagi = [[Buf() for _ in range(4)] for _ in range(2)]
        self.bago = [Buf() for _ in range(8)]
        self._pend = [[], []]
        self.direct = (self.TPC % 512 == 0)
        self.agi = self.ag_in.ap().rearrange("(j f) t -> j f t", j=4)

    def _collective(self, half, j):
        P = self.P
        c = 2 * j + half
        ag_in, ag_out = self.ag_in, self.ag_out
        P.collective(lambda e, c=c: e.collective_compute(
            "AllGather", ALU.bypass, replica_groups=[[0, 1, 2, 3], [4, 5, 6, 7]],
            ins=[ag_in.ap()[c * 128:(c + 1) * 128, :]], outs=[ag_out.ap()[c * 512:(c + 1) * 512, :]]),
            self.cc_sem, reads=[self.bagi[half][j]], writes=[self.bago[c]])

    def mix_out(self, half, q8, ms, bms):
        P, TPC = self.P, self.TPC
        self._mix_init()
        r0, r1 = half * 128, (half + 1) * 128
        cols = slice(q8 * 512, (q8 + 1) * 512)
        bm = (self.bMIXa if half == 0 else self.bMIXg)[q8]
        for j in self._pend[half]:
            self._collective(half, j)
        self._pend[half] = []
        if self.debug or not self.direct:
            P.dma("sp", self.MIXD[r0:r1, cols], ms[:], reads=[bms], writes=[bm])
        if not self.direct:
            return
        j = (q8 * 512) // TPC
        off = q8 * 512 - j * TPC
        bi = self.bagi[half]
        if q8 == 0:
            P.dma("sp", self.agi[0, r0:r1, 0:2], self.zt[:, 0:2], reads=[self.bz], writes=[bi[0]])
        P.dma("sp", self.agi[j, r0:r1, 2 + off:2 + off + 512], ms[:], reads=[bms], writes=[bi[j]])
        if off + 512 == TPC:
            if j + 1 < 4:
                P.dma("sp", self.agi[j + 1, r0:r1, 0:2], ms[:, 510:512], reads=[bms], writes=[bi[j + 1]])
            self._pend[half].append(j)

    def mix_flush(self, half):
        P, TPC = self.P, self.TPC
        self._mix_init()
        W2 = TPC + 2
        if not self.direct:
            allm = self.bMIXa if half == 0 else self.bMIXg
            r0, r1 = half * 128, (half + 1) * 128
            for j in range(4):
                bi = self.bagi[half][j]
                P.dma("sp", self.agi[j, r0:r1, 2:W2], self.MIXD[r0:r1, j * TPC:(j + 1) * TPC], reads=allm, writes=[bi])
                if j > 0:
                    P.dma("sp", self.agi[j, r0:r1, 0:2], self.MIXD[r0:r1, j * TPC - 2:j * TPC], reads=allm, writes=[bi])
                else:
                    P.dma("sp", self.agi[0, r0:r1, 0:2], self.zt[:, 0:2], reads=[self.bz], writes=[bi])
                self._pend[half].append(j)
        for j in self._pend[half]:
            self._collective(half, j)
        self._pend[half] = []

    def phase2(self):
        nc, P, S, TPC = self.nc, self.P, self.S, self.TPC
        ps, psb, bc = self.ps, self.psb, self.b_const
        W2 = TPC + 2
        NH = 2
        HT = TPC // NH
        nt = -(-(HT + 2) // 512)
        base = (HT + 2) // nt
        tiles = []
        a0 = 0
        for i in range(nt):
            w = base + (1 if i < (HT + 2) - base * nt else 0)
            tiles.append((a0, w))
            a0 += w
        WM = max(w for _, w in tiles)
        NFC = D_FF // 128
        with ExitStack() as ws:
            wsb = lambda n, s, d: ws.enter_context(nc.sbuf_tensor(n, s, d))
            MIXT = wsb("MIXT", [128, 8 * W2], BF16)
            Wout = wsb("Wout", [128, 8 * 1024], BF16)
            Wd = wsb("Wd", [128, NFC * 1024], BF16)
            H2 = wsb("H2", [128, 8 * (HT + 2)], BF16)
            ACTT = wsb("ACTT", [128, NFC * HT], BF16)
            g2 = wsb("g2_sb", [128, 8], F32)
            fcw = wsb("fcw_sb", [128, 44 * 4], F32)
            gfin = wsb("gfin_sb", [128, 1024], F32)
            bWout, bWd, bH2, bsm = Buf(), Buf(), Buf(), Buf()
            bMIXTk = [Buf() for _ in range(8)]
            bACTT = [Buf() for _ in range(NFC)]
            agv = self.ag_out.ap().rearrange("(r f) t -> r f t", f=128)
            MIXTv = MIXT[:].rearrange("p (c t) -> p c t", c=8)
            for h in range(4):
                for half in range(2):
                    P.dma("pool", MIXTv[:, 2 * h + half, :],
                          lambda e, h=h, half=half: agv[bass.ds(self._pidj(e) + (4 * half + h), 1), :, :]
                          .rearrange("o f t -> (o f) t"),
                          reads=self.bago, writes=[bMIXTk[2 * h + half]])
            if self.stop == "p2x1":
                return
            P.dma("sp", Wout[:].rearrange("p (c n) -> p c n", c=8), self.wout_bf.rearrange("(c p) n -> p c n", p=128),
                  reads=[self.bwcast], writes=[bWout])
            Wdv = Wd[:].rearrange("p (c n) -> p c n", c=NFC)
            wdv = self.wdown_bf.rearrange("(c p) n -> p c n", p=128)
            for i in range(0, NFC, 4):
                P.dma("sp", Wdv[:, i:min(i + 4, NFC), :], wdv[:, i:min(i + 4, NFC), :], reads=[self.bwcast], writes=[bWd])
            P.dma("sp", g2[:], self.g2[:, :], writes=[bsm])
            P.dma("sp", fcw[:], self.fcw[:, :], writes=[bsm])
            P.dma("sp", gfin[:], self.gfin[:, :], writes=[bsm])

            for hf in range(NH):
                c0 = hf * HT
                with ExitStack() as wa:
                    asb = lambda n, s, d: wa.enter_context(nc.sbuf_tensor(n + '_h%d' % hf, s, d))
                    x2t = [asb("x2t%d" % i, [128, 8 * WM], F32) for i in range(2)]
                    x1T = asb("x1T", [128, 8 * WM], F32)
                    sq = asb("sq2", [128, 8 * WM], BF16)
                    lnt = asb("lnA", [128, WM], F32)
                    rstd = asb("rstdA", [128, WM], F32)
                    bx2t = [Buf() for _ in range(2)]
                    bx1, bsq, bln, brs = Buf(), Buf(), Buf(), Buf()
                    for ti, (a0, w) in enumerate(tiles):
                        cols = slice(c0 + a0, c0 + a0 + w)
                        xs = x2t[ti % 2]
                        P.dma("sp", xs[:, 0:8 * w].rearrange("p (c t) -> p c t", c=8),
                              self.xT2[:, cols].rearrange("(c p) t -> p c t", p=128), writes=[bx2t[ti % 2]])
                        for oc in range(8):
                            b = oc % 4
                            for kc in range(8):
                                P.op("pe", mm(ps[b][:, 0:w], Wout[:, kc * 1024 + oc * 128: kc * 1024 + (oc + 1) * 128],
                                              MIXT[:, kc * W2 + c0 + a0: kc * W2 + c0 + a0 + w], start=(kc == 0), stop=(kc == 7)),
                                     reads=[bWout, bMIXTk[kc]], writes=[psb[b]])
                            P.op("dve", tt(x1T[:, oc * w:(oc + 1) * w], ps[b][:, 0:w], xs[:, oc * w:(oc + 1) * w], ALU.add),
                                 reads=[psb[b], bx2t[ti % 2]], writes=[bx1])
                        P.op("pool", tt(sq[:, 0:8 * w], x1T[:, 0:8 * w], x1T[:, 0:8 * w], ALU.mult), reads=[bx1], writes=[bsq])
                        for c in range(8):
                            P.op("pe", mm(ps[4][:, 0:w], self.onesmean, sq[:, c * w:(c + 1) * w], start=(c == 0), stop=(c == 7)),
                                 reads=[bsq, bc], writes=[psb[4]])
                        P.op("act", act(lnt[:, 0:w], ps[4][:, 0:w], AF.Ln, bias=EPS), reads=[psb[4]], writes=[bln])
                        P.op("act", act(rstd[:, 0:w], lnt[:, 0:w], AF.Exp, scale=-0.5), reads=[bln], writes=[brs])
                        for kc in range(8):
                            P.op("dve", stt(H2[:, kc * (HT + 2) + a0: kc * (HT + 2) + a0 + w], x1T[:, kc * w:(kc + 1) * w],
                                            g2[:, kc:kc + 1], rstd[:, 0:w], ALU.mult, ALU.mult),
                                 reads=[bx1, brs, bsm], writes=[bH2])
                    P.barrier()
                if self.stop == "p2A":
                    break
                with ExitStack() as wc:
                    csb = lambda n, s, d: wc.enter_context(nc.sbuf_tensor(n + '_h%d' % hf, s, d))
                    Wg = [csb("Wg%d" % i, [128, 8 * 128], BF16) for i in range(2)]
                    Wv = [csb("Wv%d" % i, [128, 8 * 128], BF16) for i in range(2)]
                    cg = [csb("cg%d" % i, [128, HT], F32) for i in range(2)]
                    cv = [csb("cv%d" % i, [128, HT], F32) for i in range(2)]
                    sg = [csb("sg%d" % i, [128, HT], F32) for i in range(2)]
                    bWg = [Buf() for _ in range(2)]
                    bWv = [Buf() for _ in range(2)]
                    bcg = [Buf() for _ in range(2)]
                    bcv = [Buf() for _ in range(2)]
                    bsg = [Buf() for _ in range(2)]
                    nto = -(-HT // 510)
                    bo = HT // nto
                    otiles = []
                    o0 = 0
                    for i in range(nto):
                        wo = bo + (1 if i < HT - bo * nto else 0)
                        otiles.append((o0, wo))
                        o0 += wo
                    H2W = HT + 2

                    def loadw(fc):
                        sl_ = fc % 2
                        P.dma("sp", Wg[sl_][:].rearrange("p (c n) -> p c n", c=8),
                              self.wup_bf[:, fc * 128:(fc + 1) * 128].rearrange("(c p) n -> p c n", p=128),
                              reads=[self.bwcast], writes=[bWg[sl_]])
                        P.dma("sp", Wv[sl_][:].rearrange("p (c n) -> p c n", c=8),
                              self.wup_bf[:, D_FF + fc * 128: D_FF + (fc + 1) * 128].rearrange("(c p) n -> p c n", p=128),
                              reads=[self.bwcast], writes=[bWv[sl_]])

                    loadw(0)
                    pr = 0
                    for fc in range(NFC):
                        sl_ = fc % 2
                        if fc + 1 < NFC:
                            loadw(fc + 1)
                        wg_ = fcw[:, fc * 4: fc * 4 + 4]
                        wv_ = fcw[:, (NFC + fc) * 4: (NFC + fc) * 4 + 4]
                        for (o0, wo) in otiles:
                            bg_, bv_ = pr % 8, (pr + 1) % 8
                            pr += 2
                            n_ = wo + 2
                            for kc in range(8):
                                P.op("pe", mm(ps[bg_][:, 0:n_], Wg[sl_][:, kc * 128:(kc + 1) * 128],
                                              H2[:, kc * H2W + o0: kc * H2W + o0 + n_], start=(kc == 0), stop=(kc == 7)),
                                     reads=[bWg[sl_], bH2], writes=[psb[bg_]])
                            for kc in range(8):
                                P.op("pe", mm(ps[bv_][:, 0:n_], Wv[sl_][:, kc * 128:(kc + 1) * 128],
                                              H2[:, kc * H2W + o0: kc * H2W + o0 + n_], start=(kc == 0), stop=(kc == 7)),
                                     reads=[bWv[sl_], bH2], writes=[psb[bv_]])
                            og_ = cg[sl_][:, o0:o0 + wo]
                            ov_ = cv[sl_][:, o0:o0 + wo]
                            P.op("act", act(og_, ps[bg_][:, 0:wo], AF.Identity, bias=wg_[:, 3:4], scale=wg_[:, 0:1]),
                                 reads=[psb[bg_], bsm], writes=[bcg[sl_]])
                            P.op("act", act(ov_, ps[bv_][:, 0:wo], AF.Identity, bias=wv_[:, 3:4], scale=wv_[:, 0:1]),
                                 reads=[psb[bv_], bsm], writes=[bcv[sl_]])
                            for j in (1, 2):
                                P.op("dve", stt(og_, ps[bg_][:, j:j + wo], wg_[:, j:j + 1], og_, ALU.mult, ALU.add),
                                     reads=[psb[bg_], bsm], writes=[bcg[sl_]])
                                P.op("dve", stt(ov_, ps[bv_][:, j:j + wo], wv_[:, j:j + 1], ov_, ALU.mult, ALU.add),
                                     reads=[psb[bv_], bsm], writes=[bcv[sl_]])
                        P.op("act", act(sg[sl_][:], cg[sl_][:], AF.Silu), reads=[bcg[sl_]], writes=[bsg[sl_]])
                        P.op("pool", tt(ACTT[:, fc * HT:(fc + 1) * HT], sg[sl_][:], cv[sl_][:], ALU.mult),
                             reads=[bsg[sl_], bcv[sl_]], writes=[bACTT[fc]])
                    P.barrier()
                if self.stop == "p2C":
                    break
                with ExitStack() as wd:
                    dsb = lambda n, s, d: wd.enter_context(nc.sbuf_tensor(n + '_h%d' % hf, s, d))
                    xk = [dsb("xk%d" % i, [128, 1024], F32) for i in range(2)]
                    x2 = [dsb("x2_%d" % i, [128, 1024], F32) for i in range(2)]
                    sqd = dsb("sqd", [128, 1024], F32)
                    ot = [dsb("ot%d" % i, [128, 1024], F32) for i in range(2)]
                    st = dsb("std", [128, 8], F32)
                    bxk = [Buf() for _ in range(2)]
                    bx2 = [Buf() for _ in range(2)]
                    bot = [Buf() for _ in range(2)]
                    bsqd, bst = Buf(), Buf()
                    for sub in range(HT // 128):
                        s2_ = sub % 2
                        tok0 = hf * HT + sub * 128
                        P.dma("sp", xk[s2_][:], self.xtok2[tok0:tok0 + 128, :], writes=[bxk[s2_]])
                        for oh in range(2):
                            b = (sub * 2 + oh) % 4
                            for fc in range(NFC):
                                P.op("pe", mm(ps[b][:, :], ACTT[:, fc * HT + sub * 128: fc * HT + (sub + 1) * 128],
                                              Wd[:, fc * 1024 + oh * 512: fc * 1024 + (oh + 1) * 512], start=(fc == 0), stop=False),
                                     reads=[bACTT[fc], bWd], writes=[psb[b]])
                            for kc in range(8):
                                P.op("pe", mm(ps[b][:, :], MIXT[:, kc * W2 + 2 + tok0: kc * W2 + 2 + tok0 + 128],
                                              Wout[:, kc * 1024 + oh * 512: kc * 1024 + (oh + 1) * 512], start=False, stop=(kc == 7)),
                                     reads=[bMIXTk[kc], bWout], writes=[psb[b]])
                            P.op("dve", tt(x2[s2_][:, oh * 512:(oh + 1) * 512], ps[b][:, :], xk[s2_][:, oh * 512:(oh + 1) * 512], ALU.add),
                                 reads=[psb[b], bxk[s2_]], writes=[bx2[s2_]])
                        P.op("pool", tt(sqd[:], x2[s2_][:], x2[s2_][:], ALU.mult), reads=[bx2[s2_]], writes=[bsqd])
                        P.op("dve", lambda e, st=st, sqd=sqd: e.reduce_sum(out=st[:, 0:1], in_=sqd[:], axis=AX.X), reads=[bsqd], writes=[bst])
                        P.op("act", act(st[:, 1:2], st[:, 0:1], AF.Ln, bias=EPS, scale=1.0 / 1024), reads=[bst], writes=[bst])
                        P.op("act", act(st[:, 2:3], st[:, 1:2], AF.Exp, scale=-0.5), reads=[bst], writes=[bst])
                        P.op("dve", stt(ot[s2_][:], x2[s2_][:], st[:, 2:3], gfin[:], ALU.mult, ALU.mult),
                             reads=[bx2[s2_], bst, bsm], writes=[bot[s2_]])
                        P.dma("sp", self.out[tok0:tok0 + 128, :], ot[s2_][:], reads=[bot[s2_]])
                    P.barrier()

    def dump_1a(self):
        P, d = self.P, self.dbg
        allb = self.bQD + self.bKD + self.bVD
        P.dma("sp", d["QD"], self.QD, reads=allb)
        P.dma("sp", d["KD"], self.KD, reads=allb)
        P.dma("sp", d["VD"], self.VD, reads=allb)
        P.dma("sp", d["QgT"], self.QgT[:], reads=self.bQg)
        P.dma("sp", d["KgT"], self.KgT[:], reads=self.bKg)
        P.dma("sp", d["Vg"], self.Vg[:], reads=self.bVg)
        P.dma("sp", d["Kt"], self.Kt[:], reads=self.bKt)
        P.dma("sp", d["Zs"], self.Zs[:], reads=self.bZs)
        P.dma("sp", d["GAB"], self.GAB[:], reads=self.bGAB)


def bf(a):
    return np.ascontiguousarray(a).astype(ml_dtypes.bfloat16)


def host_consts(S, h):
    slope = 2.0 ** (-8.0 * (h + 1) / 4)
    pos = np.arange(S)
    a, b = pos // 128, pos % 128
    qaug = np.stack([-slope * 128.0 * a, -slope * b, np.ones(S), np.ones(S)]).astype(np.float32)
    kaug = np.stack([np.ones(S), np.ones(S), slope * 128.0 * a, slope * b]).astype(np.float32)
    ident = np.eye(128, dtype=np.float32)
    ones = np.ones((128, 128), np.float32)
    k = np.arange(128)[:, None]
    q = np.arange(128)[None, :]
    tri = (q >= k).astype(np.float32)
    cbf = np.concatenate([ident, ones, ones / 1024.0, tri], axis=1)
    same = (k // 64) == (q // 64)
    tri_incl = ((k <= q) & same).astype(np.float32)
    blk = same.astype(np.float32)
    negm = np.where((q >= k) & same, 0.0, NEG).astype(np.float32)
    strict = ((q > k) & same).astype(np.float32)
    ch0 = np.repeat((np.arange(128) < 64).astype(np.float32)[:, None], 128, axis=1)
    ch1 = 1.0 - ch0
    cf32 = np.concatenate([ident, ones, tri_incl, blk, negm, strict, ch0, ch1], axis=1)
    return bf(qaug), bf(kaug), bf(cbf), cf32.astype(np.float32)


def split_cols(h):
    r = lambda base, n: list(range(base + h * n, base + (h + 1) * n))
    cols = r(0, 128) + r(512, 128) + r(1536, 128) + r(2048, 128) + r(2560, 128) + r(3080, 128) + r(1024, 128)
    cols += [3072 + h, 3076 + h]
    return np.array(cols)


def make_in_maps(inputs, S):
    x = np.asarray(inputs["x"], np.float32)
    B = x.shape[0]
    TPC = S // 4
    w_in = np.asarray(inputs["w_in"], np.float32)[0]
    per = lambda v: np.ascontiguousarray(np.asarray(v, np.float32).reshape(8, 128).T)
    maps = []
    for r in range(NCORES):
        b, h = r // 4, r % 4
        j = h
        qaug, kaug, cbf, cf32 = host_consts(S, h)
        m = {}
        m["xT"] = np.ascontiguousarray(x[b].T)
        m["wh"] = np.ascontiguousarray(w_in[:, split_cols(h)])
        m["gA"] = per(inputs["attn_norm_g"][0])
        cwfull = np.asarray(inputs["gdn_conv_w"], np.float32)[0]
        cw = np.zeros((128, 12), np.float32)
        for g in range(3):
            cw[:, g * 4:(g + 1) * 4] = cwfull[:, g * 512 + h * 128: g * 512 + (h + 1) * 128].T
        m["convw"] = cw
        m["qaug"], m["kaug"], m["cbf"], m["cf32"] = qaug, kaug, cbf, cf32
        pv = np.zeros((128, 16), np.float32)
        pv[:, 0] = np.asarray(inputs["da_subln_g"], np.float32)[0]
        pv[:, 1] = np.asarray(inputs["gdn_norm_g"], np.float32)[0]
        pv[:, 2] = np.asarray(inputs["gdn_a_log"], np.float32)[0, h]
        pv[:, 3] = np.asarray(inputs["gdn_dt_bias"], np.float32)[0, h]
        m["pvec"] = pv
        lv = np.concatenate([np.asarray(inputs[k], np.float32)[0] for k in
                             ("da_lambda_q1", "da_lambda_k1", "da_lambda_q2", "da_lambda_k2")])
        m["lamv"] = np.ascontiguousarray(np.broadcast_to(lv[None, :], (128, 256)))
        xT2 = np.zeros((D_MODEL, TPC + 2), np.float32)
        lo = j * TPC
        xT2[:, 2:] = x[b, lo:lo + TPC].T
        if j > 0:
            xT2[:, 0:2] = x[b, lo - 2:lo].T
        m["xT2"] = xT2
        m["xtok2"] = np.ascontiguousarray(x[b, lo:lo + TPC])
        wo = np.asarray(inputs["w_out"], np.float32)[0]
        rows = []
        for hh in range(4):
            rows += list(range(hh * 128, (hh + 1) * 128)) + list(range(512 + hh * 128, 512 + (hh + 1) * 128))
        m["wout"] = np.ascontiguousarray(wo[np.array(rows)])
        m["wup"] = np.ascontiguousarray(np.asarray(inputs["w_up"], np.float32)[0])
        m["wdown"] = np.ascontiguousarray(np.asarray(inputs["w_down"], np.float32)[0])
        m["g2"] = per(inputs["ffn_norm_g"][0])
        fw = np.asarray(inputs["ffn_conv_w"], np.float32)[0]
        fb = np.asarray(inputs["ffn_conv_b"], np.float32)[0]
        fcw = np.zeros((128, 44 * 4), np.float32)
        for c in range(44):
            fcw[:, c * 4:c * 4 + 3] = fw[:, c * 128:(c + 1) * 128].T
            fcw[:, c * 4 + 3] = fb[c * 128:(c + 1) * 128]
        m["fcw"] = fcw
        m["gfin"] = np.ascontiguousarray(np.broadcast_to(np.asarray(inputs["final_norm_g"], np.float32)[None, :], (128, D_MODEL)))
        maps.append(m)
    return maps


_CACHE = {}


def kernel(**inputs):
    x = np.asarray(inputs["x"])
    B, S, _ = x.shape
    if S not in _CACHE:
        _CACHE[S] = Builder(S).build()
    nc = _CACHE[S]
    maps = make_in_maps(inputs, S)
    res = run_bass_kernel_spmd(nc, maps, core_ids=list(range(NCORES)))
    out = np.zeros((B, S, D_MODEL), np.float32)
    TPC = S // 4
    for r in range(NCORES):
        b, j = r // 4, r % 4
        out[b, j * TPC:(j + 1) * TPC] = np.asarray(res.results[r]["out"], np.float32)
    return out
```
